# Optimizing a Trainium2 kernel written in Bass

```python
import jax, jax.numpy as jnp
from jax import lax
import numpy as np

D_MODEL = 1024
BATCH = 4
SEQ = 4096
DEPTH = 1

GRID_W = 64
CTX_LEN = 256
N_HEADS = 8
QK_NOPE_DIM = 64
QK_ROPE_DIM = 32
V_HEAD_DIM = 64
QK_DIM = QK_NOPE_DIM + QK_ROPE_DIM
Q_LORA_RANK = 384
KV_LORA_RANK = 256
ROPE_BASE = 10000.0
Q_BLOCK = 128
LRU_WIDTH = 1280
LRU_BLOCKS = 10
LRU_BLOCK_W = LRU_WIDTH // LRU_BLOCKS
LRU_CONV_W = 4
LRU_C = 8.0
FFN_DIM = 2816
FFN_CONV_W = 3
EPS = 1e-6
OFF_KV = Q_LORA_RANK
OFF_KR = OFF_KV + KV_LORA_RANK
OFF_XB = OFF_KR + QK_ROPE_DIM
OFF_YB = OFF_XB + LRU_WIDTH
OFF_G = OFF_YB + LRU_WIDTH
IN_DIM = OFF_G + 2 * D_MODEL

kernel_name = "hybrid_mla_rglru_convffn_dit_block"


def rms_norm(x, g):
    xf = x.astype(jnp.float32)
    y = xf * lax.rsqrt(jnp.mean(xf * xf, axis=-1, keepdims=True) + EPS)
    return (y * g.astype(jnp.float32)).astype(x.dtype)


def modulate(h, shift, scale):
    return h * (1 + scale) + shift


def dwconv(x, w, b, left, right):
    t = x.shape[1]
    xp = jnp.pad(x, ((0, 0), (left, right), (0, 0)))
    out = b
    for k in range(w.shape[0]):
        out = out + xp[:, k:k + t] * w[k]
    return out


def axial_rope_tables(n):
    rows = n // GRID_W
    row_ids = jnp.repeat(jnp.arange(rows), GRID_W).astype(jnp.float32)
    col_ids = jnp.tile(jnp.arange(GRID_W), rows).astype(jnp.float32)
    axis_dim = QK_ROPE_DIM // 2
    inv = 1.0 / (ROPE_BASE ** (jnp.arange(0, axis_dim, 2, dtype=jnp.float32) / axis_dim))
    ang = jnp.concatenate([row_ids[:, None] * inv, col_ids[:, None] * inv], axis=-1)
    return jnp.cos(ang), jnp.sin(ang)


def apply_rope(x, cos, sin):
    half = QK_ROPE_DIM // 2
    cos = cos.astype(x.dtype)
    sin = sin.astype(x.dtype)
    x1, x2 = x[..., :half], x[..., half:]
    return jnp.concatenate([x1 * cos - x2 * sin, x2 * cos + x1 * sin], axis=-1)


def sdpa(q, k, v):
    s = jnp.einsum('bqhd,bkhd->bhqk', q, k).astype(jnp.float32) * (QK_DIM ** -0.5)
    p = jax.nn.softmax(s, axis=-1).astype(v.dtype)
    return jnp.einsum('bhqk,bkhd->bqhd', p, v)


def attend_blocks(q, k, v):
    b, s, h, dq = q.shape
    nb = s // Q_BLOCK
    qb = q.reshape(b, nb, Q_BLOCK, h, dq).transpose(1, 0, 2, 3, 4)
    ob = lax.map(lambda qq: sdpa(qq, k, v), qb)
    return ob.transpose(1, 0, 2, 3, 4).reshape(b, s, h * V_HEAD_DIM)


def rglru(x, w_a, b_a, w_x, b_x, lam, h0, reverse):
    b, t, w = x.shape
    xb = x.reshape(b, t, LRU_BLOCKS, LRU_BLOCK_W)
    r = jax.nn.sigmoid(jnp.einsum('btnd,nde->btne', xb, w_a).reshape(b, t, w) + b_a)
    i = jax.nn.sigmoid(jnp.einsum('btnd,nde->btne', xb, w_x).reshape(b, t, w) + b_x)
    log_a = -LRU_C * r.astype(jnp.float32) * jax.nn.softplus(-lam.astype(jnp.float32))
    a = jnp.exp(log_a)
    mult = jnp.sqrt(-jnp.expm1(2.0 * log_a))
    u = mult * (i * x).astype(jnp.float32)
    if reverse:
        u = u.at[:, -1].add(a[:, -1] * h0)
    else:
        u = u.at[:, 0].add(a[:, 0] * h0)

    def combine(e1, e2):
        a1, b1 = e1
        a2, b2 = e2
        return a1 * a2, a2 * b1 + b2

    _, h = lax.associative_scan(combine, (a, u), reverse=reverse, axis=1)
    return h


def lru_bidir(xc, lp, h0f, h0b):
    hf = rglru(xc, lp['lru_w_a'][0], lp['lru_b_a'][0], lp['lru_w_x'][0], lp['lru_b_x'][0],
               lp['lru_lambda'][0], h0f, reverse=False)
    hb = rglru(xc, lp['lru_w_a'][1], lp['lru_b_a'][1], lp['lru_w_x'][1], lp['lru_b_x'][1],
               lp['lru_lambda'][1], h0b, reverse=True)
    return hf, hb


def mixer_inputs(h, lp, cos, sin):
    b, t, _ = h.shape
    z = h @ lp['w_in']
    q_lat, kv_lat, k_rope, xb, yb, gl = jnp.split(z, (OFF_KV, OFF_KR, OFF_XB, OFF_YB, OFF_G), axis=-1)
    q = (rms_norm(q_lat, lp['q_norm_g']) @ lp['w_uq']).reshape(b, t, N_HEADS, QK_DIM)
    kv = (rms_norm(kv_lat, lp['kv_norm_g']) @ lp['w_ukv']).reshape(b, t, N_HEADS, QK_NOPE_DIM + V_HEAD_DIM)
    k_nope, v = kv[..., :QK_NOPE_DIM], kv[..., QK_NOPE_DIM:]
    if cos is not None:
        q = jnp.concatenate([q[..., :QK_NOPE_DIM], apply_rope(q[..., QK_NOPE_DIM:], cos[:, None, :], sin[:, None, :])], axis=-1)
        k_rope = apply_rope(k_rope, cos, sin)
    k = jnp.concatenate([k_nope, jnp.broadcast_to(k_rope[:, :, None, :], (b, t, N_HEADS, QK_ROPE_DIM))], axis=-1)
    xc = dwconv(xb, lp['lru_conv_w'], lp['lru_conv_b'], LRU_CONV_W // 2, LRU_CONV_W - 1 - LRU_CONV_W // 2)
    return q, k, v, xc, yb, gl


def merge_out(attn, hf, hb, yb, gl, lp):
    y_a = attn @ lp['w_o_attn']
    y_b = (((hf + hb).astype(yb.dtype)) * jax.nn.gelu(yb)) @ lp['w_o_lru']
    g_a, g_b = jnp.split(jax.nn.sigmoid(gl + lp['b_gate']), 2, axis=-1)
    return (g_a * y_a + g_b * y_b) @ lp['w_out']


def conv_ffn(h, lp):
    u = h @ lp['w_up']
    a, g = jnp.split(u, 2, axis=-1)
    a = dwconv(a, lp['ffn_conv_w'], lp['ffn_conv_b'], FFN_CONV_W // 2, FFN_CONV_W // 2)
    return (jax.nn.silu(a) * g) @ lp['w_down']


def setup_inputs(seed: int = 0) -> dict:
    key = jax.random.key(seed)
    ks = jax.random.split(key, 32)
    f32 = jnp.float32

    def nrm(k, shape, fan_in):
        return jax.random.normal(k, shape, f32) * (fan_in ** -0.5)

    def gain(k, shape):
        return 1.0 + 0.05 * jax.random.normal(k, shape, f32)

    def bias(k, shape):
        return 0.02 * jax.random.normal(k, shape, f32)

    a0 = jax.random.uniform(ks[20], (DEPTH, 2, LRU_WIDTH), f32, 0.9, 0.999)
    return {
        'x': jax.random.normal(ks[0], (BATCH, SEQ, D_MODEL), f32),
        'c': jax.random.normal(ks[1], (BATCH, D_MODEL), f32),
        'ctx': jax.random.normal(ks[2], (BATCH, CTX_LEN, D_MODEL), f32),
        'c_ctx': jax.random.normal(ks[3], (D_MODEL,), f32),
        'w_mod': nrm(ks[4], (DEPTH, D_MODEL, 6 * D_MODEL), D_MODEL),
        'b_mod': bias(ks[5], (DEPTH, 6 * D_MODEL)),
        'norm1_g': gain(ks[6], (DEPTH, D_MODEL)),
        'w_in': nrm(ks[7], (DEPTH, D_MODEL, IN_DIM), D_MODEL),
        'b_gate': bias(ks[8], (DEPTH, 2 * D_MODEL)),
        'q_norm_g': gain(ks[9], (DEPTH, Q_LORA_RANK)),
        'kv_norm_g': gain(ks[10], (DEPTH, KV_LORA_RANK)),
        'w_uq': nrm(ks[11], (DEPTH, Q_LORA_RANK, N_HEADS * QK_DIM), Q_LORA_RANK),
        'w_ukv': nrm(ks[12], (DEPTH, KV_LORA_RANK, N_HEADS * (QK_NOPE_DIM + V_HEAD_DIM)), KV_LORA_RANK),
        'w_o_attn': nrm(ks[13], (DEPTH, N_HEADS * V_HEAD_DIM, D_MODEL), N_HEADS * V_HEAD_DIM),
        'lru_conv_w': nrm(ks[14], (DEPTH, LRU_CONV_W, LRU_WIDTH), LRU_CONV_W),
        'lru_conv_b': bias(ks[15], (DEPTH, LRU_WIDTH)),
        'lru_w_a': nrm(ks[16], (DEPTH, 2, LRU_BLOCKS, LRU_BLOCK_W, LRU_BLOCK_W), LRU_BLOCK_W),
        'lru_b_a': bias(ks[17], (DEPTH, 2, LRU_WIDTH)),
        'lru_w_x': nrm(ks[18], (DEPTH, 2, LRU_BLOCKS, LRU_BLOCK_W, LRU_BLOCK_W), LRU_BLOCK_W),
        'lru_b_x': bias(ks[19], (DEPTH, 2, LRU_WIDTH)),
        'lru_lambda': jnp.log(a0 / (1.0 - a0)),
        'w_o_lru': nrm(ks[21], (DEPTH, LRU_WIDTH, D_MODEL), LRU_WIDTH),
        'w_out': nrm(ks[22], (DEPTH, D_MODEL, D_MODEL), D_MODEL),
        'norm2_g': gain(ks[23], (DEPTH, D_MODEL)),
        'w_up': nrm(ks[24], (DEPTH, D_MODEL, 2 * FFN_DIM), D_MODEL),
        'ffn_conv_w': nrm(ks[25], (DEPTH, FFN_CONV_W, FFN_DIM), FFN_CONV_W),
        'ffn_conv_b': bias(ks[26], (DEPTH, FFN_DIM)),
        'w_down': nrm(ks[27], (DEPTH, FFN_DIM, D_MODEL), FFN_DIM),
        'final_g': gain(ks[28], (D_MODEL,)),
    }


def reference(x, c, ctx, c_ctx, w_mod, b_mod, norm1_g, w_in, b_gate, q_norm_g, kv_norm_g,
              w_uq, w_ukv, w_o_attn, lru_conv_w, lru_conv_b, lru_w_a, lru_b_a, lru_w_x,
              lru_b_x, lru_lambda, w_o_lru, w_out, norm2_g, w_up, ffn_conv_w, ffn_conv_b,
              w_down, final_g):
    b, s, _ = x.shape
    cos, sin = axial_rope_tables(s)
    h_zero = jnp.zeros((b, LRU_WIDTH), jnp.float32)
    for i in range(DEPTH):
        lp = {
            'w_in': w_in[i], 'b_gate': b_gate[i], 'q_norm_g': q_norm_g[i], 'kv_norm_g': kv_norm_g[i],
            'w_uq': w_uq[i], 'w_ukv': w_ukv[i], 'w_o_attn': w_o_attn[i],
            'lru_conv_w': lru_conv_w[i], 'lru_conv_b': lru_conv_b[i],
            'lru_w_a': lru_w_a[i], 'lru_b_a': lru_b_a[i], 'lru_w_x': lru_w_x[i], 'lru_b_x': lru_b_x[i],
            'lru_lambda': lru_lambda[i], 'w_o_lru': w_o_lru[i], 'w_out': w_out[i],
            'w_up': w_up[i], 'ffn_conv_w': ffn_conv_w[i], 'ffn_conv_b': ffn_conv_b[i], 'w_down': w_down[i],
        }
        mod_l = (jax.nn.silu(c) @ w_mod[i] + b_mod[i])[:, None, :]
        mod_c = jax.nn.silu(c_ctx) @ w_mod[i] + b_mod[i]
        sh1_l, sc1_l, g1_l, sh2_l, sc2_l, g2_l = jnp.split(mod_l, 6, axis=-1)
        sh1_c, sc1_c, g1_c, sh2_c, sc2_c, g2_c = jnp.split(mod_c, 6, axis=-1)

        hc = modulate(rms_norm(ctx, norm1_g[i]), sh1_c, sc1_c)
        q_c, k_c, v_c, xc_c, yb_c, gl_c = mixer_inputs(hc, lp, None, None)
        hf_c, hb_c = lru_bidir(xc_c, lp, h_zero, h_zero)
        state_f, state_b = hf_c[:, -1], hb_c[:, 0]

        hl = modulate(rms_norm(x, norm1_g[i]), sh1_l, sc1_l)
        q_l, k_l, v_l, xc_l, yb_l, gl_l = mixer_inputs(hl, lp, cos, sin)
        attn_l = attend_blocks(q_l, jnp.concatenate([k_l, k_c], axis=1), jnp.concatenate([v_l, v_c], axis=1))
        hf_l, hb_l = lru_bidir(xc_l, lp, state_f, state_b)
        x = x + g1_l * merge_out(attn_l, hf_l, hb_l, yb_l, gl_l, lp)
        x = x + g2_l * conv_ffn(modulate(rms_norm(x, norm2_g[i]), sh2_l, sc2_l), lp)

        if i < DEPTH - 1:
            attn_c = sdpa(q_c, k_c, v_c).reshape(b, ctx.shape[1], N_HEADS * V_HEAD_DIM)
            ctx = ctx + g1_c * merge_out(attn_c, hf_c, hb_c, yb_c, gl_c, lp)
            ctx = ctx + g2_c * conv_ffn(modulate(rms_norm(ctx, norm2_g[i]), sh2_c, sc2_c), lp)
    return rms_norm(x, final_g)
```

```python
import numpy as np
from collections import defaultdict
import concourse.bass as bass
import concourse.mybir as mybir
from concourse.bass_utils import run_bass_kernel_spmd

F32 = mybir.dt.float32
BF16 = mybir.dt.bfloat16
AF = mybir.ActivationFunctionType
ALU = mybir.AluOpType

D = 1024
NL = 4096
NCX = 256
NA = NL + NCX
NQ = 2176
NO = 2048
QL, KVL, KR = 384, 256, 32
OFF_KV = QL
OFF_KR = OFF_KV + KVL
OFF_XB = OFF_KR + KR
LW = 1280
OFF_YB = OFF_XB + LW
OFF_G = OFF_YB + LW
IN_DIM = OFF_G + 2 * D
FFN = 2816
NFC = FFN // 128
EPS = 1e-6
SM_SCALE = 96 ** -0.5
GELU_C = 0.7978845608028654

OPTS = {}
ENGS = ['pe', 'act', 'dve', 'pool', 'sp']
BLOCK_ATTR = {'pe': 'tensor', 'act': 'scalar', 'dve': 'vector', 'pool': 'gpsimd', 'sp': 'sync'}
SAME_ENG_WINDOW = 1 << 30
NDMASEM = 8
FUSE_WAIT = True
BIN = 11


def dsize(dt):
    return mybir.dt.size(dt)


class Op:
    __slots__ = ('eng', 'idx', 'fn', 'waits', 'signal', 'dma', 'dsem', 'dcnt', 'snap', 'sigcount', 'dprev')


class Prog:
    def __init__(self, nc):
        self.nc = nc
        self.ops = {e: [] for e in ENGS}
        self.base = {}
        self.bins = {'sb': defaultdict(list), 'ps': defaultdict(list)}
        self.seen = {e: {e2: -1 for e2 in ENGS} for e in ENGS}
        self.seen_dma = {e: set() for e in ENGS}
        self.ndma = {e: 0 for e in ENGS}
        self.dram_w = {}
        self.dram_r = defaultdict(list)

    def region(self, ap):
        t = ap.tensor
        name = t.name
        if name not in self.base:
            return None
        space, base = self.base[name]
        pat = list(ap.ap)
        es = dsize(t.dtype)
        row = pat[0][0]
        off = ap.offset
        if row == 0:
            row = 1 << 40
        p0 = off // row
        f0 = off % row
        p1 = p0 + pat[0][1]
        lo = f0
        hi = f0
        for st, cnt in pat[1:]:
            ext = st * (cnt - 1)
            if ext < 0:
                lo += ext
            else:
                hi += ext
        b0, b1 = base + lo * es, base + (hi + 1) * es
        if space == 'ps':
            b0 = (b0 // 2048) * 2048
            b1 = ((b1 + 2047) // 2048) * 2048
            p0 = (p0 // 32) * 32
            p1 = ((p1 + 31) // 32) * 32
        return (space, p0, p1, b0, b1)

    def _conflicts(self, reg, is_write, out):
        space, p0, p1, b0, b1 = reg
        bins = self.bins[space]
        for bn in range(b0 >> BIN, ((b1 - 1) >> BIN) + 1):
            lst = bins.get(bn)
            if not lst:
                continue
            for rec in lst:
                if rec[1] < p1 and p0 < rec[2] and rec[3] < b1 and b0 < rec[4]:
                    if is_write or rec[0]:
                        out.add(rec[5])

    def _register(self, reg, is_write, opref):
        space, p0, p1, b0, b1 = reg
        bins = self.bins[space]
        rec = (is_write, p0, p1, b0, b1, opref)
        for bn in range(b0 >> BIN, ((b1 - 1) >> BIN) + 1):
            lst = bins[bn]
            if is_write:
                lst[:] = [r for r in lst if not (p0 <= r[1] and r[2] <= p1 and b0 <= r[3] and r[4] <= b1)]
            elif not opref[2]:
                lst[:] = [r for r in lst if not ((not r[0]) and r[1] == p0 and r[2] == p1 and r[3] == b0
                                                 and r[4] == b1 and r[5][0] == opref[0] and not r[5][2])]
            lst.append(rec)

    def emit(self, eng, fn, reads=(), writes=(), dma=False, dram_reads=(), dram_writes=()):
        deps = set()
        rr = [r for r in (self.region(a) for a in reads) if r is not None]
        ww = [r for r in (self.region(a) for a in writes) if r is not None]
        ww = ww + [r for r in rr if r[0] == 'ps']
        rr = [r for r in rr if r[0] != 'ps']
        for r in rr:
            self._conflicts(r, False, deps)
        for r in ww:
            self._conflicts(r, True, deps)
        for tag in dram_reads:
            if tag in self.dram_w:
                deps.add(self.dram_w[tag])
        for tag in dram_writes:
            if tag in self.dram_w:
                deps.add(self.dram_w[tag])
            deps.update(self.dram_r.get(tag, ()))
        op = Op()
        op.eng = eng
        op.idx = len(self.ops[eng])
        op.fn = fn
        op.signal = False
        op.dma = dma
        op.dsem = None
        op.dcnt = 0
        op.sigcount = 0
        op.dprev = None
        seen = self.seen[eng]
        sdma = self.seen_dma[eng]
        waits = []
        if dma:
            k = self.ndma[eng]
            self.ndma[eng] += 1
            op.dsem = (eng, k % NDMASEM)
            op.dcnt = 16 * (k // NDMASEM + 1)
            if k >= NDMASEM:
                prev = self._dma_ops[eng][k - NDMASEM]
                deps.add((eng, prev.idx, True))
            self._dma_ops.setdefault(eng, []).append(op)
        best = {}
        dl = []
        for (e2, i2, d2) in deps:
            if d2:
                dl.append((e2, i2, d2))
            elif i2 > best.get(e2, -1):
                best[e2] = i2
        dl.sort()
        dl += [(e2, i2, False) for e2, i2 in sorted(best.items())]
        for (e2, i2, d2) in dl:
            src = self.ops[e2][i2]
            if d2:
                if (e2, i2) in sdma:
                    continue
                sdma.add((e2, i2))
                waits.append((e2, i2))
            else:
                if e2 == eng:
                    if eng == 'pe' or op.idx - i2 > SAME_ENG_WINDOW:
                        continue
                if i2 <= seen[e2]:
                    continue
                seen[e2] = i2
                src.signal = True
                waits.append((e2, i2))
            for e3, v in src.snap.items():
                if v > seen[e3]:
                    seen[e3] = v
        op.waits = waits
        op.snap = dict(seen)
        self.ops[eng].append(op)
        ref = (eng, op.idx, dma)
        for r in rr:
            self._register(r, False, ref)
        for r in ww:
            self._register(r, True, ref)
        for tag in dram_reads:
            self.dram_r[tag].append(ref)
        for tag in dram_writes:
            self.dram_w[tag] = ref
            self.dram_r[tag] = []
        return op

    _dma_ops = None

    def finalize(self, es):
        nc = self.nc
        self.sem = {e: es.enter_context(nc.semaphore("s_" + e)) for e in ENGS}
        self.dsems = {}
        for e in ENGS:
            if self.ndma[e]:
                for j in range(min(NDMASEM, self.ndma[e])):
                    self.dsems[(e, j)] = es.enter_context(nc.semaphore("d_%s%d" % (e, j)))
        for e in ENGS:
            c = 0
            for op in self.ops[e]:
                if op.signal and not op.dma:
                    c += 1
                    op.sigcount = c
        block = es.enter_context(nc.Block())
        for e in ENGS:
            self._replay_engine(block, e)

    def _replay_engine(self, block, e):
        ops = self.ops[e]
        allops = self.ops
        sem = self.sem
        dsems = self.dsems

        def body(eng):
            for op in ops:
                wl = []
                for (e2, i2) in op.waits:
                    src = allops[e2][i2]
                    if src.dma:
                        wl.append((dsems[src.dsem], src.dcnt))
                    else:
                        wl.append((sem[e2], src.sigcount))
                fuse = wl.pop() if (wl and FUSE_WAIT and not op.dma) else None
                for (sm, v) in wl:
                    eng.wait_ge(sm, v)
                ins = op.fn(eng)
                if fuse is not None:
                    ins._wait_ge(fuse[0], fuse[1])
                if op.dma:
                    ins.then_inc(dsems[op.dsem], 16)
                elif op.signal:
                    ins.then_inc(sem[e], 1)
        getattr(block, BLOCK_ATTR[e])(body)


class Arena:
    def __init__(self, nc, prog, lo=20480, hi=229376):
        self.nc, self.prog, self.lo, self.hi, self.cur = nc, prog, lo, hi, lo
        self.n = 0
        self.peak = lo

    def alloc(self, name, shape, dt):
        per = int(np.prod(shape[1:])) * dsize(dt)
        per = (per + 63) // 64 * 64
        assert self.cur + per <= self.hi, "SBUF arena overflow at %s: need %d have %d" % (name, per, self.hi - self.cur)
        self.n += 1
        t = self.nc.alloc_sbuf_tensor_at("%s_%d_%d" % (name, self.lo, self.n), list(shape), dt, offset=self.cur)
        self.prog.base[t.name] = ('sb', self.cur)
        self.cur += per
        self.peak = max(self.peak, self.cur)
        return t

    def alloc_top(self, name, shape, dt):
        per = int(np.prod(shape[1:])) * dsize(dt)
        per = (per + 63) // 64 * 64
        assert self.hi - per >= self.cur, "SBUF arena overflow (top) at %s" % name
        self.hi -= per
        self.n += 1
        t = self.nc.alloc_sbuf_tensor_at("%s_t%d" % (name, self.n), list(shape), dt, offset=self.hi)
        self.prog.base[t.name] = ('sb', self.hi)
        return t

    def mark(self):
        return (self.cur, self.hi)

    def release(self, m):
        self.cur, self.hi = m


def blocks(lo, hi, step=512):
    out = []
    s = lo
    while s < hi:
        out.append((s, min(step, hi - s)))
        s += step
    return out


def build_program(debug=False, stop_after=None):
    nc = bass.Bass("TRN2", target_bir_lowering=False)
    P = Prog(nc)
    P._dma_ops = {}
    LO0 = 20480
    PERS = 5 * 1024
    HTO = 8 * NQ * 2
    A = Arena(nc, P, LO0, LO0 + PERS)
    AH = Arena(nc, P, LO0 + PERS, LO0 + PERS + HTO)
    Z = Arena(nc, P, LO0 + PERS + HTO, 229312)
    dbg_outs = {}

    def din(name, shape, dt=F32):
        return nc.dram_tensor(name, list(shape), dt, kind="ExternalInput").ap()

    xs = din("xs", [NL, D])
    ctxs = din("ctxs", [NCX, D])
    d_cT = din("cT", [128, 8, 2])
    d_bmodT = din("bmodT", [128, 48])
    d_bmodg = din("bmodg", [128, 2048])
    d_n1g = din("n1g", [128, 8])
    d_n2g = din("n2g", [128, 8])
    d_fing = din("fing", [128, D])
    d_bgate = din("bgate", [128, 16])
    d_qng = din("qng", [128, 3])
    d_kvng = din("kvng", [128, 2])
    d_conv5 = din("conv5", [128, 10, 5])
    d_convb = din("convb", [128, 10])
    d_lba = din("lba", [128, 2, 10])
    d_lbx = din("lbx", [128, 2, 10])
    d_lam = din("lam", [128, 2, 10])
    d_fcw = din("fcw", [128, NFC, 3])
    d_fcb = din("fcb", [128, NFC])
    d_cs = din("cs", [128, 32, 32])
    d_identf = din("identf", [128, 128])
    d_identb = din("identb", [128, 128], BF16)
    w_mod = din("w_mod", [D, 6 * D])
    w_in = din("w_in", [D, IN_DIM])
    w_uq = din("w_uq", [QL, 768])
    w_ukv = din("w_ukv", [KVL, 1024])
    w_o_attn = din("w_o_attn", [512, D])
    w_o_lru = din("w_o_lru", [LW, D])
    w_out = din("w_out", [D, D])
    w_up = din("w_up", [D, 2 * FFN])
    w_down = din("w_down", [FFN, D])
    lru_wa = din("lru_wa", [2, 10, 128, 128])
    lru_wx = din("lru_wx", [2, 10, 128, 128])
    out = nc.dram_tensor("out", [NO, D], F32, kind="ExternalOutput").ap()
    s_dram = nc.dram_tensor("s_scr", [10, 128, NQ], BF16, kind="Internal").ap()
    gbc_dram = nc.dram_tensor("gbc_scr", [128, 2, D], F32, kind="Internal").ap()
    x1_dram = nc.dram_tensor("x1_scr", [NO, D], F32, kind="Internal").ap()

    ps = nc.alloc_psum_tensor("ps", [128, 8, 512], F32)
    P.base[ps.name] = ('ps', 0)
    psb = ps[:, :, :].bitcast(BF16)
    P.base[psb.tensor.name] = ('ps', 0)

    def dma(q, out_, in_, dram_reads=(), dram_writes=()):
        return P.emit(q, lambda e: e.dma_start(out=out_, in_=in_), reads=[in_], writes=[out_], dma=True,
                      dram_reads=dram_reads, dram_writes=dram_writes)

    def mm(out_, lhsT, rhs, start=True, stop=True):
        return P.emit('pe', lambda e: e.matmul(out_, lhsT=lhsT, rhs=rhs, start=start, stop=stop),
                      reads=[lhsT, rhs], writes=[out_])

    def tr(out_, in_, ident):
        return P.emit('pe', lambda e: e.transpose(out=out_, in_=in_, identity=ident), reads=[in_, ident], writes=[out_])

    def act(out_, in_, func, bias=0.0, scale=1.0, accum_out=None, eng='act'):
        rd = [in_]
        if not isinstance(bias, float):
            rd.append(bias)
        if not isinstance(scale, float):
            rd.append(scale)
        wr = [out_]
        kw = {}
        if accum_out is not None:
            wr.append(accum_out)
            kw['accum_out'] = accum_out
        return P.emit('act', lambda e: e.activation(out=out_, in_=in_, func=func, bias=bias, scale=scale, **kw),
                      reads=rd, writes=wr)

    def tt(eng, out_, in0, in1, op):
        return P.emit(eng, lambda e: e.tensor_tensor(out=out_, in0=in0, in1=in1, op=op), reads=[in0, in1], writes=[out_])

    def ts(eng, out_, in0, s1, s2, op0, op1=None):
        rd = [in0]
        if not isinstance(s1, float):
            rd.append(s1)
        if s2 is not None and not isinstance(s2, float):
            rd.append(s2)
        if op1 is None:
            return P.emit(eng, lambda e: e.tensor_scalar(out=out_, in0=in0, scalar1=s1, scalar2=None, op0=op0),
                          reads=rd, writes=[out_])
        return P.emit(eng, lambda e: e.tensor_scalar(out=out_, in0=in0, scalar1=s1, scalar2=s2, op0=op0, op1=op1),
                      reads=rd, writes=[out_])

    def stt(eng, out_, in0, scalar, in1, op0, op1):
        rd = [in0, in1]
        if not isinstance(scalar, float):
            rd.append(scalar)
        return P.emit(eng, lambda e: e.scalar_tensor_tensor(out=out_, in0=in0, scalar=scalar, in1=in1, op0=op0, op1=op1),
                      reads=rd, writes=[out_])

    def cp(eng, out_, in_):
        if eng == 'act':
            return act(out_, in_, AF.Identity)
        return P.emit(eng, lambda e: e.tensor_copy(out=out_, in_=in_), reads=[in_], writes=[out_])

    def memset(eng, ap, val):
        return P.emit(eng, lambda e: e.memset(ap, val), writes=[ap])

    def recip(out_, in_):
        return P.emit('dve', lambda e: e.reciprocal(out=out_, in_=in_), reads=[in_], writes=[out_])

    def scan(out_, d0, d1, initial):
        rd = [d0, d1]
        if not isinstance(initial, float):
            rd.append(initial)
        return P.emit('dve', lambda e: e.tensor_tensor_scan(out=out_, data0=d0, data1=d1, initial=initial,
                                                            op0=ALU.mult, op1=ALU.add), reads=rd, writes=[out_])

    def dump(name, ap, dt=F32):
        if not debug:
            return
        shape = list(ap.shape)
        t = nc.dram_tensor("dbg_" + name, shape, dt, kind="ExternalOutput").ap()
        dbg_outs[name] = t
        dma('sp', t, ap)

    def wload(dst, src):
        return dma('pool', dst, src)

    def _finish():
        P.emit('sp', lambda e: e.nop(), reads=[], writes=[])
        last = P.ops['sp'][-1]
        for o in P._dma_ops.get('sp', [])[-NDMASEM:]:
            if (o.eng, o.idx) not in last.waits:
                last.waits.append((o.eng, o.idx))
        return nc, P, A, dbg_outs

    cT = A.alloc("cT", [128, 8, 2], F32)
    bmodT = A.alloc("bmodT", [128, 48], F32)
    n1g = A.alloc("n1g", [128, 8], F32)
    n2g = A.alloc("n2g", [128, 8], F32)
    bgate = A.alloc("bgate", [128, 16], F32)
    qng = A.alloc("qng", [128, 3], F32)
    kvng = A.alloc("kvng", [128, 2], F32)
    conv5 = A.alloc("conv5", [128, 10, 5], F32)
    convb = A.alloc("convb", [128, 10], F32)
    lba = A.alloc("lba", [128, 2, 10], F32)
    lbx = A.alloc("lbx", [128, 2, 10], F32)
    lam = A.alloc("lam", [128, 2, 10], F32)
    fcw = A.alloc("fcw", [128, NFC, 3], F32)
    fcb = A.alloc("fcb", [128, NFC], F32)
    identf = A.alloc("identf", [128, 128], F32)
    identb = A.alloc("identb", [128, 128], BF16)
    for dst, src in ((cT, d_cT), (bmodT, d_bmodT), (n1g, d_n1g), (n2g, d_n2g), (bgate, d_bgate),
                     (qng, d_qng), (kvng, d_kvng), (conv5, d_conv5), (convb, d_convb), (lba, d_lba), (lbx, d_lbx),
                     (lam, d_lam), (fcw, d_fcw), (fcb, d_fcb), (identf, d_identf), (identb, d_identb)):
        dma('sp', dst[:], src)

    epsv = A.alloc("epsv", [128, 1], F32)
    memset('dve', epsv[:], EPS)
    qtr = A.alloc("qtr", [128, 1], F32)
    memset('dve', qtr[:], 0.25)
    modfm = A.alloc("modfm", [128, 4, 8, 2], F32)
    A1 = A.alloc("A1", [128, 8, 2], F32)
    A2 = A.alloc("A2", [128, 8], F32)
    hbg = A.alloc("hbg", [128, 16], F32)
    hba = A.alloc("hba", [128, 2, 10], F32)
    hbx = A.alloc("hbx", [128, 2, 10], F32)
    c1 = A.alloc("c1", [128, 2, 10], F32)
    hc = A.alloc("hc", [128, 2, 10], F32)
    hTo = AH.alloc("hTo", [128, 8, NQ], BF16)
    hTx = Z.alloc("hTx", [128, 8, NA - NQ], BF16)

    def hTc(k, st, sz):
        if st + sz <= NQ:
            return hTo[:, k, st:st + sz]
        assert st >= NQ
        return hTx[:, k, st - NQ:st - NQ + sz]

    ALLBLK = blocks(0, NQ) + blocks(NQ, NL) + blocks(NL, NA)

    sc = Z.alloc("sc", [128, 8, 2], F32)
    sc_rep = Z.alloc("sc_rep", [128, 8, 128], F32)
    m0 = Z.mark()
    th0 = Z.alloc("th0", [128, 8, 2], F32)
    wm = [Z.alloc("wm%d" % i, [128, 8, 1024], F32) for i in range(2)]
    act(th0[:], cT[:], AF.Tanh, scale=0.5)
    ts('dve', th0[:], th0[:], 0.5, 0.5, ALU.mult, ALU.add)
    tt('dve', sc[:], th0[:], cT[:], ALU.mult)
    for k in range(8):
        cp('dve', sc_rep[:, k, :], sc[:, k, 0:1].to_broadcast([128, 128]))
    act(c1[:], lam[:], AF.Exp, scale=-1.0)
    act(c1[:], c1[:], AF.Ln, bias=1.0)
    ts('dve', hc[:], c1[:], -4.0, None, ALU.mult)
    ts('dve', c1[:], c1[:], -8.0, None, ALU.mult)
    ts('dve', hba[:], lba[:], 0.5, None, ALU.mult)
    ts('dve', hbx[:], lbx[:], 0.5, None, ALU.mult)
    ts('dve', hbg[:], bgate[:], 0.5, None, ALU.mult)
    w_mod_v = w_mod.rearrange("(k p) n -> p k n", p=128)
    fm_idx = {0: 0, 1: 1, 3: 2, 4: 3}

    def mod_slab(s):
        wb = wm[s % 2]
        dma('pool', wb[:], w_mod_v[:, :, s * 1024:(s + 1) * 1024])
        if s in fm_idx:
            bank = 4 + s % 2
            for j in range(8):
                for k in range(8):
                    mm(ps[:, bank, 2 * j:2 * j + 2], wb[:, k, j * 128:(j + 1) * 128], sc[:, k, :], start=(k == 0), stop=(k == 7))
            for col in range(2):
                tt('dve', modfm[:, fm_idx[s], :, col], ps[:, bank, col:16:2], bmodT[:, s * 8:(s + 1) * 8], ALU.add)
        else:
            gi = 0 if s == 2 else 1
            for half in range(2):
                bank = 6 + half
                for k in range(8):
                    mm(ps[:, bank, :], sc_rep[:, k, :], wb[:, k, half * 512:(half + 1) * 512], start=(k == 0), stop=(k == 7))
                tt('dve', gbc[:, gi, half * 512:(half + 1) * 512], ps[:, bank, :],
                   bmodg[:, gi * 1024 + half * 512: gi * 1024 + (half + 1) * 512], ALU.add)

    mod_slab(0)
    mod_slab(1)
    for col in range(2):
        stt('dve', A1[:, :, col], modfm[:, 1, :, col], 1.0, n1g[:], ALU.add, ALU.mult)

    xt = [Z.alloc("xt%d" % i, [128, D], F32) for i in range(3)]
    junk = Z.alloc("junk", [128, D], BF16)
    ssq = Z.alloc("ssq", [128, 34], F32)
    sqv = Z.alloc("sqv", [128, 34], F32)
    rsv = Z.alloc("rsv", [128, 34], F32)
    memset('dve', ssq[:], 0.0)

    def norm_p1(buf, bufo, statc):
        act(junk[:], buf[:], AF.Square, accum_out=ssq[:, statc:statc + 1])
        act(sqv[:, statc:statc + 1], ssq[:, statc:statc + 1], AF.Sqrt, bias=epsv[:], scale=1.0 / D)
        recip(rsv[:, statc:statc + 1], sqv[:, statc:statc + 1])
        ts('dve', bufo[:], buf[:], rsv[:, statc:statc + 1], None, ALU.mult)

    def norm_p2(bufo, dstf, Ascale, Abias, bank0):
        for k in range(8):
            tr(ps[:, bank0 + k // 4, (k % 4) * 128:(k % 4 + 1) * 128], bufo[:, k * 128:(k + 1) * 128], identf[:])
        for k in range(8):
            src = ps[:, bank0 + k // 4, (k % 4) * 128:(k % 4 + 1) * 128]
            if k < 4:
                act(dstf(k), src, AF.Identity, bias=Abias(k), scale=Ascale(k))
            else:
                ts('dve', dstf(k), src, Ascale(k), Abias(k), ALU.mult, ALU.add)

    def norm_to_T(buf, dstf, statc, Ascale, Abias, bank0):
        norm_p1(buf, buf, statc)
        norm_p2(buf, dstf, Ascale, Abias, bank0)

    for it in range(35):
        if it < 34:
            i = it
            buf = xt[i % 3]
            src = xs[i * 128:(i + 1) * 128, :] if i < 32 else ctxs[(i - 32) * 128:(i - 31) * 128, :]
            dma('sp', buf[:], src)
            norm_p1(buf, buf, i)
        if it >= 1:
            i = it - 1
            col = 0 if i < 32 else 1
            norm_p2(xt[i % 3], lambda k, i=i: hTc(k, i * 128, 128),
                    lambda k, col=col: A1[:, k, col:col + 1], lambda k, col=col: modfm[:, 0, k, col:col + 1],
                    2 * (i % 2))
    dump("hT", hTo[:, :, 0:256], BF16)
    dump("hTc", hTx[:, :, NL - NQ:NA - NQ], BF16)
    Z.release(m0)

    if stop_after in ('0', 'A'):
        return _finish()

    mL = Z.mark()
    SB = 1216
    wxb = [Z.alloc("wxb%d" % i, [128, 8, 128], BF16) for i in range(2)]
    wyb = [Z.alloc("wyb%d" % i, [128, 8, 128], BF16) for i in range(2)]
    wg = [Z.alloc("wg%d" % i, [128, 4, 128], BF16) for i in range(2)]
    xbL = Z.alloc("xbL", [128, NL + 4], F32)
    xbC = Z.alloc("xbC", [128, NCX + 4], F32)
    xc = Z.alloc("xc", [128, NA], F32)
    xcb = Z.alloc("xcb", [128, NA], BF16)
    TS = [[Z.alloc("T%d%s" % (j, "ab"[i]), [128, SB], F32) for j in range(3)] for i in range(2)]
    trb = [Z.alloc("trb%d" % i, [128, 512], F32) for i in range(2)]
    hB = Z.alloc("hB", [128, NQ], F32)
    ybs = Z.alloc("ybs", [128, NQ], F32)
    gtmp = Z.alloc("gtmp", [128, NQ], F32)
    sblk = Z.alloc("sblk", [128, NQ], BF16)
    wmp = [Z.alloc("wmp%d" % i, [128, 8, 128], F32) for i in range(2)]
    bmp = [Z.alloc("bmp%d" % i, [128, 128], F32) for i in range(2)]
    gpc = [Z.alloc("gpc%d" % i, [128, 128], F32) for i in range(2)]
    mod_pieces = []
    GB_TAGS = []
    for s_ in (2, 3, 4, 5):
        for j in range(8):
            def piece(s_=s_, j=j):
                i_ = len(piece_cnt)
                piece_cnt.append(1)
                w_ = wmp[i_ % 2]
                dma('pool', w_[:], w_mod_v[:, :, s_ * 1024 + j * 128: s_ * 1024 + (j + 1) * 128])
                b = nbank()
                if s_ in fm_idx:
                    for k in range(8):
                        mm(ps[:, b, 0:2], w_[:, k, :], sc[:, k, :], start=(k == 0), stop=(k == 7))
                    tt('dve', modfm[:, fm_idx[s_], j, :], ps[:, b, 0:2],
                       bmodT[:, s_ * 8 + j:s_ * 8 + j + 1].to_broadcast([128, 2]), ALU.add)
                else:
                    gi = 0 if s_ == 2 else 1
                    bm_, g_ = bmp[i_ % 2], gpc[i_ % 2]
                    dma('sp', bm_[:], d_bmodg[:, gi * 1024 + j * 128: gi * 1024 + (j + 1) * 128])
                    for k in range(8):
                        mm(ps[:, b, 0:128], sc_rep[:, k, :], w_[:, k, :], start=(k == 0), stop=(k == 7))
                    tt('dve', g_[:], ps[:, b, 0:128], bm_[:], ALU.add)
                    if gi == 0:
                        ts('dve', g_[:], g_[:], 0.5, None, ALU.mult)
                    tag = ("gbc", gi, j)
                    GB_TAGS.append(tag)
                    dma('sp', gbc_dram[:, gi, j * 128:(j + 1) * 128], g_[:], dram_writes=[tag])
            mod_pieces.append(piece)
    piece_cnt = []
    memset('pool', xbL[:, 0:2], 0.0)
    memset('pool', xbL[:, NL + 2:NL + 4], 0.0)
    memset('pool', xbC[:, 0:2], 0.0)
    memset('pool', xbC[:, NCX + 2:NCX + 4], 0.0)
    w_in_v = w_in.rearrange("(k p) n -> p k n", p=128)
    psrot = [0]

    def nbank():
        b = psrot[0]
        psrot[0] = (b + 1) % 8
        return b

    def lru_loads(n):
        wload(wxb[n % 2][:], w_in_v[:, :, OFF_XB + n * 128: OFF_XB + (n + 1) * 128])
        wload(wyb[n % 2][:], w_in_v[:, :, OFF_YB + n * 128: OFF_YB + (n + 1) * 128])
        for d in range(2):
            wload(wg[n % 2][:, 2 * d, :], lru_wa[d, n])
            wload(wg[n % 2][:, 2 * d + 1, :], lru_wx[d, n])

    def xb_tasks(n):
        wx_ = wxb[n % 2]
        out = []
        for (st, sz) in blocks(NQ, NL) + blocks(NL, NA) + blocks(0, NQ):
            def task(st=st, sz=sz):
                b = nbank()
                for k in range(8):
                    mm(ps[:, b, 0:sz], wx_[:, k, :], hTc(k, st, sz), start=(k == 0), stop=(k == 7))
                ev = 'dve' if (OPTS.get('DVEEV', 1) and (st // 512) % 2 == 0) else 'act'
                if st < NL:
                    cp(ev, xbL[:, 2 + st:2 + st + sz], ps[:, b, 0:sz])
                else:
                    cp(ev, xbC[:, 2 + st - NL:2 + st - NL + sz], ps[:, b, 0:sz])
            out.append(task)
        return out

    def yb_tasks(n):
        wy_ = wyb[n % 2]
        out = []
        for (st, sz) in blocks(0, NQ):
            def task(st=st, sz=sz):
                b = nbank()
                for k in range(8):
                    mm(ps[:, b, 0:sz], wy_[:, k, :], hTc(k, st, sz), start=(k == 0), stop=(k == 7))
                cp('dve' if OPTS.get('DVEEV', 1) else 'act', ybs[:, st:st + sz], ps[:, b, 0:sz])
            out.append(task)
        return out

    SUBS = [
        (1, [(NL, NA, True, None), (3136, NL, True, None)]),
        (1, [(NQ, 3136, True, None), (NO, NQ, True, ('hB', NO))]),
        (1, [(1024, NO, True, ('hB', 1024))]),
        (1, [(0, 1024, True, ('hB', 0))]),
        (0, [(NL, NA, False, None), (0, 960, False, ('add', 0))]),
        (0, [(960, NQ, False, ('add', 960))]),
    ]

    lru_loads(0)
    lru_loads(1)
    for t in xb_tasks(0):
        t()
    for n in range(10):
        wg_ = wg[n % 2]
        def conv(nn, parts, cast=True):
            for (src, o0, st, sz) in parts:
                dst = xc[:, o0 + st:o0 + st + sz]
                ts('dve', dst, src[:, st:st + sz], conv5[:, nn, 0:1], convb[:, nn:nn + 1], ALU.mult, ALU.add)
                for j in range(1, 5):
                    stt('dve', dst, src[:, st + j:st + j + sz], conv5[:, nn, j:j + 1], dst, ALU.mult, ALU.add)
                if cast:
                    cp('act', xcb[:, o0 + st:o0 + st + sz], dst)

        CE = 2304 if OPTS.get('CSPLIT', 1) else NL
        if n == 0 and CE < NL:
            conv(0, [(xbL, 0, CE, NL - CE)])
        if CE < NL:
            conv(n, [(xbC, NL, 0, NCX), (xbL, 0, NO, CE - NO)])
            own_parts = [(xbL, 0, 0, 1024), (xbL, 0, 1024, 1024)]
        else:
            conv(n, [(xbL, 0, 0, 2048), (xbL, 0, 2048, 2048), (xbC, NL, 0, NCX)])
            own_parts = []
        pend = yb_tasks(n) + (xb_tasks(n + 1) if n + 1 < 10 else [])
        n_after_yb = len(pend) - len(blocks(0, NQ))
        for _ in range(4):
            if mod_pieces:
                pend.append(mod_pieces.pop(0))
        per = (len(pend) + 5) // 6

        def gates(d, segs, tset):
            T1, T2, T3 = tset
            pos = 0
            seginfo = []
            for (lo, hi, rev, outspec) in segs:
                for (st, sz) in blocks(lo, hi):
                    b1_, b2_ = nbank(), nbank()
                    mm(ps[:, b1_, 0:sz], wg_[:, 2 * d, :], xcb[:, st:st + sz])
                    mm(ps[:, b2_, 0:sz], wg_[:, 2 * d + 1, :], xcb[:, st:st + sz])
                    tb = trb[(st // 512) % 2]
                    q0 = pos + (st - lo)
                    act(tb[:, 0:sz], ps[:, b1_, 0:sz], AF.Tanh, bias=hba[:, d, n:n + 1], scale=0.5)
                    act(T1[:, q0:q0 + sz], ps[:, b2_, 0:sz], AF.Tanh, bias=hbx[:, d, n:n + 1], scale=0.5)
                    act(T2[:, q0:q0 + sz], tb[:, 0:sz], AF.Exp, bias=hc[:, d, n:n + 1], scale=hc[:, d, n:n + 1])
                    act(T3[:, q0:q0 + sz], tb[:, 0:sz], AF.Exp, bias=c1[:, d, n:n + 1], scale=c1[:, d, n:n + 1])
                    stt('dve', T1[:, q0:q0 + sz], T1[:, q0:q0 + sz], 1.0, xc[:, st:st + sz], ALU.add, ALU.mult)
                seginfo.append((pos, hi - lo, rev, outspec))
                pos += hi - lo
            return seginfo, pos

        def sqrt_u(tset, L):
            T1, T2, T3 = tset
            act(T3[:, 0:L], T3[:, 0:L], AF.Sqrt, bias=qtr[:], scale=-0.25)
            tt('dve', T1[:, 0:L], T1[:, 0:L], T3[:, 0:L], ALU.mult)

        def scans(tset, seginfo, cur):
            T1, T2, T3 = tset
            for (p0, ln, rev, outspec) in seginfo:
                if outspec is not None and outspec[0] == 'hB':
                    o = hB[:, outspec[1]:outspec[1] + ln]
                else:
                    o = T3[:, p0:p0 + ln]
                a_, u_ = T2[:, p0:p0 + ln], T1[:, p0:p0 + ln]
                if rev:
                    scan(o[:, ::-1], a_[:, ::-1], u_[:, ::-1], cur)
                    cur = o[:, 0:1]
                else:
                    scan(o, a_, u_, cur)
                    cur = o[:, ln - 1:ln]
                if outspec is not None and outspec[0] == 'add':
                    c0 = outspec[1]
                    tt('dve', hB[:, c0:c0 + ln], hB[:, c0:c0 + ln], T3[:, p0:p0 + ln], ALU.add)
            return cur

        cur = 0.0
        for pr in range(3):
            infos = []
            for j in range(2):
                d, segs = SUBS[2 * pr + j]
                infos.append(gates(d, segs, TS[j]))
                if own_parts:
                    pA, pB = own_parts
                    if pr == 0 and j == 0:
                        conv(n, [pB], cast=False)
                    if pr == 0 and j == 1:
                        cp('act', xcb[:, pB[2]:pB[2] + pB[3]], xc[:, pB[2]:pB[2] + pB[3]])
                    if pr == 1 and j == 0:
                        cp('act', xcb[:, pA[2]:pA[2] + pA[3]], xc[:, pA[2]:pA[2] + pA[3]])
                for _ in range(per):
                    if pend:
                        pend.pop(0)()
            if pr == 2:
                cur = 0.0
            for j in range(2):
                sqrt_u(TS[j], infos[j][1])
            for j in range(2):
                cur = scans(TS[j], infos[j][0], cur)
            if pr == 0 and own_parts:
                conv(n, [own_parts[0]], cast=False)
            if pr == 1 and n + 1 < 10 and CE < NL:
                conv(n + 1, [(xbL, 0, CE, NL - CE)])
            if pr == 0:
                while len(pend) > n_after_yb:
                    pend.pop(0)()
                if not OPTS.get('ACTGELU', 1):
                    tt('dve', gtmp[:], ybs[:], ybs[:], ALU.mult)
                    ts('dve', gtmp[:], gtmp[:], 0.044715, 1.0, ALU.mult, ALU.add)
                    tt('dve', gtmp[:], gtmp[:], ybs[:], ALU.mult)
            if pr == 1:
                if OPTS.get('ACTGELU', 1):
                    act(gtmp[:], ybs[:], AF.Gelu_apprx_tanh)
                else:
                    act(gtmp[:], gtmp[:], AF.Tanh, scale=GELU_C)
                    stt('dve', gtmp[:], gtmp[:], 1.0, ybs[:], ALU.add, ALU.mult)
        while pend:
            pend.pop(0)()
        if n == 0:
            dump("xc0", xc[:, 0:512])
            dump("xc0c", xc[:, NL:NA])
            dump("hsum0", hB[:])
        if n + 2 < 10:
            lru_loads(n + 2)
        if OPTS.get('ACTGELU', 1):
            tt('dve', sblk[:], gtmp[:], hB[:], ALU.mult)
        else:
            stt('dve', sblk[:], gtmp[:], 0.5, hB[:], ALU.mult, ALU.mult)
        dma('sp', s_dram[n], sblk[:], dram_writes=[("s", n)])
        if n == 0:
            dump("s0", sblk[:], BF16)
    assert not mod_pieces
    stt('dve', A2[:], modfm[:, 3, :, 0], 1.0, n2g[:], ALU.add, ALU.mult)
    dump("modfm", modfm[:])
    Z.release(mL)

    if stop_after == 'L':
        return _finish()

    qT = Z.alloc_top("qT", [96, 8, NQ], BF16)
    krT = Z.alloc_top("krT", [96, NA], BF16)
    kvT = Z.alloc_top("kvT", [128, 2, NA], BF16)
    cs = Z.alloc_top("cs", [128, 32, 32], F32)
    wuq = Z.alloc_top("wuq", [128, 3, 768], BF16)
    wK = Z.alloc_top("wK", [128, 2, 8, 64], BF16)
    wV = Z.alloc_top("wV", [128, 2, 8, 64], BF16)
    dma('sp', cs[:], d_cs)
    mB = Z.mark()
    stg_q = Z.alloc("stg_q", [128, 3, 768], F32)
    stg_kv = Z.alloc("stg_kv", [128, 2, 8, 128], F32)
    dma('sp', stg_q[:], w_uq.rearrange("(k p) n -> p k n", p=128))
    dma('sp', stg_kv[:], w_ukv.rearrange("(k p) (h j) -> p k h j", p=128, j=128))
    for k in range(3):
        ts('dve', wuq[:, k, :], stg_q[:, k, :], qng[:, k:k + 1], None, ALU.mult)
    for k in range(2):
        ts('dve', wK[:, k, :, :], stg_kv[:, k, :, 0:64], kvng[:, k:k + 1], None, ALU.mult)
        ts('dve', wV[:, k, :, :], stg_kv[:, k, :, 64:128], kvng[:, k:k + 1], None, ALU.mult)
    Z.release(mB)
    qlT = Z.alloc("qlT", [128, 3, NQ], BF16)
    wA = Z.alloc("wA", [128, 8, 672], BF16)
    wload(wA[:], w_in_v[:, :, 0:672])
    latn = [Z.alloc("latn%d" % i, [128, 672], BF16) for i in range(2)]
    ssq2 = Z.alloc("ssq2", [128, 34, 2], F32)
    sq2 = Z.alloc("sq2", [128, 34, 2], F32)
    rs2 = Z.alloc("rs2", [128, 34, 2], F32)
    rtmp = Z.alloc("rtmp", [128, 4, 16], F32)
    junk2 = Z.alloc("junk2", [128, 384], BF16)
    qtm = [Z.alloc("qtm%d" % i, [128, 8, 96], BF16) for i in range(2)]
    rq = Z.alloc("rq", [128, 4, 4, 16], F32)
    memset('dve', ssq2[:], 0.0)
    b1state = {}

    def b1_s12(i):
        own = i < 17
        col0 = i * 128 if i < 32 else NL + (i - 32) * 128
        ln_ = latn[i % 2]
        bq, bk = nbank(), nbank()
        if own:
            for k in range(8):
                mm(ps[:, bq, 0:QL], hTc(k, col0, 128), wA[:, k, 0:QL], start=(k == 0), stop=(k == 7))
        for k in range(8):
            mm(ps[:, bk, 0:288], hTc(k, col0, 128), wA[:, k, QL:672], start=(k == 0), stop=(k == 7))
        if own:
            act(junk2[:, 0:QL], ps[:, bq, 0:QL], AF.Square, accum_out=ssq2[:, i, 0:1])
            act(sq2[:, i, 0:1], ssq2[:, i, 0:1], AF.Sqrt, bias=epsv[:], scale=1.0 / QL)
            recip(rs2[:, i, 0:1], sq2[:, i, 0:1])
            act(ln_[:, 0:QL], ps[:, bq, 0:QL], AF.Identity, scale=rs2[:, i, 0:1])
        act(junk2[:, 0:KVL], ps[:, bk, 0:KVL], AF.Square, accum_out=ssq2[:, i, 1:2])
        act(sq2[:, i, 1:2], ssq2[:, i, 1:2], AF.Sqrt, bias=epsv[:], scale=1.0 / KVL)
        recip(rs2[:, i, 1:2], sq2[:, i, 1:2])
        ts('dve', ln_[:, QL:QL + KVL], ps[:, bk, 0:KVL], rs2[:, i, 1:2], None, ALU.mult)
        if i < 32:
            x1_, x2_ = ps[:, bk, 256:272], ps[:, bk, 272:288]
            cos_, sin_ = cs[:, i, 0:16], cs[:, i, 16:32]
            tt('dve', rtmp[:, 0, :], x1_, cos_, ALU.mult)
            tt('dve', rtmp[:, 1, :], x2_, sin_, ALU.mult)
            tt('dve', rtmp[:, 2, :], x2_, cos_, ALU.mult)
            tt('dve', rtmp[:, 3, :], x1_, sin_, ALU.mult)
            tt('dve', ln_[:, 640:656], rtmp[:, 0, :], rtmp[:, 1, :], ALU.subtract)
            tt('dve', ln_[:, 656:672], rtmp[:, 2, :], rtmp[:, 3, :], ALU.add)
        else:
            cp('dve', ln_[:, 640:672], ps[:, bk, 256:288])

    def b1_s34(i):
        own = i < 17
        col0 = i * 128 if i < 32 else NL + (i - 32) * 128
        ln_ = latn[i % 2]
        bt = nbank()
        if own:
            for k in range(3):
                tr(psb[:, bt, k * 128:(k + 1) * 128], ln_[:, k * 128:(k + 1) * 128], identb[:])
        for k in range(2):
            tr(psb[:, bt, (3 + k) * 128:(4 + k) * 128], ln_[:, QL + k * 128:QL + (k + 1) * 128], identb[:])
        tr(psb[0:96, bt, 640:768], ln_[:, 576:672], identb[:])
        if own:
            cp('act', qlT[:, :, col0:col0 + 128], psb[:, bt, 0:384].rearrange("p (k c) -> p k c", c=128))
        cp('act', kvT[:, :, col0:col0 + 128], psb[:, bt, 384:640].rearrange("p (k c) -> p k c", c=128))
        cp('act', krT[64:96, col0:col0 + 128], psb[64:96, bt, 640:768])

    for it in range(35):
        if it < 34:
            b1_s12(it)
        if it >= 1:
            b1_s34(it - 1)
    dump("qlT", qlT[:, :, 0:256], BF16)
    dump("kvT", kvT[:, :, 0:256], BF16)
    dump("krT", krT[64:96, 0:256], BF16)
    def q_s12(i):
        col0 = i * 128
        qt_ = qtm[i % 2]
        bb = [nbank(), nbank()]
        for hh in range(2):
            for k in range(3):
                mm(ps[:, bb[hh], 0:384], qlT[:, k, col0:col0 + 128], wuq[:, k, hh * 384:(hh + 1) * 384],
                   start=(k == 0), stop=(k == 2))
        for hh in range(2):
            src = ps[:, bb[hh], 0:384].rearrange("p (h d) -> p h d", d=96)
            x1_, x2_ = src[:, :, 64:80], src[:, :, 80:96]
            cos_ = cs[:, i:i + 1, 0:16].to_broadcast([128, 4, 16])
            sin_ = cs[:, i:i + 1, 16:32].to_broadcast([128, 4, 16])
            cp('dve', qt_[:, hh * 4:(hh + 1) * 4, 0:64], src[:, :, 0:64])
            tt('dve', rq[:, 0], x1_, cos_, ALU.mult)
            tt('dve', rq[:, 1], x2_, sin_, ALU.mult)
            tt('dve', rq[:, 2], x2_, cos_, ALU.mult)
            tt('dve', rq[:, 3], x1_, sin_, ALU.mult)
            tt('dve', qt_[:, hh * 4:(hh + 1) * 4, 64:80], rq[:, 0], rq[:, 1], ALU.subtract)
            tt('dve', qt_[:, hh * 4:(hh + 1) * 4, 80:96], rq[:, 2], rq[:, 3], ALU.add)

    def q_s34(i):
        col0 = i * 128
        qt_ = qtm[i % 2]
        bt = nbank()
        for h in range(8):
            tr(psb[0:96, bt, h * 128:(h + 1) * 128], qt_[:, h, :], identb[:])
        cp('act', qT[:, :, col0:col0 + 128], psb[0:96, bt, :].rearrange("p (h c) -> p h c", c=128))

    for it in range(18):
        if it < 17:
            q_s12(it)
        if it >= 1:
            q_s34(it - 1)
    dump("qT", qT[:, :, 0:256], BF16)
    Z.release(mB)

    if stop_after == 'B':
        return _finish()

    Z.cur = Z.lo
    OT = Z.alloc("OT", [128, 4, NQ], BF16)
    mT = Z.mark()
    Kt = Z.alloc("Kt", [96, 2, NA], BF16)
    Vp = Z.alloc("Vp", [128, 34, 3, 64], BF16)
    pT = [Z.alloc("pT%d" % i, [128, 2, 512], BF16) for i in range(3)]
    rd = Z.alloc("rd", [128, 512], F32)
    memset('pool', Vp[:, :, 1, :], 1.0)
    unit = 0
    for p in range(4):
        for hh in range(2):
            h = 2 * p + hh
            for (st, sz) in blocks(0, NA):
                b = 6 + (st // 512) % 2
                for k in range(2):
                    mm(ps[0:64, b, 0:sz], wK[:, k, h, :], kvT[:, k, st:st + sz], start=(k == 0), stop=(k == 1))
                cp('dve' if (st // 512) % 2 else 'act', Kt[0:64, hh, st:st + sz], ps[0:64, b, 0:sz])
            cp('dve', Kt[64:96, hh, :], krT[64:96, :])
        for g in range(9):
            t0 = g * 4
            nt = min(4, 34 - t0)
            b = 6 + g % 2
            for t in range(nt):
                c0 = (t0 + t) * 128
                for k in range(2):
                    mm(ps[:, b, t * 128:(t + 1) * 128], kvT[:, k, c0:c0 + 128],
                       wV[:, k, 2 * p:2 * p + 2, :].rearrange("p h d -> p (h d)"), start=(k == 0), stop=(k == 1))
            cp('dve', Vp[:, t0:t0 + nt, 0:3:2, :],
               ps[:, b, 0:nt * 128].rearrange("p (t h d) -> p t h d", h=2, d=64))
        if p == 0:
            dump("Kt0", Kt[:, :, 0:256], BF16)
            dump("Vp0", Vp[:, 0:2, :, :], BF16)
        for hh in range(2):
            h = 2 * p + hh
            vsel = slice(0, 2) if hh == 0 else slice(1, 3)
            for (qs, qz) in blocks(0, NQ):
                ob = 4 + unit % 2
                unit += 1
                pend = None
                for ktp in range(17):
                    sb0 = 2 * (ktp % 2)
                    for j in range(2):
                        kt = 2 * ktp + j
                        mm(ps[:, sb0 + j, 0:qz], Kt[0:96, hh, kt * 128:(kt + 1) * 128], qT[0:96, h, qs:qs + qz])
                    pt_ = pT[ktp % 3]
                    act(pt_[:, :, 0:qz], ps[:, sb0:sb0 + 2, 0:qz], AF.Exp, scale=SM_SCALE)
                    if pend is not None:
                        pend()

                    def pv(ktp=ktp, pt_=pt_):
                        for j in range(2):
                            kt = 2 * ktp + j
                            mm(ps[:, ob, 0:qz], Vp[:, kt, vsel, :].rearrange("p a d -> p (a d)"), pt_[:, j, 0:qz],
                               start=(kt == 0), stop=(kt == 33))
                    pend = pv
                pend()
                if hh == 0:
                    recip(rd[0:64, 0:qz], ps[64:128, ob, 0:qz])
                    tt('dve', OT[0:64, p, qs:qs + qz], ps[0:64, ob, 0:qz], rd[0:64, 0:qz], ALU.mult)
                else:
                    recip(rd[64:128, 0:qz], ps[0:64, ob, 0:qz])
                    tt('dve', OT[64:128, p, qs:qs + qz], ps[64:128, ob, 0:qz], rd[64:128, 0:qz], ALU.mult)
    dump("OT", OT[:, :, 0:256], BF16)
    Z.release(mT)
    Z.hi = 229312

    if stop_after == 'T':
        return _finish()

    mix = Z.alloc("mix", [128, 8, NQ], BF16)
    wout = Z.alloc("wout", [128, 8, D], BF16)
    mM = Z.mark()
    s_sb = Z.alloc("s_sb", [128, 10, NQ], BF16)
    wsl = [Z.alloc("wsl%d" % i, [128, 30, 128], BF16) for i in range(2)]
    tg = [Z.alloc("tg%d" % i, [128, 2, 512], F32) for i in range(2)]
    for n in range(10):
        dma('sp', s_sb[:, n, :], s_dram[n], dram_reads=[("s", n)])
    wload(wout[:], w_out.rearrange("(k p) n -> p k n", p=128))
    woa_v = w_o_attn.rearrange("(k p) n -> p k n", p=128)
    wol_v = w_o_lru.rearrange("(k p) n -> p k n", p=128)

    def merge_loads(m):
        w = wsl[m % 2]
        wload(w[:, 0:4, :], woa_v[:, :, m * 128:(m + 1) * 128])
        wload(w[:, 4:14, :], wol_v[:, :, m * 128:(m + 1) * 128])
        wload(w[:, 14:22, :], w_in_v[:, :, OFF_G + m * 128:OFF_G + (m + 1) * 128])
        wload(w[:, 22:30, :], w_in_v[:, :, OFF_G + D + m * 128:OFF_G + D + (m + 1) * 128])

    merge_loads(0)
    it = 0
    for m in range(8):
        if m + 1 < 8:
            merge_loads(m + 1)
        w = wsl[m % 2]
        for (st, sz) in blocks(0, NQ):
            b0 = 4 * (it % 2)
            tg_ = tg[it % 2]
            mt_ = tg_
            it += 1
            for k in range(4):
                mm(ps[:, b0, 0:sz], w[:, k, :], OT[:, k, st:st + sz], start=(k == 0), stop=(k == 3))
            for k in range(10):
                mm(ps[:, b0 + 1, 0:sz], w[:, 4 + k, :], s_sb[:, k, st:st + sz], start=(k == 0), stop=(k == 9))
            for k in range(8):
                mm(ps[:, b0 + 2, 0:sz], w[:, 14 + k, :], hTo[:, k, st:st + sz], start=(k == 0), stop=(k == 7))
            for k in range(8):
                mm(ps[:, b0 + 3, 0:sz], w[:, 22 + k, :], hTo[:, k, st:st + sz], start=(k == 0), stop=(k == 7))
            act(tg_[:, 0, 0:sz], ps[:, b0 + 2, 0:sz], AF.Tanh, bias=hbg[:, m:m + 1], scale=0.5)
            act(tg_[:, 1, 0:sz], ps[:, b0 + 3, 0:sz], AF.Tanh, bias=hbg[:, 8 + m:9 + m], scale=0.5)
            stt('dve', mt_[:, 0, 0:sz], tg_[:, 0, 0:sz], 1.0, ps[:, b0, 0:sz], ALU.add, ALU.mult)
            stt('dve', mt_[:, 1, 0:sz], tg_[:, 1, 0:sz], 1.0, ps[:, b0 + 1, 0:sz], ALU.add, ALU.mult)
            tt('dve', mix[:, m, st:st + sz], mt_[:, 0, 0:sz], mt_[:, 1, 0:sz], ALU.add)
    dump("mix", mix[:, :, 0:256], BF16)
    h2T = hTo
    Z.release(mM)
    xr = [Z.alloc("xr%d" % i, [128, D], F32) for i in range(3)]
    x1t = [Z.alloc("x1t%d" % i, [128, D], F32) for i in range(3)]
    xn2 = [Z.alloc("xn2%d" % i, [128, D], F32) for i in range(3)]
    junk3 = Z.alloc("junk3", [128, D], BF16)
    g1h = Z.alloc("g1h", [128, D], F32)
    dma('sp', g1h[:], gbc_dram[:, 0, :], dram_reads=[t for t in GB_TAGS if t[1] == 0])
    ssq = Z.alloc("ssq3", [128, 34], F32)
    sqv = Z.alloc("sqv3", [128, 34], F32)
    rsv = Z.alloc("rsv3", [128, 34], F32)
    junk = junk3
    memset('dve', ssq[:], 0.0)

    def m2_s1(i):
        xr_, x1_, xn_ = xr[i % 3], x1t[i % 3], xn2[i % 3]
        dma('sp', xr_[:], xs[i * 128:(i + 1) * 128, :])
        for half in range(2):
            b = 4 + 2 * (i % 2) + half
            hs = slice(half * 512, (half + 1) * 512)
            for k in range(8):
                mm(ps[:, b, :], mix[:, k, i * 128:(i + 1) * 128], wout[:, k, hs], start=(k == 0), stop=(k == 7))
            tt('dve', x1_[:, hs], ps[:, b, :], g1h[:, hs], ALU.mult)
            tt('dve', x1_[:, hs], x1_[:, hs], xr_[:, hs], ALU.add)
        if i < 16:
            dma('sp', x1_dram[i * 128:(i + 1) * 128, :], x1_[:], dram_writes=[("x1", i)])
        if i == 0:
            dump("x1_0", x1_[:])
        norm_p1(x1_, xn_, i)

    def m2_s2(i):
        norm_p2(xn2[i % 3], lambda k, i=i: h2T[:, k, i * 128:(i + 1) * 128],
                lambda k: A2[:, k:k + 1], lambda k: modfm[:, 2, k, 0:1], 2 * (i % 2))

    for it in range(18):
        if it < 17:
            m2_s1(it)
        if it >= 1:
            m2_s2(it - 1)
    dump("h2T", h2T[:, :, 0:256], BF16)
    Z.cur = Z.lo
    if stop_after == 'M':
        return _finish()

    actb = Z.alloc("actb", [128, NFC, NO], BF16)
    mF = Z.mark()
    wup = [Z.alloc("wup%d" % i, [128, 2, 8, 128], BF16) for i in range(2)]
    abuf = Z.alloc("abuf", [128, NQ + 2], F32)
    acv = Z.alloc("acv", [128, NO], F32)
    sgv = Z.alloc("sgv", [128, NO], F32)
    memset('pool', abuf[:, 0:1], 0.0)
    w_up_v = w_up.rearrange("(k p) n -> p k n", p=128)

    def ffn_loads(c):
        wload(wup[c % 2][:, 0], w_up_v[:, :, c * 128:(c + 1) * 128])
        wload(wup[c % 2][:, 1], w_up_v[:, :, FFN + c * 128:FFN + (c + 1) * 128])

    ffn_loads(0)
    for c in range(NFC):
        if c + 1 < NFC:
            ffn_loads(c + 1)
        wu = wup[c % 2]
        for (st, sz) in blocks(0, NQ):
            b = nbank()
            for k in range(8):
                mm(ps[:, b, 0:sz], wu[:, 0, k, :], h2T[:, k, st:st + sz], start=(k == 0), stop=(k == 7))
            cp('act', abuf[:, 1 + st:1 + st + sz], ps[:, b, 0:sz])
        ts('dve', acv[:], abuf[:, 0:NO], fcw[:, c, 0:1], fcb[:, c:c + 1], ALU.mult, ALU.add)
        stt('dve', acv[:], abuf[:, 1:NO + 1], fcw[:, c, 1:2], acv[:], ALU.mult, ALU.add)
        stt('dve', acv[:], abuf[:, 2:NO + 2], fcw[:, c, 2:3], acv[:], ALU.mult, ALU.add)
        act(sgv[:], acv[:], AF.Tanh, scale=0.5)
        stt('dve', sgv[:], sgv[:], 1.0, acv[:], ALU.add, ALU.mult)
        for (st, sz) in blocks(0, NO):
            b = nbank()
            for k in range(8):
                mm(ps[:, b, 0:sz], wu[:, 1, k, :], h2T[:, k, st:st + sz], start=(k == 0), stop=(k == 7))
            stt('dve', actb[:, c, st:st + sz], sgv[:, st:st + sz], 0.5, ps[:, b, 0:sz], ALU.mult, ALU.mult)
        if c == 0:
            dump("act0", actb[:, 0, 0:256], BF16)
    Z.release(mF)
    wdn = Z.alloc("wdn", [128, NFC, D], BF16)
    wload(wdn[:], w_down.rearrange("(k p) n -> p k n", p=128))
    AH.cur = AH.lo
    x1l = [AH.alloc("x1l%d" % i, [128, D], F32) for i in range(2)]
    x2t = [AH.alloc("x2t%d" % i, [128, D], F32) for i in range(2)]
    junk4 = AH.alloc("junk4", [128, D], BF16)
    g2b = Z.alloc("g2b", [128, D], F32)
    fing = Z.alloc("fing", [128, D], F32)
    dma('sp', g2b[:], gbc_dram[:, 1, :], dram_reads=[t for t in GB_TAGS if t[1] == 1])
    dma('sp', fing[:], d_fing)
    ssq4 = Z.alloc("ssq4", [128, 16], F32)
    sq4 = Z.alloc("sq4", [128, 16], F32)
    rs4 = Z.alloc("rs4", [128, 16], F32)
    memset('dve', ssq4[:], 0.0)
    for i in range(16):
        xl, x2 = x1l[i % 2], x2t[i % 2]
        dma('sp', xl[:], x1_dram[i * 128:(i + 1) * 128, :], dram_reads=[("x1", i)])
        for half in range(2):
            b = 2 * (i % 2) + half
            for c in range(NFC):
                mm(ps[:, b, :], actb[:, c, i * 128:(i + 1) * 128], wdn[:, c, half * 512:(half + 1) * 512],
                   start=(c == 0), stop=(c == NFC - 1))
            tt('dve', x2[:, half * 512:(half + 1) * 512], ps[:, b, :], g2b[:, half * 512:(half + 1) * 512], ALU.mult)
            tt('pool', x2[:, half * 512:(half + 1) * 512], x2[:, half * 512:(half + 1) * 512],
               xl[:, half * 512:(half + 1) * 512], ALU.add)
        act(junk4[:], x2[:], AF.Square, accum_out=ssq4[:, i:i + 1])
        act(sq4[:, i:i + 1], ssq4[:, i:i + 1], AF.Sqrt, bias=epsv[:], scale=1.0 / D)
        recip(rs4[:, i:i + 1], sq4[:, i:i + 1])
        stt('dve', x2[:], x2[:], rs4[:, i:i + 1], fing[:], ALU.mult, ALU.mult)
        dma('sp', out[i * 128:(i + 1) * 128, :], x2[:])
    return _finish()


def _fm(v):
    v = np.asarray(v, np.float32)
    n = v.shape[-1] // 128
    return np.ascontiguousarray(v.reshape(n, 128).T)


def _rope_tables(pos):
    inv = (1.0 / (np.float32(10000.0) ** (np.arange(0, 16, 2, dtype=np.float32) / np.float32(16)))).astype(np.float32)
    row = (pos // 64).astype(np.float32)
    colp = (pos % 64).astype(np.float32)
    ang = np.concatenate([row[:, None] * inv[None, :], colp[:, None] * inv[None, :]], axis=-1).astype(np.float32)
    return np.cos(ang).astype(np.float32), np.sin(ang).astype(np.float32)


def make_in_maps(x, c, ctx, c_ctx, w_mod, b_mod, norm1_g, w_in, b_gate, q_norm_g, kv_norm_g,
                 w_uq, w_ukv, w_o_attn, lru_conv_w, lru_conv_b, lru_w_a, lru_b_a, lru_w_x,
                 lru_b_x, lru_lambda, w_o_lru, w_out, norm2_g, w_up, ffn_conv_w, ffn_conv_b,
                 w_down, final_g):
    import ml_dtypes
    f = lambda a: np.ascontiguousarray(np.asarray(a, np.float32))
    x, c, ctx, c_ctx = f(x), f(c), f(ctx), f(c_ctx)
    shared = {
        "w_mod": f(w_mod[0]), "w_in": f(w_in[0]), "w_uq": f(w_uq[0]), "w_ukv": f(w_ukv[0]),
        "w_o_attn": f(w_o_attn[0]), "w_o_lru": f(w_o_lru[0]), "w_out": f(w_out[0]), "w_up": f(w_up[0]),
        "w_down": f(w_down[0]),
        "bmodT": _fm(f(b_mod[0])),
        "bmodg": np.ascontiguousarray(np.broadcast_to(
            np.concatenate([f(b_mod[0])[2 * D:3 * D], f(b_mod[0])[5 * D:6 * D]])[None, :], (128, 2048))),
        "n1g": _fm(norm1_g[0]), "n2g": _fm(norm2_g[0]),
        "fing": np.ascontiguousarray(np.broadcast_to(f(final_g)[None, :], (128, D))),
        "bgate": _fm(b_gate[0]), "qng": _fm(q_norm_g[0]), "kvng": _fm(kv_norm_g[0]),
        "convb": _fm(lru_conv_b[0]), "fcb": _fm(ffn_conv_b[0]),
        "identf": np.eye(128, dtype=np.float32),
        "identb": np.eye(128, dtype=np.float32).astype(ml_dtypes.bfloat16),
    }
    lcw = f(lru_conv_w[0])
    fcw_ = f(ffn_conv_w[0])
    in_maps = []
    for core in range(8):
        b, half = core // 2, core % 2
        m = dict(shared)
        xb_ = x[b]
        cx = ctx[b]
        if half == 1:
            xb_ = xb_[::-1]
            cx = cx[::-1]
        m["xs"] = np.ascontiguousarray(xb_)
        m["ctxs"] = np.ascontiguousarray(cx)
        cT = np.stack([_fm(c[b]), _fm(c_ctx)], axis=-1)
        m["cT"] = np.ascontiguousarray(cT)
        dirs = [0, 1] if half == 0 else [1, 0]
        m["lru_wa"] = np.ascontiguousarray(f(lru_w_a[0])[dirs])
        m["lru_wx"] = np.ascontiguousarray(f(lru_w_x[0])[dirs])
        m["lba"] = np.ascontiguousarray(np.stack([_fm(f(lru_b_a[0])[d]) for d in dirs], axis=1))
        m["lbx"] = np.ascontiguousarray(np.stack([_fm(f(lru_b_x[0])[d]) for d in dirs], axis=1))
        m["lam"] = np.ascontiguousarray(np.stack([_fm(f(lru_lambda[0])[d]) for d in dirs], axis=1))
        w5 = np.zeros((5, LW), np.float32)
        if half == 0:
            w5[0:4] = lcw
        else:
            w5[1:5] = lcw[::-1]
        m["conv5"] = np.ascontiguousarray(np.stack([_fm(w5[j]) for j in range(5)], axis=-1))
        w3 = fcw_ if half == 0 else fcw_[::-1]
        m["fcw"] = np.ascontiguousarray(np.stack([_fm(w3[j]) for j in range(3)], axis=-1))
        pos = np.arange(NL)
        if half == 1:
            pos = NL - 1 - pos
        cos, sin = _rope_tables(pos)
        cs = np.concatenate([cos, sin], axis=-1).reshape(32, 128, 32).transpose(1, 0, 2)
        m["cs"] = np.ascontiguousarray(cs)
        in_maps.append(m)
    return in_maps


_CACHE = {}


def kernel(**inputs):
    in_maps = make_in_maps(**inputs)
    if "nc" not in _CACHE:
        from contextlib import ExitStack
        nc, P, A, _ = build_program(False)
        es = ExitStack()
        P.finalize(es)
        _CACHE["nc"] = nc
        _CACHE["es"] = es
    nc = _CACHE["nc"]
    res = run_bass_kernel_spmd(nc, in_maps, core_ids=list(range(8)))
    B = 4
    outp = np.zeros((B, NL, D), np.float32)
    for core in range(8):
        b, half = core // 2, core % 2
        o = np.asarray(res.results[core]["out"], np.float32)
        if half == 0:
            outp[b, 0:NO] = o
        else:
            outp[b, NO:NL] = o[::-1]
    return outp
```

```python
import numpy as np
from collections import defaultdict
import concourse.bass as bass
import concourse.mybir as mybir
from concourse.bass_utils import run_bass_kernel_spmd

F32 = mybir.dt.float32
BF16 = mybir.dt.bfloat16
AF = mybir.ActivationFunctionType
ALU = mybir.AluOpType

D = 1024
NL = 4096
NCX = 256
NA = NL + NCX
NQ = 2176
NO = 2048
QL, KVL, KR = 384, 256, 32
OFF_KV = QL
OFF_KR = OFF_KV + KVL
OFF_XB = OFF_KR + KR
LW = 1280
OFF_YB = OFF_XB + LW
OFF_G = OFF_YB + LW
IN_DIM = OFF_G + 2 * D
FFN = 2816
NFC = FFN // 128
EPS = 1e-6
SM_SCALE = 96 ** -0.5
GELU_C = 0.7978845608028654

OPTS = {}
ENGS = ['pe', 'act', 'dve', 'pool', 'sp']
BLOCK_ATTR = {'pe': 'tensor', 'act': 'scalar', 'dve': 'vector', 'pool': 'gpsimd', 'sp': 'sync'}
SAME_ENG_WINDOW = 1 << 30
NDMASEM = 8
FUSE_WAIT = True
BIN = 11


def dsize(dt):
    return mybir.dt.size(dt)


class Op:
    __slots__ = ('eng', 'idx', 'fn', 'waits', 'signal', 'dma', 'dsem', 'dcnt', 'snap', 'sigcount', 'dprev')


class Prog:
    def __init__(self, nc):
        self.nc = nc
        self.ops = {e: [] for e in ENGS}
        self.base = {}
        self.bins = {'sb': defaultdict(list), 'ps': defaultdict(list)}
        self.seen = {e: {e2: -1 for e2 in ENGS} for e in ENGS}
        self.seen_dma = {e: set() for e in ENGS}
        self.ndma = {e: 0 for e in ENGS}
        self.dram_w = {}
        self.dram_r = defaultdict(list)

    def region(self, ap):
        t = ap.tensor
        name = t.name
        if name not in self.base:
            return None
        space, base = self.base[name]
        pat = list(ap.ap)
        es = dsize(t.dtype)
        row = pat[0][0]
        off = ap.offset
        if row == 0:
            row = 1 << 40
        p0 = off // row
        f0 = off % row
        p1 = p0 + pat[0][1]
        lo = f0
        hi = f0
        for st, cnt in pat[1:]:
            ext = st * (cnt - 1)
            if ext < 0:
                lo += ext
            else:
                hi += ext
        b0, b1 = base + lo * es, base + (hi + 1) * es
        if space == 'ps':
            b0 = (b0 // 2048) * 2048
            b1 = ((b1 + 2047) // 2048) * 2048
            p0 = (p0 // 32) * 32
            p1 = ((p1 + 31) // 32) * 32
        return (space, p0, p1, b0, b1)

    def _conflicts(self, reg, is_write, out):
        space, p0, p1, b0, b1 = reg
        bins = self.bins[space]
        for bn in range(b0 >> BIN, ((b1 - 1) >> BIN) + 1):
            lst = bins.get(bn)
            if not lst:
                continue
            for rec in lst:
                if rec[1] < p1 and p0 < rec[2] and rec[3] < b1 and b0 < rec[4]:
                    if is_write or rec[0]:
                        out.add(rec[5])

    def _register(self, reg, is_write, opref):
        space, p0, p1, b0, b1 = reg
        bins = self.bins[space]
        rec = (is_write, p0, p1, b0, b1, opref)
        for bn in range(b0 >> BIN, ((b1 - 1) >> BIN) + 1):
            lst = bins[bn]
            if is_write:
                lst[:] = [r for r in lst if not (p0 <= r[1] and r[2] <= p1 and b0 <= r[3] and r[4] <= b1)]
            elif not opref[2]:
                lst[:] = [r for r in lst if not ((not r[0]) and r[1] == p0 and r[2] == p1 and r[3] == b0
                                                 and r[4] == b1 and r[5][0] == opref[0] and not r[5][2])]
            lst.append(rec)

    def emit(self, eng, fn, reads=(), writes=(), dma=False, dram_reads=(), dram_writes=()):
        deps = set()
        rr = [r for r in (self.region(a) for a in reads) if r is not None]
        ww = [r for r in (self.region(a) for a in writes) if r is not None]
        ww = ww + [r for r in rr if r[0] == 'ps']
        rr = [r for r in rr if r[0] != 'ps']
        for r in rr:
            self._conflicts(r, False, deps)
        for r in ww:
            self._conflicts(r, True, deps)
        for tag in dram_reads:
            if tag in self.dram_w:
                deps.add(self.dram_w[tag])
        for tag in dram_writes:
            if tag in self.dram_w:
                deps.add(self.dram_w[tag])
            deps.update(self.dram_r.get(tag, ()))
        op = Op()
        op.eng = eng
        op.idx = len(self.ops[eng])
        op.fn = fn
        op.signal = False
        op.dma = dma
        op.dsem = None
        op.dcnt = 0
        op.sigcount = 0
        op.dprev = None
        seen = self.seen[eng]
        sdma = self.seen_dma[eng]
        waits = []
        if dma:
            k = self.ndma[eng]
            self.ndma[eng] += 1
            op.dsem = (eng, k % NDMASEM)
            op.dcnt = 16 * (k // NDMASEM + 1)
            if k >= NDMASEM:
                prev = self._dma_ops[eng][k - NDMASEM]
                deps.add((eng, prev.idx, True))
            self._dma_ops.setdefault(eng, []).append(op)
        best = {}
        dl = []
        for (e2, i2, d2) in deps:
            if d2:
                dl.append((e2, i2, d2))
            elif i2 > best.get(e2, -1):
                best[e2] = i2
        dl.sort()
        dl += [(e2, i2, False) for e2, i2 in sorted(best.items())]
        for (e2, i2, d2) in dl:
            src = self.ops[e2][i2]
            if d2:
                if (e2, i2) in sdma:
                    continue
                sdma.add((e2, i2))
                waits.append((e2, i2))
            else:
                if e2 == eng:
                    if eng == 'pe' or op.idx - i2 > SAME_ENG_WINDOW:
                        continue
                if i2 <= seen[e2]:
                    continue
                seen[e2] = i2
                src.signal = True
                waits.append((e2, i2))
            for e3, v in src.snap.items():
                if v > seen[e3]:
                    seen[e3] = v
        op.waits = waits
        op.snap = dict(seen)
        self.ops[eng].append(op)
        ref = (eng, op.idx, dma)
        for r in rr:
            self._register(r, False, ref)
        for r in ww:
            self._register(r, True, ref)
        for tag in dram_reads:
            self.dram_r[tag].append(ref)
        for tag in dram_writes:
            self.dram_w[tag] = ref
            self.dram_r[tag] = []
        return op

    _dma_ops = None

    def finalize(self, es):
        nc = self.nc
        self.sem = {e: es.enter_context(nc.semaphore("s_" + e)) for e in ENGS}
        self.dsems = {}
        for e in ENGS:
            if self.ndma[e]:
                for j in range(min(NDMASEM, self.ndma[e])):
                    self.dsems[(e, j)] = es.enter_context(nc.semaphore("d_%s%d" % (e, j)))
        for e in ENGS:
            c = 0
            for op in self.ops[e]:
                if op.signal and not op.dma:
                    c += 1
                    op.sigcount = c
        block = es.enter_context(nc.Block())
        for e in ENGS:
            self._replay_engine(block, e)

    def _replay_engine(self, block, e):
        ops = self.ops[e]
        allops = self.ops
        sem = self.sem
        dsems = self.dsems

        def body(eng):
            for op in ops:
                wl = []
                for (e2, i2) in op.waits:
                    src = allops[e2][i2]
                    if src.dma:
                        wl.append((dsems[src.dsem], src.dcnt))
                    else:
                        wl.append((sem[e2], src.sigcount))
                fuse = wl.pop() if (wl and FUSE_WAIT and not op.dma) else None
                for (sm, v) in wl:
                    eng.wait_ge(sm, v)
                ins = op.fn(eng)
                if fuse is not None:
                    ins._wait_ge(fuse[0], fuse[1])
                if op.dma:
                    ins.then_inc(dsems[op.dsem], 16)
                elif op.signal:
                    ins.then_inc(sem[e], 1)
        getattr(block, BLOCK_ATTR[e])(body)


class Arena:
    def __init__(self, nc, prog, lo=20480, hi=229376):
        self.nc, self.prog, self.lo, self.hi, self.cur = nc, prog, lo, hi, lo
        self.n = 0
        self.peak = lo

    def alloc(self, name, shape, dt):
        per = int(np.prod(shape[1:])) * dsize(dt)
        per = (per + 63) // 64 * 64
        assert self.cur + per <= self.hi, "SBUF arena overflow at %s: need %d have %d" % (name, per, self.hi - self.cur)
        self.n += 1
        t = self.nc.alloc_sbuf_tensor_at("%s_%d_%d" % (name, self.lo, self.n), list(shape), dt, offset=self.cur)
        self.prog.base[t.name] = ('sb', self.cur)
        self.cur += per
        self.peak = max(self.peak, self.cur)
        return t

    def alloc_top(self, name, shape, dt):
        per = int(np.prod(shape[1:])) * dsize(dt)
        per = (per + 63) // 64 * 64
        assert self.hi - per >= self.cur, "SBUF arena overflow (top) at %s" % name
        self.hi -= per
        self.n += 1
        t = self.nc.alloc_sbuf_tensor_at("%s_t%d" % (name, self.n), list(shape), dt, offset=self.hi)
        self.prog.base[t.name] = ('sb', self.hi)
        return t

    def mark(self):
        return (self.cur, self.hi)

    def release(self, m):
        self.cur, self.hi = m


def blocks(lo, hi, step=512):
    out = []
    s = lo
    while s < hi:
        out.append((s, min(step, hi - s)))
        s += step
    return out


def build_program(debug=False, stop_after=None):
    nc = bass.Bass("TRN2", target_bir_lowering=False)
    P = Prog(nc)
    P._dma_ops = {}
    LO0 = 20480
    PERS = 5 * 1024
    HTO = 8 * NQ * 2
    A = Arena(nc, P, LO0, LO0 + PERS)
    AH = Arena(nc, P, LO0 + PERS, LO0 + PERS + HTO)
    Z = Arena(nc, P, LO0 + PERS + HTO, 229312)
    dbg_outs = {}

    def din(name, shape, dt=F32):
        return nc.dram_tensor(name, list(shape), dt, kind="ExternalInput").ap()

    xs = din("xs", [NL, D])
    ctxs = din("ctxs", [NCX, D])
    d_cT = din("cT", [128, 8, 2])
    d_bmodT = din("bmodT", [128, 48])
    d_bmodg = din("bmodg", [128, 2048])
    d_n1g = din("n1g", [128, 8])
    d_n2g = din("n2g", [128, 8])
    d_fing = din("fing", [128, D])
    d_bgate = din("bgate", [128, 16])
    d_qng = din("qng", [128, 3])
    d_kvng = din("kvng", [128, 2])
    d_conv5 = din("conv5", [128, 10, 5])
    d_convb = din("convb", [128, 10])
    d_lba = din("lba", [128, 2, 10])
    d_lbx = din("lbx", [128, 2, 10])
    d_lam = din("lam", [128, 2, 10])
    d_fcw = din("fcw", [128, NFC, 3])
    d_fcb = din("fcb", [128, NFC])
    d_cs = din("cs", [128, 32, 32])
    d_identf = din("identf", [128, 128])
    d_identb = din("identb", [128, 128], BF16)
    w_mod = din("w_mod", [D, 6 * D])
    w_in = din("w_in", [D, IN_DIM])
    w_uq = din("w_uq", [QL, 768])
    w_ukv = din("w_ukv", [KVL, 1024])
    w_o_attn = din("w_o_attn", [512, D])
    w_o_lru = din("w_o_lru", [LW, D])
    w_out = din("w_out", [D, D])
    w_up = din("w_up", [D, 2 * FFN])
    w_down = din("w_down", [FFN, D])
    lru_wa = din("lru_wa", [2, 10, 128, 128])
    lru_wx = din("lru_wx", [2, 10, 128, 128])
    out = nc.dram_tensor("out", [NO, D], F32, kind="ExternalOutput").ap()
    s_dram = nc.dram_tensor("s_scr", [10, 128, NQ], BF16, kind="Internal").ap()
    gbc_dram = nc.dram_tensor("gbc_scr", [128, 2, D], F32, kind="Internal").ap()
    x1_dram = nc.dram_tensor("x1_scr", [NO, D], F32, kind="Internal").ap()

    ps = nc.alloc_psum_tensor("ps", [128, 8, 512], F32)
    P.base[ps.name] = ('ps', 0)
    psb = ps[:, :, :].bitcast(BF16)
    P.base[psb.tensor.name] = ('ps', 0)

    def dma(q, out_, in_, dram_reads=(), dram_writes=()):
        return P.emit(q, lambda e: e.dma_start(out=out_, in_=in_), reads=[in_], writes=[out_], dma=True,
                      dram_reads=dram_reads, dram_writes=dram_writes)

    def mm(out_, lhsT, rhs, start=True, stop=True):
        return P.emit('pe', lambda e: e.matmul(out_, lhsT=lhsT, rhs=rhs, start=start, stop=stop),
                      reads=[lhsT, rhs], writes=[out_])

    def tr(out_, in_, ident):
        return P.emit('pe', lambda e: e.transpose(out=out_, in_=in_, identity=ident), reads=[in_, ident], writes=[out_])

    def act(out_, in_, func, bias=0.0, scale=1.0, accum_out=None, eng='act'):
        rd = [in_]
        if not isinstance(bias, float):
            rd.append(bias)
        if not isinstance(scale, float):
            rd.append(scale)
        wr = [out_]
        kw = {}
        if accum_out is not None:
            wr.append(accum_out)
            kw['accum_out'] = accum_out
        return P.emit('act', lambda e: e.activation(out=out_, in_=in_, func=func, bias=bias, scale=scale, **kw),
                      reads=rd, writes=wr)

    def tt(eng, out_, in0, in1, op):
        return P.emit(eng, lambda e: e.tensor_tensor(out=out_, in0=in0, in1=in1, op=op), reads=[in0, in1], writes=[out_])

    def ts(eng, out_, in0, s1, s2, op0, op1=None):
        rd = [in0]
        if not isinstance(s1, float):
            rd.append(s1)
        if s2 is not None and not isinstance(s2, float):
            rd.append(s2)
        if op1 is None:
            return P.emit(eng, lambda e: e.tensor_scalar(out=out_, in0=in0, scalar1=s1, scalar2=None, op0=op0),
                          reads=rd, writes=[out_])
        return P.emit(eng, lambda e: e.tensor_scalar(out=out_, in0=in0, scalar1=s1, scalar2=s2, op0=op0, op1=op1),
                      reads=rd, writes=[out_])

    def stt(eng, out_, in0, scalar, in1, op0, op1):
        rd = [in0, in1]
        if not isinstance(scalar, float):
            rd.append(scalar)
        return P.emit(eng, lambda e: e.scalar_tensor_tensor(out=out_, in0=in0, scalar=scalar, in1=in1, op0=op0, op1=op1),
                      reads=rd, writes=[out_])

    def cp(eng, out_, in_):
        if eng == 'act':
            return act(out_, in_, AF.Identity)
        return P.emit(eng, lambda e: e.tensor_copy(out=out_, in_=in_), reads=[in_], writes=[out_])

    def memset(eng, ap, val):
        return P.emit(eng, lambda e: e.memset(ap, val), writes=[ap])

    def recip(out_, in_):
        return P.emit('dve', lambda e: e.reciprocal(out=out_, in_=in_), reads=[in_], writes=[out_])

    def scan(out_, d0, d1, initial):
        rd = [d0, d1]
        if not isinstance(initial, float):
            rd.append(initial)
        return P.emit('dve', lambda e: e.tensor_tensor_scan(out=out_, data0=d0, data1=d1, initial=initial,
                                                            op0=ALU.mult, op1=ALU.add), reads=rd, writes=[out_])

    def dump(name, ap, dt=F32):
        if not debug:
            return
        shape = list(ap.shape)
        t = nc.dram_tensor("dbg_" + name, shape, dt, kind="ExternalOutput").ap()
        dbg_outs[name] = t
        dma('sp', t, ap)

    def wload(dst, src):
        return dma('pool', dst, src)

    def _finish():
        P.emit('sp', lambda e: e.nop(), reads=[], writes=[])
        last = P.ops['sp'][-1]
        for o in P._dma_ops.get('sp', [])[-NDMASEM:]:
            if (o.eng, o.idx) not in last.waits:
                last.waits.append((o.eng, o.idx))
        return nc, P, A, dbg_outs

    cT = A.alloc("cT", [128, 8, 2], F32)
    bmodT = A.alloc("bmodT", [128, 48], F32)
    n1g = A.alloc("n1g", [128, 8], F32)
    n2g = A.alloc("n2g", [128, 8], F32)
    bgate = A.alloc("bgate", [128, 16], F32)
    qng = A.alloc("qng", [128, 3], F32)
    kvng = A.alloc("kvng", [128, 2], F32)
    conv5 = A.alloc("conv5", [128, 10, 5], F32)
    convb = A.alloc("convb", [128, 10], F32)
    lba = A.alloc("lba", [128, 2, 10], F32)
    lbx = A.alloc("lbx", [128, 2, 10], F32)
    lam = A.alloc("lam", [128, 2, 10], F32)
    fcw = A.alloc("fcw", [128, NFC, 3], F32)
    fcb = A.alloc("fcb", [128, NFC], F32)
    identf = A.alloc("identf", [128, 128], F32)
    identb = A.alloc("identb", [128, 128], BF16)
    for dst, src in ((cT, d_cT), (bmodT, d_bmodT), (n1g, d_n1g), (n2g, d_n2g), (bgate, d_bgate),
                     (qng, d_qng), (kvng, d_kvng), (conv5, d_conv5), (convb, d_convb), (lba, d_lba), (lbx, d_lbx),
                     (lam, d_lam), (fcw, d_fcw), (fcb, d_fcb), (identf, d_identf), (identb, d_identb)):
        dma('sp', dst[:], src)

    epsv = A.alloc("epsv", [128, 1], F32)
    memset('dve', epsv[:], EPS)
    qtr = A.alloc("qtr", [128, 1], F32)
    memset('dve', qtr[:], 0.25)
    modfm = A.alloc("modfm", [128, 4, 8, 2], F32)
    A1 = A.alloc("A1", [128, 8, 2], F32)
    A2 = A.alloc("A2", [128, 8], F32)
    hbg = A.alloc("hbg", [128, 16], F32)
    hba = A.alloc("hba", [128, 2, 10], F32)
    hbx = A.alloc("hbx", [128, 2, 10], F32)
    c1 = A.alloc("c1", [128, 2, 10], F32)
    hc = A.alloc("hc", [128, 2, 10], F32)
    hTo = AH.alloc("hTo", [128, 8, NQ], BF16)
    hTx = Z.alloc("hTx", [128, 8, NA - NQ], BF16)

    def hTc(k, st, sz):
        if st + sz <= NQ:
            return hTo[:, k, st:st + sz]
        assert st >= NQ
        return hTx[:, k, st - NQ:st - NQ + sz]

    ALLBLK = blocks(0, NQ) + blocks(NQ, NL) + blocks(NL, NA)

    sc = Z.alloc("sc", [128, 8, 2], F32)
    sc_rep = Z.alloc("sc_rep", [128, 8, 128], F32)
    m0 = Z.mark()
    th0 = Z.alloc("th0", [128, 8, 2], F32)
    wm = [Z.alloc("wm%d" % i, [128, 8, 1024], F32) for i in range(2)]
    act(th0[:], cT[:], AF.Tanh, scale=0.5)
    ts('dve', th0[:], th0[:], 0.5, 0.5, ALU.mult, ALU.add)
    tt('dve', sc[:], th0[:], cT[:], ALU.mult)
    for k in range(8):
        cp('dve', sc_rep[:, k, :], sc[:, k, 0:1].to_broadcast([128, 128]))
    act(c1[:], lam[:], AF.Exp, scale=-1.0)
    act(c1[:], c1[:], AF.Ln, bias=1.0)
    ts('dve', hc[:], c1[:], -4.0, None, ALU.mult)
    ts('dve', c1[:], c1[:], -8.0, None, ALU.mult)
    ts('dve', hba[:], lba[:], 0.5, None, ALU.mult)
    ts('dve', hbx[:], lbx[:], 0.5, None, ALU.mult)
    ts('dve', hbg[:], bgate[:], 0.5, None, ALU.mult)
    w_mod_v = w_mod.rearrange("(k p) n -> p k n", p=128)
    fm_idx = {0: 0, 1: 1, 3: 2, 4: 3}

    def mod_slab(s):
        wb = wm[s % 2]
        dma('pool', wb[:], w_mod_v[:, :, s * 1024:(s + 1) * 1024])
        if s in fm_idx:
            bank = 4 + s % 2
            for j in range(8):
                for k in range(8):
                    mm(ps[:, bank, 2 * j:2 * j + 2], wb[:, k, j * 128:(j + 1) * 128], sc[:, k, :], start=(k == 0), stop=(k == 7))
            for col in range(2):
                tt('dve', modfm[:, fm_idx[s], :, col], ps[:, bank, col:16:2], bmodT[:, s * 8:(s + 1) * 8], ALU.add)
        else:
            gi = 0 if s == 2 else 1
            for half in range(2):
                bank = 6 + half
                for k in range(8):
                    mm(ps[:, bank, :], sc_rep[:, k, :], wb[:, k, half * 512:(half + 1) * 512], start=(k == 0), stop=(k == 7))
                tt('dve', gbc[:, gi, half * 512:(half + 1) * 512], ps[:, bank, :],
                   bmodg[:, gi * 1024 + half * 512: gi * 1024 + (half + 1) * 512], ALU.add)

    mod_slab(0)
    mod_slab(1)
    for col in range(2):
        stt('dve', A1[:, :, col], modfm[:, 1, :, col], 1.0, n1g[:], ALU.add, ALU.mult)

    xt = [Z.alloc("xt%d" % i, [128, D], F32) for i in range(3)]
    junk = Z.alloc("junk", [128, D], BF16)
    ssq = Z.alloc("ssq", [128, 34], F32)
    sqv = Z.alloc("sqv", [128, 34], F32)
    rsv = Z.alloc("rsv", [128, 34], F32)
    memset('dve', ssq[:], 0.0)

    def norm_p1(buf, bufo, statc):
        act(junk[:], buf[:], AF.Square, accum_out=ssq[:, statc:statc + 1])
        act(sqv[:, statc:statc + 1], ssq[:, statc:statc + 1], AF.Sqrt, bias=epsv[:], scale=1.0 / D)
        recip(rsv[:, statc:statc + 1], sqv[:, statc:statc + 1])
        ts('dve', bufo[:], buf[:], rsv[:, statc:statc + 1], None, ALU.mult)

    def norm_p2(bufo, dstf, Ascale, Abias, bank0):
        for k in range(8):
            tr(ps[:, bank0 + k // 4, (k % 4) * 128:(k % 4 + 1) * 128], bufo[:, k * 128:(k + 1) * 128], identf[:])
        for k in range(8):
            src = ps[:, bank0 + k // 4, (k % 4) * 128:(k % 4 + 1) * 128]
            if k < 4:
                act(dstf(k), src, AF.Identity, bias=Abias(k), scale=Ascale(k))
            else:
                ts('dve', dstf(k), src, Ascale(k), Abias(k), ALU.mult, ALU.add)

    def norm_to_T(buf, dstf, statc, Ascale, Abias, bank0):
        norm_p1(buf, buf, statc)
        norm_p2(buf, dstf, Ascale, Abias, bank0)

    for it in range(35):
        if it < 34:
            i = it
            buf = xt[i % 3]
            src = xs[i * 128:(i + 1) * 128, :] if i < 32 else ctxs[(i - 32) * 128:(i - 31) * 128, :]
            dma('sp', buf[:], src)
            norm_p1(buf, buf, i)
        if it >= 1:
            i = it - 1
            col = 0 if i < 32 else 1
            norm_p2(xt[i % 3], lambda k, i=i: hTc(k, i * 128, 128),
                    lambda k, col=col: A1[:, k, col:col + 1], lambda k, col=col: modfm[:, 0, k, col:col + 1],
                    2 * (i % 2))
    dump("hT", hTo[:, :, 0:256], BF16)
    dump("hTc", hTx[:, :, NL - NQ:NA - NQ], BF16)
    Z.release(m0)

    if stop_after in ('0', 'A'):
        return _finish()

    mL = Z.mark()
    SB = 1216
    wxb = [Z.alloc("wxb%d" % i, [128, 8, 128], BF16) for i in range(2)]
    wyb = [Z.alloc("wyb%d" % i, [128, 8, 128], BF16) for i in range(2)]
    wg = [Z.alloc("wg%d" % i, [128, 4, 128], BF16) for i in range(2)]
    xbL = Z.alloc("xbL", [128, NL + 4], F32)
    xbC = Z.alloc("xbC", [128, NCX + 4], F32)
    xc = Z.alloc("xc", [128, NA], F32)
    xcb = Z.alloc("xcb", [128, NA], BF16)
    TS = [[Z.alloc("T%d%s" % (j, "ab"[i]), [128, SB], F32) for j in range(3)] for i in range(2)]
    trb = [Z.alloc("trb%d" % i, [128, 512], F32) for i in range(2)]
    hB = Z.alloc("hB", [128, NQ], F32)
    ybs = Z.alloc("ybs", [128, NQ], F32)
    gtmp = Z.alloc("gtmp", [128, NQ], F32)
    sblk = Z.alloc("sblk", [128, NQ], BF16)
    wmp = [Z.alloc("wmp%d" % i, [128, 8, 128], F32) for i in range(2)]
    bmp = [Z.alloc("bmp%d" % i, [128, 128], F32) for i in range(2)]
    gpc = [Z.alloc("gpc%d" % i, [128, 128], F32) for i in range(2)]
    mod_pieces = []
    GB_TAGS = []
    for s_ in (2, 3, 4, 5):
        for j in range(8):
            def piece(s_=s_, j=j):
                i_ = len(piece_cnt)
                piece_cnt.append(1)
                w_ = wmp[i_ % 2]
                dma('pool', w_[:], w_mod_v[:, :, s_ * 1024 + j * 128: s_ * 1024 + (j + 1) * 128])
                b = nbank()
                if s_ in fm_idx:
                    for k in range(8):
                        mm(ps[:, b, 0:2], w_[:, k, :], sc[:, k, :], start=(k == 0), stop=(k == 7))
                    tt('dve', modfm[:, fm_idx[s_], j, :], ps[:, b, 0:2],
                       bmodT[:, s_ * 8 + j:s_ * 8 + j + 1].to_broadcast([128, 2]), ALU.add)
                else:
                    gi = 0 if s_ == 2 else 1
                    bm_, g_ = bmp[i_ % 2], gpc[i_ % 2]
                    dma('sp', bm_[:], d_bmodg[:, gi * 1024 + j * 128: gi * 1024 + (j + 1) * 128])
                    for k in range(8):
                        mm(ps[:, b, 0:128], sc_rep[:, k, :], w_[:, k, :], start=(k == 0), stop=(k == 7))
                    tt('dve', g_[:], ps[:, b, 0:128], bm_[:], ALU.add)
                    if gi == 0:
                        ts('dve', g_[:], g_[:], 0.5, None, ALU.mult)
                    tag = ("gbc", gi, j)
                    GB_TAGS.append(tag)
                    dma('sp', gbc_dram[:, gi, j * 128:(j + 1) * 128], g_[:], dram_writes=[tag])
            mod_pieces.append(piece)
    piece_cnt = []
    memset('pool', xbL[:, 0:2], 0.0)
    memset('pool', xbL[:, NL + 2:NL + 4], 0.0)
    memset('pool', xbC[:, 0:2], 0.0)
    memset('pool', xbC[:, NCX + 2:NCX + 4], 0.0)
    w_in_v = w_in.rearrange("(k p) n -> p k n", p=128)
    psrot = [0]

    def nbank():
        b = psrot[0]
        psrot[0] = (b + 1) % 8
        return b

    def lru_loads(n):
        wload(wxb[n % 2][:], w_in_v[:, :, OFF_XB + n * 128: OFF_XB + (n + 1) * 128])
        wload(wyb[n % 2][:], w_in_v[:, :, OFF_YB + n * 128: OFF_YB + (n + 1) * 128])
        for d in range(2):
            wload(wg[n % 2][:, 2 * d, :], lru_wa[d, n])
            wload(wg[n % 2][:, 2 * d + 1, :], lru_wx[d, n])

    def xb_tasks(n):
        wx_ = wxb[n % 2]
        out = []
        for (st, sz) in blocks(NQ, NL) + blocks(NL, NA) + blocks(0, NQ):
            def task(st=st, sz=sz):
                b = nbank()
                for k in range(8):
                    mm(ps[:, b, 0:sz], wx_[:, k, :], hTc(k, st, sz), start=(k == 0), stop=(k == 7))
                ev = 'dve' if (OPTS.get('DVEEV', 1) and (st // 512) % 2 == 0) else 'act'
                if st < NL:
                    cp(ev, xbL[:, 2 + st:2 + st + sz], ps[:, b, 0:sz])
                else:
                    cp(ev, xbC[:, 2 + st - NL:2 + st - NL + sz], ps[:, b, 0:sz])
            out.append(task)
        return out

    def yb_tasks(n):
        wy_ = wyb[n % 2]
        out = []
        for (st, sz) in blocks(0, NQ):
            def task(st=st, sz=sz):
                b = nbank()
                for k in range(8):
                    mm(ps[:, b, 0:sz], wy_[:, k, :], hTc(k, st, sz), start=(k == 0), stop=(k == 7))
                cp('dve' if OPTS.get('DVEEV', 1) else 'act', ybs[:, st:st + sz], ps[:, b, 0:sz])
            out.append(task)
        return out

    SUBS = [
        (1, [(NL, NA, True, None), (3136, NL, True, None)]),
        (1, [(NQ, 3136, True, None), (NO, NQ, True, ('hB', NO))]),
        (1, [(1024, NO, True, ('hB', 1024))]),
        (1, [(0, 1024, True, ('hB', 0))]),
        (0, [(NL, NA, False, None), (0, 960, False, ('add', 0))]),
        (0, [(960, NQ, False, ('add', 960))]),
    ]

    lru_loads(0)
    lru_loads(1)
    for t in xb_tasks(0):
        t()
    for n in range(10):
        wg_ = wg[n % 2]
        def conv(nn, parts, cast=True):
            for (src, o0, st, sz) in parts:
                dst = xc[:, o0 + st:o0 + st + sz]
                ts('dve', dst, src[:, st:st + sz], conv5[:, nn, 0:1], convb[:, nn:nn + 1], ALU.mult, ALU.add)
                for j in range(1, 5):
                    stt('dve', dst, src[:, st + j:st + j + sz], conv5[:, nn, j:j + 1], dst, ALU.mult, ALU.add)
                if cast:
                    cp('act', xcb[:, o0 + st:o0 + st + sz], dst)

        CE = 2304 if OPTS.get('CSPLIT', 1) else NL
        if n == 0 and CE < NL:
            conv(0, [(xbL, 0, CE, NL - CE)])
        if CE < NL:
            conv(n, [(xbC, NL, 0, NCX), (xbL, 0, NO, CE - NO)])
            own_parts = [(xbL, 0, 0, 1024), (xbL, 0, 1024, 1024)]
        else:
            conv(n, [(xbL, 0, 0, 2048), (xbL, 0, 2048, 2048), (xbC, NL, 0, NCX)])
            own_parts = []
        pend = yb_tasks(n) + (xb_tasks(n + 1) if n + 1 < 10 else [])
        n_after_yb = len(pend) - len(blocks(0, NQ))
        for _ in range(4):
            if mod_pieces:
                pend.append(mod_pieces.pop(0))
        per = (len(pend) + 5) // 6

        def gates(d, segs, tset):
            T1, T2, T3 = tset
            pos = 0
            seginfo = []
            for (lo, hi, rev, outspec) in segs:
                for (st, sz) in blocks(lo, hi):
                    b1_, b2_ = nbank(), nbank()
                    mm(ps[:, b1_, 0:sz], wg_[:, 2 * d, :], xcb[:, st:st + sz])
                    mm(ps[:, b2_, 0:sz], wg_[:, 2 * d + 1, :], xcb[:, st:st + sz])
                    tb = trb[(st // 512) % 2]
                    q0 = pos + (st - lo)
                    act(tb[:, 0:sz], ps[:, b1_, 0:sz], AF.Tanh, bias=hba[:, d, n:n + 1], scale=0.5)
                    act(T1[:, q0:q0 + sz], ps[:, b2_, 0:sz], AF.Tanh, bias=hbx[:, d, n:n + 1], scale=0.5)
                    act(T2[:, q0:q0 + sz], tb[:, 0:sz], AF.Exp, bias=hc[:, d, n:n + 1], scale=hc[:, d, n:n + 1])
                    act(T3[:, q0:q0 + sz], tb[:, 0:sz], AF.Exp, bias=c1[:, d, n:n + 1], scale=c1[:, d, n:n + 1])
                    stt('dve', T1[:, q0:q0 + sz], T1[:, q0:q0 + sz], 1.0, xc[:, st:st + sz], ALU.add, ALU.mult)
                seginfo.append((pos, hi - lo, rev, outspec))
                pos += hi - lo
            return seginfo, pos

        def sqrt_u(tset, L):
            T1, T2, T3 = tset
            act(T3[:, 0:L], T3[:, 0:L], AF.Sqrt, bias=qtr[:], scale=-0.25)
            tt('dve', T1[:, 0:L], T1[:, 0:L], T3[:, 0:L], ALU.mult)

        def scans(tset, seginfo, cur):
            T1, T2, T3 = tset
            for (p0, ln, rev, outspec) in seginfo:
                if outspec is not None and outspec[0] == 'hB':
                    o = hB[:, outspec[1]:outspec[1] + ln]
                else:
                    o = T3[:, p0:p0 + ln]
                a_, u_ = T2[:, p0:p0 + ln], T1[:, p0:p0 + ln]
                if rev:
                    scan(o[:, ::-1], a_[:, ::-1], u_[:, ::-1], cur)
                    cur = o[:, 0:1]
                else:
                    scan(o, a_, u_, cur)
                    cur = o[:, ln - 1:ln]
                if outspec is not None and outspec[0] == 'add':
                    c0 = outspec[1]
                    tt('dve', hB[:, c0:c0 + ln], hB[:, c0:c0 + ln], T3[:, p0:p0 + ln], ALU.add)
            return cur

        cur = 0.0
        for pr in range(3):
            infos = []
            for j in range(2):
                d, segs = SUBS[2 * pr + j]
                infos.append(gates(d, segs, TS[j]))
                if own_parts:
                    pA, pB = own_parts
                    if pr == 0 and j == 0:
                        conv(n, [pB], cast=False)
                    if pr == 0 and j == 1:
                        cp('act', xcb[:, pB[2]:pB[2] + pB[3]], xc[:, pB[2]:pB[2] + pB[3]])
                    if pr == 1 and j == 0:
                        cp('act', xcb[:, pA[2]:pA[2] + pA[3]], xc[:, pA[2]:pA[2] + pA[3]])
                    if pr == 2 and j == 1 and n + 1 < 10:
                        cp('act', xcb[:, CE:NL], xc[:, CE:NL])
                for _ in range(per):
                    if pend:
                        pend.pop(0)()
            if pr == 2:
                cur = 0.0
            for j in range(2):
                sqrt_u(TS[j], infos[j][1])
            for j in range(2):
                cur = scans(TS[j], infos[j][0], cur)
            if pr == 0 and own_parts:
                conv(n, [own_parts[0]], cast=False)
            if pr == 1 and n + 1 < 10 and CE < NL:
                conv(n + 1, [(xbL, 0, CE, NL - CE)], cast=False)
            if pr == 0:
                while len(pend) > n_after_yb:
                    pend.pop(0)()
                if not OPTS.get('ACTGELU', 1):
                    tt('dve', gtmp[:], ybs[:], ybs[:], ALU.mult)
                    ts('dve', gtmp[:], gtmp[:], 0.044715, 1.0, ALU.mult, ALU.add)
                    tt('dve', gtmp[:], gtmp[:], ybs[:], ALU.mult)
            if pr == 1:
                if OPTS.get('ACTGELU', 1):
                    act(gtmp[:], ybs[:], AF.Gelu_apprx_tanh)
                else:
                    act(gtmp[:], gtmp[:], AF.Tanh, scale=GELU_C)
                    stt('dve', gtmp[:], gtmp[:], 1.0, ybs[:], ALU.add, ALU.mult)
        while pend:
            pend.pop(0)()
        if n == 0:
            dump("xc0", xc[:, 0:512])
            dump("xc0c", xc[:, NL:NA])
            dump("hsum0", hB[:])
        if n + 2 < 10:
            lru_loads(n + 2)
        if OPTS.get('ACTGELU', 1):
            tt('dve', sblk[:], gtmp[:], hB[:], ALU.mult)
        else:
            stt('dve', sblk[:], gtmp[:], 0.5, hB[:], ALU.mult, ALU.mult)
        dma('sp', s_dram[n], sblk[:], dram_writes=[("s", n)])
        if n == 0:
            dump("s0", sblk[:], BF16)
    assert not mod_pieces
    stt('dve', A2[:], modfm[:, 3, :, 0], 1.0, n2g[:], ALU.add, ALU.mult)
    dump("modfm", modfm[:])
    Z.release(mL)

    if stop_after == 'L':
        return _finish()

    qT = Z.alloc_top("qT", [96, 8, NQ], BF16)
    krT = Z.alloc_top("krT", [96, NA], BF16)
    kvT = Z.alloc_top("kvT", [128, 2, NA], BF16)
    cs = Z.alloc_top("cs", [128, 32, 32], F32)
    wuq = Z.alloc_top("wuq", [128, 3, 768], BF16)
    wK = Z.alloc_top("wK", [128, 2, 8, 64], BF16)
    wV = Z.alloc_top("wV", [128, 2, 8, 64], BF16)
    dma('sp', cs[:], d_cs)
    mB = Z.mark()
    stg_q = Z.alloc("stg_q", [128, 3, 768], F32)
    stg_kv = Z.alloc("stg_kv", [128, 2, 8, 128], F32)
    dma('sp', stg_q[:], w_uq.rearrange("(k p) n -> p k n", p=128))
    dma('sp', stg_kv[:], w_ukv.rearrange("(k p) (h j) -> p k h j", p=128, j=128))
    for k in range(3):
        ts('dve', wuq[:, k, :], stg_q[:, k, :], qng[:, k:k + 1], None, ALU.mult)
    for k in range(2):
        ts('dve', wK[:, k, :, :], stg_kv[:, k, :, 0:64], kvng[:, k:k + 1], None, ALU.mult)
        ts('dve', wV[:, k, :, :], stg_kv[:, k, :, 64:128], kvng[:, k:k + 1], None, ALU.mult)
    Z.release(mB)
    qlT = Z.alloc("qlT", [128, 3, NQ], BF16)
    wA = Z.alloc("wA", [128, 8, 672], BF16)
    wload(wA[:], w_in_v[:, :, 0:672])
    latn = [Z.alloc("latn%d" % i, [128, 672], BF16) for i in range(2)]
    ssq2 = Z.alloc("ssq2", [128, 34, 2], F32)
    sq2 = Z.alloc("sq2", [128, 34, 2], F32)
    rs2 = Z.alloc("rs2", [128, 34, 2], F32)
    rtmp = Z.alloc("rtmp", [128, 4, 16], F32)
    junk2 = Z.alloc("junk2", [128, 384], BF16)
    qtm = [Z.alloc("qtm%d" % i, [128, 8, 96], BF16) for i in range(2)]
    rq = Z.alloc("rq", [128, 4, 4, 16], F32)
    memset('dve', ssq2[:], 0.0)
    b1state = {}

    def b1_s12(i):
        own = i < 17
        col0 = i * 128 if i < 32 else NL + (i - 32) * 128
        ln_ = latn[i % 2]
        bq, bk = nbank(), nbank()
        if own:
            for k in range(8):
                mm(ps[:, bq, 0:QL], hTc(k, col0, 128), wA[:, k, 0:QL], start=(k == 0), stop=(k == 7))
        for k in range(8):
            mm(ps[:, bk, 0:288], hTc(k, col0, 128), wA[:, k, QL:672], start=(k == 0), stop=(k == 7))
        if own:
            act(junk2[:, 0:QL], ps[:, bq, 0:QL], AF.Square, accum_out=ssq2[:, i, 0:1])
            act(sq2[:, i, 0:1], ssq2[:, i, 0:1], AF.Sqrt, bias=epsv[:], scale=1.0 / QL)
            recip(rs2[:, i, 0:1], sq2[:, i, 0:1])
            act(ln_[:, 0:QL], ps[:, bq, 0:QL], AF.Identity, scale=rs2[:, i, 0:1])
        act(junk2[:, 0:KVL], ps[:, bk, 0:KVL], AF.Square, accum_out=ssq2[:, i, 1:2])
        act(sq2[:, i, 1:2], ssq2[:, i, 1:2], AF.Sqrt, bias=epsv[:], scale=1.0 / KVL)
        recip(rs2[:, i, 1:2], sq2[:, i, 1:2])
        ts('dve', ln_[:, QL:QL + KVL], ps[:, bk, 0:KVL], rs2[:, i, 1:2], None, ALU.mult)
        if i < 32:
            x1_, x2_ = ps[:, bk, 256:272], ps[:, bk, 272:288]
            cos_, sin_ = cs[:, i, 0:16], cs[:, i, 16:32]
            tt('dve', rtmp[:, 0, :], x1_, cos_, ALU.mult)
            tt('dve', rtmp[:, 1, :], x2_, sin_, ALU.mult)
            tt('dve', rtmp[:, 2, :], x2_, cos_, ALU.mult)
            tt('dve', rtmp[:, 3, :], x1_, sin_, ALU.mult)
            tt('dve', ln_[:, 640:656], rtmp[:, 0, :], rtmp[:, 1, :], ALU.subtract)
            tt('dve', ln_[:, 656:672], rtmp[:, 2, :], rtmp[:, 3, :], ALU.add)
        else:
            cp('dve', ln_[:, 640:672], ps[:, bk, 256:288])

    def b1_s34(i):
        own = i < 17
        col0 = i * 128 if i < 32 else NL + (i - 32) * 128
        ln_ = latn[i % 2]
        bt = nbank()
        if own:
            for k in range(3):
                tr(psb[:, bt, k * 128:(k + 1) * 128], ln_[:, k * 128:(k + 1) * 128], identb[:])
        for k in range(2):
            tr(psb[:, bt, (3 + k) * 128:(4 + k) * 128], ln_[:, QL + k * 128:QL + (k + 1) * 128], identb[:])
        tr(psb[0:96, bt, 640:768], ln_[:, 576:672], identb[:])
        if own:
            cp('act', qlT[:, :, col0:col0 + 128], psb[:, bt, 0:384].rearrange("p (k c) -> p k c", c=128))
        cp('act', kvT[:, :, col0:col0 + 128], psb[:, bt, 384:640].rearrange("p (k c) -> p k c", c=128))
        cp('act', krT[64:96, col0:col0 + 128], psb[64:96, bt, 640:768])

    for it in range(35):
        if it < 34:
            b1_s12(it)
        if it >= 1:
            b1_s34(it - 1)
    dump("qlT", qlT[:, :, 0:256], BF16)
    dump("kvT", kvT[:, :, 0:256], BF16)
    dump("krT", krT[64:96, 0:256], BF16)
    def q_s12(i):
        col0 = i * 128
        qt_ = qtm[i % 2]
        bb = [nbank(), nbank()]
        for hh in range(2):
            for k in range(3):
                mm(ps[:, bb[hh], 0:384], qlT[:, k, col0:col0 + 128], wuq[:, k, hh * 384:(hh + 1) * 384],
                   start=(k == 0), stop=(k == 2))
        for hh in range(2):
            src = ps[:, bb[hh], 0:384].rearrange("p (h d) -> p h d", d=96)
            x1_, x2_ = src[:, :, 64:80], src[:, :, 80:96]
            cos_ = cs[:, i:i + 1, 0:16].to_broadcast([128, 4, 16])
            sin_ = cs[:, i:i + 1, 16:32].to_broadcast([128, 4, 16])
            cp('dve', qt_[:, hh * 4:(hh + 1) * 4, 0:64], src[:, :, 0:64])
            tt('dve', rq[:, 0], x1_, cos_, ALU.mult)
            tt('dve', rq[:, 1], x2_, sin_, ALU.mult)
            tt('dve', rq[:, 2], x2_, cos_, ALU.mult)
            tt('dve', rq[:, 3], x1_, sin_, ALU.mult)
            tt('dve', qt_[:, hh * 4:(hh + 1) * 4, 64:80], rq[:, 0], rq[:, 1], ALU.subtract)
            tt('dve', qt_[:, hh * 4:(hh + 1) * 4, 80:96], rq[:, 2], rq[:, 3], ALU.add)

    def q_s34(i):
        col0 = i * 128
        qt_ = qtm[i % 2]
        bt = nbank()
        for h in range(8):
            tr(psb[0:96, bt, h * 128:(h + 1) * 128], qt_[:, h, :], identb[:])
        cp('act', qT[:, :, col0:col0 + 128], psb[0:96, bt, :].rearrange("p (h c) -> p h c", c=128))

    for it in range(18):
        if it < 17:
            q_s12(it)
        if it >= 1:
            q_s34(it - 1)
    dump("qT", qT[:, :, 0:256], BF16)
    Z.release(mB)

    if stop_after == 'B':
        return _finish()

    Z.cur = Z.lo
    OT = Z.alloc("OT", [128, 4, NQ], BF16)
    mT = Z.mark()
    Kt = Z.alloc("Kt", [96, 2, NA], BF16)
    Vp = Z.alloc("Vp", [128, 34, 3, 64], BF16)
    pT = [Z.alloc("pT%d" % i, [128, 2, 512], BF16) for i in range(3)]
    rd = Z.alloc("rd", [128, 512], F32)
    memset('pool', Vp[:, :, 1, :], 1.0)
    unit = 0
    for p in range(4):
        for hh in range(2):
            h = 2 * p + hh
            for (st, sz) in blocks(0, NA):
                b = 6 + (st // 512) % 2
                for k in range(2):
                    mm(ps[0:64, b, 0:sz], wK[:, k, h, :], kvT[:, k, st:st + sz], start=(k == 0), stop=(k == 1))
                cp('dve' if (st // 512) % 2 else 'act', Kt[0:64, hh, st:st + sz], ps[0:64, b, 0:sz])
            cp('dve', Kt[64:96, hh, :], krT[64:96, :])
        for g in range(9):
            t0 = g * 4
            nt = min(4, 34 - t0)
            b = 6 + g % 2
            for t in range(nt):
                c0 = (t0 + t) * 128
                for k in range(2):
                    mm(ps[:, b, t * 128:(t + 1) * 128], kvT[:, k, c0:c0 + 128],
                       wV[:, k, 2 * p:2 * p + 2, :].rearrange("p h d -> p (h d)"), start=(k == 0), stop=(k == 1))
            cp('dve', Vp[:, t0:t0 + nt, 0:3:2, :],
               ps[:, b, 0:nt * 128].rearrange("p (t h d) -> p t h d", h=2, d=64))
        if p == 0:
            dump("Kt0", Kt[:, :, 0:256], BF16)
            dump("Vp0", Vp[:, 0:2, :, :], BF16)
        for hh in range(2):
            h = 2 * p + hh
            vsel = slice(0, 2) if hh == 0 else slice(1, 3)
            for (qs, qz) in blocks(0, NQ):
                ob = 4 + unit % 2
                unit += 1
                pend = None
                for ktp in range(17):
                    sb0 = 2 * (ktp % 2)
                    for j in range(2):
                        kt = 2 * ktp + j
                        mm(ps[:, sb0 + j, 0:qz], Kt[0:96, hh, kt * 128:(kt + 1) * 128], qT[0:96, h, qs:qs + qz])
                    pt_ = pT[ktp % 3]
                    act(pt_[:, :, 0:qz], ps[:, sb0:sb0 + 2, 0:qz], AF.Exp, scale=SM_SCALE)
                    if pend is not None:
                        pend()

                    def pv(ktp=ktp, pt_=pt_):
                        for j in range(2):
                            kt = 2 * ktp + j
                            mm(ps[:, ob, 0:qz], Vp[:, kt, vsel, :].rearrange("p a d -> p (a d)"), pt_[:, j, 0:qz],
                               start=(kt == 0), stop=(kt == 33))
                    pend = pv
                pend()
                if hh == 0:
                    recip(rd[0:64, 0:qz], ps[64:128, ob, 0:qz])
                    tt('dve', OT[0:64, p, qs:qs + qz], ps[0:64, ob, 0:qz], rd[0:64, 0:qz], ALU.mult)
                else:
                    recip(rd[64:128, 0:qz], ps[0:64, ob, 0:qz])
                    tt('dve', OT[64:128, p, qs:qs + qz], ps[64:128, ob, 0:qz], rd[64:128, 0:qz], ALU.mult)
    dump("OT", OT[:, :, 0:256], BF16)
    Z.release(mT)
    Z.hi = 229312

    if stop_after == 'T':
        return _finish()

    mix = Z.alloc("mix", [128, 8, NQ], BF16)
    wout = Z.alloc("wout", [128, 8, D], BF16)
    mM = Z.mark()
    s_sb = Z.alloc("s_sb", [128, 10, NQ], BF16)
    wsl = [Z.alloc("wsl%d" % i, [128, 30, 128], BF16) for i in range(2)]
    tg = [Z.alloc("tg%d" % i, [128, 2, 512], F32) for i in range(2)]
    for n in range(10):
        dma('sp', s_sb[:, n, :], s_dram[n], dram_reads=[("s", n)])
    wload(wout[:], w_out.rearrange("(k p) n -> p k n", p=128))
    woa_v = w_o_attn.rearrange("(k p) n -> p k n", p=128)
    wol_v = w_o_lru.rearrange("(k p) n -> p k n", p=128)

    def merge_loads(m):
        w = wsl[m % 2]
        wload(w[:, 0:4, :], woa_v[:, :, m * 128:(m + 1) * 128])
        wload(w[:, 4:14, :], wol_v[:, :, m * 128:(m + 1) * 128])
        wload(w[:, 14:22, :], w_in_v[:, :, OFF_G + m * 128:OFF_G + (m + 1) * 128])
        wload(w[:, 22:30, :], w_in_v[:, :, OFF_G + D + m * 128:OFF_G + D + (m + 1) * 128])

    merge_loads(0)
    it = 0
    for m in range(8):
        if m + 1 < 8:
            merge_loads(m + 1)
        w = wsl[m % 2]
        for (st, sz) in blocks(0, NQ):
            b0 = 4 * (it % 2)
            tg_ = tg[it % 2]
            mt_ = tg_
            it += 1
            for k in range(4):
                mm(ps[:, b0, 0:sz], w[:, k, :], OT[:, k, st:st + sz], start=(k == 0), stop=(k == 3))
            for k in range(10):
                mm(ps[:, b0 + 1, 0:sz], w[:, 4 + k, :], s_sb[:, k, st:st + sz], start=(k == 0), stop=(k == 9))
            for k in range(8):
                mm(ps[:, b0 + 2, 0:sz], w[:, 14 + k, :], hTo[:, k, st:st + sz], start=(k == 0), stop=(k == 7))
            for k in range(8):
                mm(ps[:, b0 + 3, 0:sz], w[:, 22 + k, :], hTo[:, k, st:st + sz], start=(k == 0), stop=(k == 7))
            act(tg_[:, 0, 0:sz], ps[:, b0 + 2, 0:sz], AF.Tanh, bias=hbg[:, m:m + 1], scale=0.5)
            act(tg_[:, 1, 0:sz], ps[:, b0 + 3, 0:sz], AF.Tanh, bias=hbg[:, 8 + m:9 + m], scale=0.5)
            stt('dve', mt_[:, 0, 0:sz], tg_[:, 0, 0:sz], 1.0, ps[:, b0, 0:sz], ALU.add, ALU.mult)
            stt('dve', mt_[:, 1, 0:sz], tg_[:, 1, 0:sz], 1.0, ps[:, b0 + 1, 0:sz], ALU.add, ALU.mult)
            tt('dve', mix[:, m, st:st + sz], mt_[:, 0, 0:sz], mt_[:, 1, 0:sz], ALU.add)
    dump("mix", mix[:, :, 0:256], BF16)
    h2T = hTo
    Z.release(mM)
    xr = [Z.alloc("xr%d" % i, [128, D], F32) for i in range(3)]
    x1t = [Z.alloc("x1t%d" % i, [128, D], F32) for i in range(3)]
    xn2 = [Z.alloc("xn2%d" % i, [128, D], F32) for i in range(3)]
    junk3 = Z.alloc("junk3", [128, D], BF16)
    g1h = Z.alloc("g1h", [128, D], F32)
    dma('sp', g1h[:], gbc_dram[:, 0, :], dram_reads=[t for t in GB_TAGS if t[1] == 0])
    ssq = Z.alloc("ssq3", [128, 34], F32)
    sqv = Z.alloc("sqv3", [128, 34], F32)
    rsv = Z.alloc("rsv3", [128, 34], F32)
    junk = junk3
    memset('dve', ssq[:], 0.0)

    def m2_s1(i):
        xr_, x1_, xn_ = xr[i % 3], x1t[i % 3], xn2[i % 3]
        dma('sp', xr_[:], xs[i * 128:(i + 1) * 128, :])
        for half in range(2):
            b = 4 + 2 * (i % 2) + half
            hs = slice(half * 512, (half + 1) * 512)
            for k in range(8):
                mm(ps[:, b, :], mix[:, k, i * 128:(i + 1) * 128], wout[:, k, hs], start=(k == 0), stop=(k == 7))
            tt('dve', x1_[:, hs], ps[:, b, :], g1h[:, hs], ALU.mult)
            tt('dve', x1_[:, hs], x1_[:, hs], xr_[:, hs], ALU.add)
        if i < 16:
            dma('sp', x1_dram[i * 128:(i + 1) * 128, :], x1_[:], dram_writes=[("x1", i)])
        if i == 0:
            dump("x1_0", x1_[:])
        norm_p1(x1_, xn_, i)

    def m2_s2(i):
        norm_p2(xn2[i % 3], lambda k, i=i: h2T[:, k, i * 128:(i + 1) * 128],
                lambda k: A2[:, k:k + 1], lambda k: modfm[:, 2, k, 0:1], 2 * (i % 2))

    for it in range(18):
        if it < 17:
            m2_s1(it)
        if it >= 1:
            m2_s2(it - 1)
    dump("h2T", h2T[:, :, 0:256], BF16)
    Z.cur = Z.lo
    if stop_after == 'M':
        return _finish()

    actb = Z.alloc("actb", [128, NFC, NO], BF16)
    mF = Z.mark()
    wup = [Z.alloc("wup%d" % i, [128, 2, 8, 128], BF16) for i in range(2)]
    abuf = Z.alloc("abuf", [128, NQ + 2], F32)
    acv = Z.alloc("acv", [128, NO], F32)
    sgv = Z.alloc("sgv", [128, NO], F32)
    memset('pool', abuf[:, 0:1], 0.0)
    w_up_v = w_up.rearrange("(k p) n -> p k n", p=128)

    def ffn_loads(c):
        wload(wup[c % 2][:, 0], w_up_v[:, :, c * 128:(c + 1) * 128])
        wload(wup[c % 2][:, 1], w_up_v[:, :, FFN + c * 128:FFN + (c + 1) * 128])

    ffn_loads(0)
    for c in range(NFC):
        if c + 1 < NFC:
            ffn_loads(c + 1)
        wu = wup[c % 2]
        for (st, sz) in blocks(0, NQ):
            b = nbank()
            for k in range(8):
                mm(ps[:, b, 0:sz], wu[:, 0, k, :], h2T[:, k, st:st + sz], start=(k == 0), stop=(k == 7))
            cp('act', abuf[:, 1 + st:1 + st + sz], ps[:, b, 0:sz])
        ts('dve', acv[:], abuf[:, 0:NO], fcw[:, c, 0:1], fcb[:, c:c + 1], ALU.mult, ALU.add)
        stt('dve', acv[:], abuf[:, 1:NO + 1], fcw[:, c, 1:2], acv[:], ALU.mult, ALU.add)
        stt('dve', acv[:], abuf[:, 2:NO + 2], fcw[:, c, 2:3], acv[:], ALU.mult, ALU.add)
        act(sgv[:], acv[:], AF.Tanh, scale=0.5)
        stt('dve', sgv[:], sgv[:], 1.0, acv[:], ALU.add, ALU.mult)
        for (st, sz) in blocks(0, NO):
            b = nbank()
            for k in range(8):
                mm(ps[:, b, 0:sz], wu[:, 1, k, :], h2T[:, k, st:st + sz], start=(k == 0), stop=(k == 7))
            stt('dve', actb[:, c, st:st + sz], sgv[:, st:st + sz], 0.5, ps[:, b, 0:sz], ALU.mult, ALU.mult)
        if c == 0:
            dump("act0", actb[:, 0, 0:256], BF16)
    Z.release(mF)
    wdn = Z.alloc("wdn", [128, NFC, D], BF16)
    wload(wdn[:], w_down.rearrange("(k p) n -> p k n", p=128))
    AH.cur = AH.lo
    x1l = [AH.alloc("x1l%d" % i, [128, D], F32) for i in range(2)]
    x2t = [AH.alloc("x2t%d" % i, [128, D], F32) for i in range(2)]
    junk4 = AH.alloc("junk4", [128, D], BF16)
    g2b = Z.alloc("g2b", [128, D], F32)
    fing = Z.alloc("fing", [128, D], F32)
    dma('sp', g2b[:], gbc_dram[:, 1, :], dram_reads=[t for t in GB_TAGS if t[1] == 1])
    dma('sp', fing[:], d_fing)
    ssq4 = Z.alloc("ssq4", [128, 16], F32)
    sq4 = Z.alloc("sq4", [128, 16], F32)
    rs4 = Z.alloc("rs4", [128, 16], F32)
    memset('dve', ssq4[:], 0.0)
    for i in range(16):
        xl, x2 = x1l[i % 2], x2t[i % 2]
        dma('sp', xl[:], x1_dram[i * 128:(i + 1) * 128, :], dram_reads=[("x1", i)])
        for half in range(2):
            b = 2 * (i % 2) + half
            for c in range(NFC):
                mm(ps[:, b, :], actb[:, c, i * 128:(i + 1) * 128], wdn[:, c, half * 512:(half + 1) * 512],
                   start=(c == 0), stop=(c == NFC - 1))
            tt('dve', x2[:, half * 512:(half + 1) * 512], ps[:, b, :], g2b[:, half * 512:(half + 1) * 512], ALU.mult)
            tt('pool', x2[:, half * 512:(half + 1) * 512], x2[:, half * 512:(half + 1) * 512],
               xl[:, half * 512:(half + 1) * 512], ALU.add)
        act(junk4[:], x2[:], AF.Square, accum_out=ssq4[:, i:i + 1])
        act(sq4[:, i:i + 1], ssq4[:, i:i + 1], AF.Sqrt, bias=epsv[:], scale=1.0 / D)
        recip(rs4[:, i:i + 1], sq4[:, i:i + 1])
        stt('dve', x2[:], x2[:], rs4[:, i:i + 1], fing[:], ALU.mult, ALU.mult)
        dma('sp', out[i * 128:(i + 1) * 128, :], x2[:])
    return _finish()


def _fm(v):
    v = np.asarray(v, np.float32)
    n = v.shape[-1] // 128
    return np.ascontiguousarray(v.reshape(n, 128).T)


def _rope_tables(pos):
    inv = (1.0 / (np.float32(10000.0) ** (np.arange(0, 16, 2, dtype=np.float32) / np.float32(16)))).astype(np.float32)
    row = (pos // 64).astype(np.float32)
    colp = (pos % 64).astype(np.float32)
    ang = np.concatenate([row[:, None] * inv[None, :], colp[:, None] * inv[None, :]], axis=-1).astype(np.float32)
    return np.cos(ang).astype(np.float32), np.sin(ang).astype(np.float32)


def make_in_maps(x, c, ctx, c_ctx, w_mod, b_mod, norm1_g, w_in, b_gate, q_norm_g, kv_norm_g,
                 w_uq, w_ukv, w_o_attn, lru_conv_w, lru_conv_b, lru_w_a, lru_b_a, lru_w_x,
                 lru_b_x, lru_lambda, w_o_lru, w_out, norm2_g, w_up, ffn_conv_w, ffn_conv_b,
                 w_down, final_g):
    import ml_dtypes
    f = lambda a: np.ascontiguousarray(np.asarray(a, np.float32))
    x, c, ctx, c_ctx = f(x), f(c), f(ctx), f(c_ctx)
    shared = {
        "w_mod": f(w_mod[0]), "w_in": f(w_in[0]), "w_uq": f(w_uq[0]), "w_ukv": f(w_ukv[0]),
        "w_o_attn": f(w_o_attn[0]), "w_o_lru": f(w_o_lru[0]), "w_out": f(w_out[0]), "w_up": f(w_up[0]),
        "w_down": f(w_down[0]),
        "bmodT": _fm(f(b_mod[0])),
        "bmodg": np.ascontiguousarray(np.broadcast_to(
            np.concatenate([f(b_mod[0])[2 * D:3 * D], f(b_mod[0])[5 * D:6 * D]])[None, :], (128, 2048))),
        "n1g": _fm(norm1_g[0]), "n2g": _fm(norm2_g[0]),
        "fing": np.ascontiguousarray(np.broadcast_to(f(final_g)[None, :], (128, D))),
        "bgate": _fm(b_gate[0]), "qng": _fm(q_norm_g[0]), "kvng": _fm(kv_norm_g[0]),
        "convb": _fm(lru_conv_b[0]), "fcb": _fm(ffn_conv_b[0]),
        "identf": np.eye(128, dtype=np.float32),
        "identb": np.eye(128, dtype=np.float32).astype(ml_dtypes.bfloat16),
    }
    lcw = f(lru_conv_w[0])
    fcw_ = f(ffn_conv_w[0])
    in_maps = []
    for core in range(8):
        b, half = core // 2, core % 2
        m = dict(shared)
        xb_ = x[b]
        cx = ctx[b]
        if half == 1:
            xb_ = xb_[::-1]
            cx = cx[::-1]
        m["xs"] = np.ascontiguousarray(xb_)
        m["ctxs"] = np.ascontiguousarray(cx)
        cT = np.stack([_fm(c[b]), _fm(c_ctx)], axis=-1)
        m["cT"] = np.ascontiguousarray(cT)
        dirs = [0, 1] if half == 0 else [1, 0]
        m["lru_wa"] = np.ascontiguousarray(f(lru_w_a[0])[dirs])
        m["lru_wx"] = np.ascontiguousarray(f(lru_w_x[0])[dirs])
        m["lba"] = np.ascontiguousarray(np.stack([_fm(f(lru_b_a[0])[d]) for d in dirs], axis=1))
        m["lbx"] = np.ascontiguousarray(np.stack([_fm(f(lru_b_x[0])[d]) for d in dirs], axis=1))
        m["lam"] = np.ascontiguousarray(np.stack([_fm(f(lru_lambda[0])[d]) for d in dirs], axis=1))
        w5 = np.zeros((5, LW), np.float32)
        if half == 0:
            w5[0:4] = lcw
        else:
            w5[1:5] = lcw[::-1]
        m["conv5"] = np.ascontiguousarray(np.stack([_fm(w5[j]) for j in range(5)], axis=-1))
        w3 = fcw_ if half == 0 else fcw_[::-1]
        m["fcw"] = np.ascontiguousarray(np.stack([_fm(w3[j]) for j in range(3)], axis=-1))
        pos = np.arange(NL)
        if half == 1:
            pos = NL - 1 - pos
        cos, sin = _rope_tables(pos)
        cs = np.concatenate([cos, sin], axis=-1).reshape(32, 128, 32).transpose(1, 0, 2)
        m["cs"] = np.ascontiguousarray(cs)
        in_maps.append(m)
    return in_maps


_CACHE = {}


def kernel(**inputs):
    in_maps = make_in_maps(**inputs)
    if "nc" not in _CACHE:
        from contextlib import ExitStack
        nc, P, A, _ = build_program(False)
        es = ExitStack()
        P.finalize(es)
        _CACHE["nc"] = nc
        _CACHE["es"] = es
    nc = _CACHE["nc"]
    res = run_bass_kernel_spmd(nc, in_maps, core_ids=list(range(8)))
    B = 4
    outp = np.zeros((B, NL, D), np.float32)
    for core in range(8):
        b, half = core // 2, core % 2
        o = np.asarray(res.results[core]["out"], np.float32)
        if half == 0:
            outp[b, 0:NO] = o
        else:
            outp[b, NO:NL] = o[::-1]
    return outp
```

```python
import numpy as np
from collections import defaultdict
import concourse.bass as bass
import concourse.mybir as mybir
from concourse.bass_utils import run_bass_kernel_spmd

F32 = mybir.dt.float32
BF16 = mybir.dt.bfloat16
AF = mybir.ActivationFunctionType
ALU = mybir.AluOpType

D = 1024
NL = 4096
NCX = 256
NA = NL + NCX
NQ = 2176
NO = 2048
QL, KVL, KR = 384, 256, 32
OFF_KV = QL
OFF_KR = OFF_KV + KVL
OFF_XB = OFF_KR + KR
LW = 1280
OFF_YB = OFF_XB + LW
OFF_G = OFF_YB + LW
IN_DIM = OFF_G + 2 * D
FFN = 2816
NFC = FFN // 128
EPS = 1e-6
SM_SCALE = 96 ** -0.5
GELU_C = 0.7978845608028654

OPTS = {}
ENGS = ['pe', 'act', 'dve', 'pool', 'sp']
BLOCK_ATTR = {'pe': 'tensor', 'act': 'scalar', 'dve': 'vector', 'pool': 'gpsimd', 'sp': 'sync'}
SAME_ENG_WINDOW = 1 << 30
NDMASEM = 8
FUSE_WAIT = True
BIN = 11


def dsize(dt):
    return mybir.dt.size(dt)


class Op:
    __slots__ = ('eng', 'idx', 'fn', 'waits', 'signal', 'dma', 'dsem', 'dcnt', 'snap', 'sigcount', 'dprev')


class Prog:
    def __init__(self, nc):
        self.nc = nc
        self.ops = {e: [] for e in ENGS}
        self.base = {}
        self.bins = {'sb': defaultdict(list), 'ps': defaultdict(list)}
        self.seen = {e: {e2: -1 for e2 in ENGS} for e in ENGS}
        self.seen_dma = {e: set() for e in ENGS}
        self.ndma = {e: 0 for e in ENGS}
        self.dram_w = {}
        self.dram_r = defaultdict(list)

    def region(self, ap):
        t = ap.tensor
        name = t.name
        if name not in self.base:
            return None
        space, base = self.base[name]
        pat = list(ap.ap)
        es = dsize(t.dtype)
        row = pat[0][0]
        off = ap.offset
        if row == 0:
            row = 1 << 40
        p0 = off // row
        f0 = off % row
        p1 = p0 + pat[0][1]
        lo = f0
        hi = f0
        for st, cnt in pat[1:]:
            ext = st * (cnt - 1)
            if ext < 0:
                lo += ext
            else:
                hi += ext
        b0, b1 = base + lo * es, base + (hi + 1) * es
        if space == 'ps':
            b0 = (b0 // 2048) * 2048
            b1 = ((b1 + 2047) // 2048) * 2048
            p0 = (p0 // 32) * 32
            p1 = ((p1 + 31) // 32) * 32
        return (space, p0, p1, b0, b1)

    def _conflicts(self, reg, is_write, out):
        space, p0, p1, b0, b1 = reg
        bins = self.bins[space]
        for bn in range(b0 >> BIN, ((b1 - 1) >> BIN) + 1):
            lst = bins.get(bn)
            if not lst:
                continue
            for rec in lst:
                if rec[1] < p1 and p0 < rec[2] and rec[3] < b1 and b0 < rec[4]:
                    if is_write or rec[0]:
                        out.add(rec[5])

    def _register(self, reg, is_write, opref):
        space, p0, p1, b0, b1 = reg
        bins = self.bins[space]
        rec = (is_write, p0, p1, b0, b1, opref)
        for bn in range(b0 >> BIN, ((b1 - 1) >> BIN) + 1):
            lst = bins[bn]
            if is_write:
                lst[:] = [r for r in lst if not (p0 <= r[1] and r[2] <= p1 and b0 <= r[3] and r[4] <= b1)]
            elif not opref[2]:
                lst[:] = [r for r in lst if not ((not r[0]) and r[1] == p0 and r[2] == p1 and r[3] == b0
                                                 and r[4] == b1 and r[5][0] == opref[0] and not r[5][2])]
            lst.append(rec)

    def emit(self, eng, fn, reads=(), writes=(), dma=False, dram_reads=(), dram_writes=()):
        deps = set()
        rr = [r for r in (self.region(a) for a in reads) if r is not None]
        ww = [r for r in (self.region(a) for a in writes) if r is not None]
        ww = ww + [r for r in rr if r[0] == 'ps']
        rr = [r for r in rr if r[0] != 'ps']
        for r in rr:
            self._conflicts(r, False, deps)
        for r in ww:
            self._conflicts(r, True, deps)
        for tag in dram_reads:
            if tag in self.dram_w:
                deps.add(self.dram_w[tag])
        for tag in dram_writes:
            if tag in self.dram_w:
                deps.add(self.dram_w[tag])
            deps.update(self.dram_r.get(tag, ()))
        op = Op()
        op.eng = eng
        op.idx = len(self.ops[eng])
        op.fn = fn
        op.signal = False
        op.dma = dma
        op.dsem = None
        op.dcnt = 0
        op.sigcount = 0
        op.dprev = None
        seen = self.seen[eng]
        sdma = self.seen_dma[eng]
        waits = []
        if dma:
            k = self.ndma[eng]
            self.ndma[eng] += 1
            op.dsem = (eng, k % NDMASEM)
            op.dcnt = 16 * (k // NDMASEM + 1)
            if k >= NDMASEM:
                prev = self._dma_ops[eng][k - NDMASEM]
                deps.add((eng, prev.idx, True))
            self._dma_ops.setdefault(eng, []).append(op)
        best = {}
        dl = []
        for (e2, i2, d2) in deps:
            if d2:
                dl.append((e2, i2, d2))
            elif i2 > best.get(e2, -1):
                best[e2] = i2
        dl.sort()
        dl += [(e2, i2, False) for e2, i2 in sorted(best.items())]
        for (e2, i2, d2) in dl:
            src = self.ops[e2][i2]
            if d2:
                if (e2, i2) in sdma:
                    continue
                sdma.add((e2, i2))
                waits.append((e2, i2))
            else:
                if e2 == eng:
                    if eng == 'pe' or op.idx - i2 > SAME_ENG_WINDOW:
                        continue
                if i2 <= seen[e2]:
                    continue
                seen[e2] = i2
                src.signal = True
                waits.append((e2, i2))
            for e3, v in src.snap.items():
                if v > seen[e3]:
                    seen[e3] = v
        op.waits = waits
        op.snap = dict(seen)
        self.ops[eng].append(op)
        ref = (eng, op.idx, dma)
        for r in rr:
            self._register(r, False, ref)
        for r in ww:
            self._register(r, True, ref)
        for tag in dram_reads:
            self.dram_r[tag].append(ref)
        for tag in dram_writes:
            self.dram_w[tag] = ref
            self.dram_r[tag] = []
        return op

    _dma_ops = None

    def finalize(self, es):
        nc = self.nc
        self.sem = {e: es.enter_context(nc.semaphore("s_" + e)) for e in ENGS}
        self.dsems = {}
        for e in ENGS:
            if self.ndma[e]:
                for j in range(min(NDMASEM, self.ndma[e])):
                    self.dsems[(e, j)] = es.enter_context(nc.semaphore("d_%s%d" % (e, j)))
        for e in ENGS:
            c = 0
            for op in self.ops[e]:
                if op.signal and not op.dma:
                    c += 1
                    op.sigcount = c
        block = es.enter_context(nc.Block())
        for e in ENGS:
            self._replay_engine(block, e)

    def _replay_engine(self, block, e):
        ops = self.ops[e]
        allops = self.ops
        sem = self.sem
        dsems = self.dsems

        def body(eng):
            for op in ops:
                wl = []
                for (e2, i2) in op.waits:
                    src = allops[e2][i2]
                    if src.dma:
                        wl.append((dsems[src.dsem], src.dcnt))
                    else:
                        wl.append((sem[e2], src.sigcount))
                fuse = wl.pop() if (wl and FUSE_WAIT and not op.dma) else None
                for (sm, v) in wl:
                    eng.wait_ge(sm, v)
                ins = op.fn(eng)
                if fuse is not None:
                    ins._wait_ge(fuse[0], fuse[1])
                if op.dma:
                    ins.then_inc(dsems[op.dsem], 16)
                elif op.signal:
                    ins.then_inc(sem[e], 1)
        getattr(block, BLOCK_ATTR[e])(body)


class Arena:
    def __init__(self, nc, prog, lo=20480, hi=229376):
        self.nc, self.prog, self.lo, self.hi, self.cur = nc, prog, lo, hi, lo
        self.n = 0
        self.peak = lo

    def alloc(self, name, shape, dt):
        per = int(np.prod(shape[1:])) * dsize(dt)
        per = (per + 63) // 64 * 64
        assert self.cur + per <= self.hi, "SBUF arena overflow at %s: need %d have %d" % (name, per, self.hi - self.cur)
        self.n += 1
        t = self.nc.alloc_sbuf_tensor_at("%s_%d_%d" % (name, self.lo, self.n), list(shape), dt, offset=self.cur)
        self.prog.base[t.name] = ('sb', self.cur)
        self.cur += per
        self.peak = max(self.peak, self.cur)
        return t

    def alloc_top(self, name, shape, dt):
        per = int(np.prod(shape[1:])) * dsize(dt)
        per = (per + 63) // 64 * 64
        assert self.hi - per >= self.cur, "SBUF arena overflow (top) at %s" % name
        self.hi -= per
        self.n += 1
        t = self.nc.alloc_sbuf_tensor_at("%s_t%d" % (name, self.n), list(shape), dt, offset=self.hi)
        self.prog.base[t.name] = ('sb', self.hi)
        return t

    def mark(self):
        return (self.cur, self.hi)

    def release(self, m):
        self.cur, self.hi = m


def blocks(lo, hi, step=512):
    out = []
    s = lo
    while s < hi:
        out.append((s, min(step, hi - s)))
        s += step
    return out


def build_program(debug=False, stop_after=None):
    nc = bass.Bass("TRN2", target_bir_lowering=False)
    P = Prog(nc)
    P._dma_ops = {}
    LO0 = 20480
    PERS = 5 * 1024
    HTO = 8 * NQ * 2
    A = Arena(nc, P, LO0, LO0 + PERS)
    AH = Arena(nc, P, LO0 + PERS, LO0 + PERS + HTO)
    Z = Arena(nc, P, LO0 + PERS + HTO, 229312)
    dbg_outs = {}

    def din(name, shape, dt=F32):
        return nc.dram_tensor(name, list(shape), dt, kind="ExternalInput").ap()

    xs = din("xs", [NL, D])
    ctxs = din("ctxs", [NCX, D])
    d_cT = din("cT", [128, 8, 2])
    d_bmodT = din("bmodT", [128, 48])
    d_bmodg = din("bmodg", [128, 2048])
    d_n1g = din("n1g", [128, 8])
    d_n2g = din("n2g", [128, 8])
    d_fing = din("fing", [128, D])
    d_bgate = din("bgate", [128, 16])
    d_qng = din("qng", [128, 3])
    d_kvng = din("kvng", [128, 2])
    d_conv5 = din("conv5", [128, 10, 5])
    d_convb = din("convb", [128, 10])
    d_lba = din("lba", [128, 2, 10])
    d_lbx = din("lbx", [128, 2, 10])
    d_lam = din("lam", [128, 2, 10])
    d_fcw = din("fcw", [128, NFC, 3])
    d_fcb = din("fcb", [128, NFC])
    d_cs = din("cs", [128, 32, 32])
    d_identf = din("identf", [128, 128])
    d_identb = din("identb", [128, 128], BF16)
    w_mod = din("w_mod", [D, 6 * D])
    w_in = din("w_in", [D, IN_DIM])
    w_uq = din("w_uq", [QL, 768])
    w_ukv = din("w_ukv", [KVL, 1024])
    w_o_attn = din("w_o_attn", [512, D])
    w_o_lru = din("w_o_lru", [LW, D])
    w_out = din("w_out", [D, D])
    w_up = din("w_up", [D, 2 * FFN])
    w_down = din("w_down", [FFN, D])
    lru_wa = din("lru_wa", [2, 10, 128, 128])
    lru_wx = din("lru_wx", [2, 10, 128, 128])
    out = nc.dram_tensor("out", [NO, D], F32, kind="ExternalOutput").ap()
    s_dram = nc.dram_tensor("s_scr", [10, 128, NQ], BF16, kind="Internal").ap()
    gbc_dram = nc.dram_tensor("gbc_scr", [128, 2, D], F32, kind="Internal").ap()
    x1_dram = nc.dram_tensor("x1_scr", [NO, D], F32, kind="Internal").ap()

    ps = nc.alloc_psum_tensor("ps", [128, 8, 512], F32)
    P.base[ps.name] = ('ps', 0)
    psb = ps[:, :, :].bitcast(BF16)
    P.base[psb.tensor.name] = ('ps', 0)

    def dma(q, out_, in_, dram_reads=(), dram_writes=()):
        return P.emit(q, lambda e: e.dma_start(out=out_, in_=in_), reads=[in_], writes=[out_], dma=True,
                      dram_reads=dram_reads, dram_writes=dram_writes)

    def mm(out_, lhsT, rhs, start=True, stop=True):
        return P.emit('pe', lambda e: e.matmul(out_, lhsT=lhsT, rhs=rhs, start=start, stop=stop),
                      reads=[lhsT, rhs], writes=[out_])

    def tr(out_, in_, ident):
        return P.emit('pe', lambda e: e.transpose(out=out_, in_=in_, identity=ident), reads=[in_, ident], writes=[out_])

    def act(out_, in_, func, bias=0.0, scale=1.0, accum_out=None, eng='act'):
        rd = [in_]
        if not isinstance(bias, float):
            rd.append(bias)
        if not isinstance(scale, float):
            rd.append(scale)
        wr = [out_]
        kw = {}
        if accum_out is not None:
            wr.append(accum_out)
            kw['accum_out'] = accum_out
        return P.emit('act', lambda e: e.activation(out=out_, in_=in_, func=func, bias=bias, scale=scale, **kw),
                      reads=rd, writes=wr)

    def tt(eng, out_, in0, in1, op):
        return P.emit(eng, lambda e: e.tensor_tensor(out=out_, in0=in0, in1=in1, op=op), reads=[in0, in1], writes=[out_])

    def ts(eng, out_, in0, s1, s2, op0, op1=None):
        rd = [in0]
        if not isinstance(s1, float):
            rd.append(s1)
        if s2 is not None and not isinstance(s2, float):
            rd.append(s2)
        if op1 is None:
            return P.emit(eng, lambda e: e.tensor_scalar(out=out_, in0=in0, scalar1=s1, scalar2=None, op0=op0),
                          reads=rd, writes=[out_])
        return P.emit(eng, lambda e: e.tensor_scalar(out=out_, in0=in0, scalar1=s1, scalar2=s2, op0=op0, op1=op1),
                      reads=rd, writes=[out_])

    def stt(eng, out_, in0, scalar, in1, op0, op1):
        rd = [in0, in1]
        if not isinstance(scalar, float):
            rd.append(scalar)
        return P.emit(eng, lambda e: e.scalar_tensor_tensor(out=out_, in0=in0, scalar=scalar, in1=in1, op0=op0, op1=op1),
                      reads=rd, writes=[out_])

    def cp(eng, out_, in_):
        if eng == 'act':
            return act(out_, in_, AF.Identity)
        return P.emit(eng, lambda e: e.tensor_copy(out=out_, in_=in_), reads=[in_], writes=[out_])

    def memset(eng, ap, val):
        return P.emit(eng, lambda e: e.memset(ap, val), writes=[ap])

    def recip(out_, in_):
        return P.emit('dve', lambda e: e.reciprocal(out=out_, in_=in_), reads=[in_], writes=[out_])

    def scan(out_, d0, d1, initial):
        rd = [d0, d1]
        if not isinstance(initial, float):
            rd.append(initial)
        return P.emit('dve', lambda e: e.tensor_tensor_scan(out=out_, data0=d0, data1=d1, initial=initial,
                                                            op0=ALU.mult, op1=ALU.add), reads=rd, writes=[out_])

    def dump(name, ap, dt=F32):
        if not debug:
            return
        shape = list(ap.shape)
        t = nc.dram_tensor("dbg_" + name, shape, dt, kind="ExternalOutput").ap()
        dbg_outs[name] = t
        dma('sp', t, ap)

    def wload(dst, src):
        return dma('pool', dst, src)

    def _finish():
        P.emit('sp', lambda e: e.nop(), reads=[], writes=[])
        last = P.ops['sp'][-1]
        for o in P._dma_ops.get('sp', [])[-NDMASEM:]:
            if (o.eng, o.idx) not in last.waits:
                last.waits.append((o.eng, o.idx))
        return nc, P, A, dbg_outs

    cT = A.alloc("cT", [128, 8, 2], F32)
    bmodT = A.alloc("bmodT", [128, 48], F32)
    n1g = A.alloc("n1g", [128, 8], F32)
    n2g = A.alloc("n2g", [128, 8], F32)
    bgate = A.alloc("bgate", [128, 16], F32)
    qng = A.alloc("qng", [128, 3], F32)
    kvng = A.alloc("kvng", [128, 2], F32)
    conv5 = A.alloc("conv5", [128, 10, 5], F32)
    convb = A.alloc("convb", [128, 10], F32)
    lba = A.alloc("lba", [128, 2, 10], F32)
    lbx = A.alloc("lbx", [128, 2, 10], F32)
    lam = A.alloc("lam", [128, 2, 10], F32)
    fcw = A.alloc("fcw", [128, NFC, 3], F32)
    fcb = A.alloc("fcb", [128, NFC], F32)
    identf = A.alloc("identf", [128, 128], F32)
    identb = A.alloc("identb", [128, 128], BF16)
    for dst, src in ((cT, d_cT), (bmodT, d_bmodT), (n1g, d_n1g), (n2g, d_n2g), (bgate, d_bgate),
                     (qng, d_qng), (kvng, d_kvng), (conv5, d_conv5), (convb, d_convb), (lba, d_lba), (lbx, d_lbx),
                     (lam, d_lam), (fcw, d_fcw), (fcb, d_fcb), (identf, d_identf), (identb, d_identb)):
        dma('sp', dst[:], src)

    epsv = A.alloc("epsv", [128, 1], F32)
    memset('dve', epsv[:], EPS)
    qtr = A.alloc("qtr", [128, 1], F32)
    memset('dve', qtr[:], 0.25)
    modfm = A.alloc("modfm", [128, 4, 8, 2], F32)
    A1 = A.alloc("A1", [128, 8, 2], F32)
    A2 = A.alloc("A2", [128, 8], F32)
    hbg = A.alloc("hbg", [128, 16], F32)
    hba = A.alloc("hba", [128, 2, 10], F32)
    hbx = A.alloc("hbx", [128, 2, 10], F32)
    c1 = A.alloc("c1", [128, 2, 10], F32)
    hc = A.alloc("hc", [128, 2, 10], F32)
    hTo = AH.alloc("hTo", [128, 8, NQ], BF16)
    hTx = Z.alloc("hTx", [128, 8, NA - NQ], BF16)

    def hTc(k, st, sz):
        if st + sz <= NQ:
            return hTo[:, k, st:st + sz]
        assert st >= NQ
        return hTx[:, k, st - NQ:st - NQ + sz]

    ALLBLK = blocks(0, NQ) + blocks(NQ, NL) + blocks(NL, NA)

    sc = Z.alloc("sc", [128, 8, 2], F32)
    sc_rep = Z.alloc("sc_rep", [128, 8, 128], F32)
    m0 = Z.mark()
    th0 = Z.alloc("th0", [128, 8, 2], F32)
    wm = [Z.alloc("wm%d" % i, [128, 8, 1024], F32) for i in range(2)]
    act(th0[:], cT[:], AF.Tanh, scale=0.5)
    ts('dve', th0[:], th0[:], 0.5, 0.5, ALU.mult, ALU.add)
    tt('dve', sc[:], th0[:], cT[:], ALU.mult)
    for k in range(8):
        cp('dve', sc_rep[:, k, :], sc[:, k, 0:1].to_broadcast([128, 128]))
    act(c1[:], lam[:], AF.Exp, scale=-1.0)
    act(c1[:], c1[:], AF.Ln, bias=1.0)
    ts('dve', hc[:], c1[:], -4.0, None, ALU.mult)
    ts('dve', c1[:], c1[:], -8.0, None, ALU.mult)
    ts('dve', hba[:], lba[:], 0.5, None, ALU.mult)
    ts('dve', hbx[:], lbx[:], 0.5, None, ALU.mult)
    ts('dve', hbg[:], bgate[:], 0.5, None, ALU.mult)
    w_mod_v = w_mod.rearrange("(k p) n -> p k n", p=128)
    fm_idx = {0: 0, 1: 1, 3: 2, 4: 3}

    def mod_slab(s):
        wb = wm[s % 2]
        dma('pool', wb[:], w_mod_v[:, :, s * 1024:(s + 1) * 1024])
        if s in fm_idx:
            bank = 4 + s % 2
            for j in range(8):
                for k in range(8):
                    mm(ps[:, bank, 2 * j:2 * j + 2], wb[:, k, j * 128:(j + 1) * 128], sc[:, k, :], start=(k == 0), stop=(k == 7))
            for col in range(2):
                tt('dve', modfm[:, fm_idx[s], :, col], ps[:, bank, col:16:2], bmodT[:, s * 8:(s + 1) * 8], ALU.add)
        else:
            gi = 0 if s == 2 else 1
            for half in range(2):
                bank = 6 + half
                for k in range(8):
                    mm(ps[:, bank, :], sc_rep[:, k, :], wb[:, k, half * 512:(half + 1) * 512], start=(k == 0), stop=(k == 7))
                tt('dve', gbc[:, gi, half * 512:(half + 1) * 512], ps[:, bank, :],
                   bmodg[:, gi * 1024 + half * 512: gi * 1024 + (half + 1) * 512], ALU.add)

    mod_slab(0)
    mod_slab(1)
    for col in range(2):
        stt('dve', A1[:, :, col], modfm[:, 1, :, col], 1.0, n1g[:], ALU.add, ALU.mult)

    xt = [Z.alloc("xt%d" % i, [128, D], F32) for i in range(3)]
    junk = Z.alloc("junk", [128, D], BF16)
    ssq = Z.alloc("ssq", [128, 34], F32)
    sqv = Z.alloc("sqv", [128, 34], F32)
    rsv = Z.alloc("rsv", [128, 34], F32)
    memset('dve', ssq[:], 0.0)

    def norm_p1(buf, bufo, statc):
        act(junk[:], buf[:], AF.Square, accum_out=ssq[:, statc:statc + 1])
        act(sqv[:, statc:statc + 1], ssq[:, statc:statc + 1], AF.Sqrt, bias=epsv[:], scale=1.0 / D)
        recip(rsv[:, statc:statc + 1], sqv[:, statc:statc + 1])
        ts('dve', bufo[:], buf[:], rsv[:, statc:statc + 1], None, ALU.mult)

    def norm_p2(bufo, dstf, Ascale, Abias, bank0):
        for k in range(8):
            tr(ps[:, bank0 + k // 4, (k % 4) * 128:(k % 4 + 1) * 128], bufo[:, k * 128:(k + 1) * 128], identf[:])
        for k in range(8):
            src = ps[:, bank0 + k // 4, (k % 4) * 128:(k % 4 + 1) * 128]
            if k < 4:
                act(dstf(k), src, AF.Identity, bias=Abias(k), scale=Ascale(k))
            else:
                ts('dve', dstf(k), src, Ascale(k), Abias(k), ALU.mult, ALU.add)

    def norm_to_T(buf, dstf, statc, Ascale, Abias, bank0):
        norm_p1(buf, buf, statc)
        norm_p2(buf, dstf, Ascale, Abias, bank0)

    for it in range(35):
        if it < 34:
            i = it
            buf = xt[i % 3]
            src = xs[i * 128:(i + 1) * 128, :] if i < 32 else ctxs[(i - 32) * 128:(i - 31) * 128, :]
            dma('sp', buf[:], src)
            norm_p1(buf, buf, i)
        if it >= 1:
            i = it - 1
            col = 0 if i < 32 else 1
            norm_p2(xt[i % 3], lambda k, i=i: hTc(k, i * 128, 128),
                    lambda k, col=col: A1[:, k, col:col + 1], lambda k, col=col: modfm[:, 0, k, col:col + 1],
                    2 * (i % 2))
    dump("hT", hTo[:, :, 0:256], BF16)
    dump("hTc", hTx[:, :, NL - NQ:NA - NQ], BF16)
    Z.release(m0)

    if stop_after in ('0', 'A'):
        return _finish()

    mL = Z.mark()
    SB = 1216
    wxb = [Z.alloc("wxb%d" % i, [128, 8, 128], BF16) for i in range(2)]
    wyb = [Z.alloc("wyb%d" % i, [128, 8, 128], BF16) for i in range(2)]
    wg = [Z.alloc("wg%d" % i, [128, 4, 128], BF16) for i in range(2)]
    xbL = Z.alloc("xbL", [128, NL + 4], F32)
    xbC = Z.alloc("xbC", [128, NCX + 4], F32)
    xc = Z.alloc("xc", [128, NA], F32)
    xcb = Z.alloc("xcb", [128, NA], BF16)
    TS = [[Z.alloc("T%d%s" % (j, "ab"[i]), [128, SB], F32) for j in range(3)] for i in range(2)]
    trb = [Z.alloc("trb%d" % i, [128, 512], F32) for i in range(2)]
    hB = Z.alloc("hB", [128, NQ], F32)
    ybs = Z.alloc("ybs", [128, NQ], F32)
    gtmp = Z.alloc("gtmp", [128, NQ], F32)
    sblk = Z.alloc("sblk", [128, NQ], BF16)
    wmp = [Z.alloc("wmp%d" % i, [128, 8, 128], F32) for i in range(2)]
    bmp = [Z.alloc("bmp%d" % i, [128, 128], F32) for i in range(2)]
    gpc = [Z.alloc("gpc%d" % i, [128, 128], F32) for i in range(2)]
    mod_pieces = []
    GB_TAGS = []
    for s_ in (2, 3, 4, 5):
        for j in range(8):
            def piece(s_=s_, j=j):
                i_ = len(piece_cnt)
                piece_cnt.append(1)
                w_ = wmp[i_ % 2]
                dma('pool', w_[:], w_mod_v[:, :, s_ * 1024 + j * 128: s_ * 1024 + (j + 1) * 128])
                b = nbank()
                if s_ in fm_idx:
                    for k in range(8):
                        mm(ps[:, b, 0:2], w_[:, k, :], sc[:, k, :], start=(k == 0), stop=(k == 7))
                    tt('dve', modfm[:, fm_idx[s_], j, :], ps[:, b, 0:2],
                       bmodT[:, s_ * 8 + j:s_ * 8 + j + 1].to_broadcast([128, 2]), ALU.add)
                else:
                    gi = 0 if s_ == 2 else 1
                    bm_, g_ = bmp[i_ % 2], gpc[i_ % 2]
                    dma('sp', bm_[:], d_bmodg[:, gi * 1024 + j * 128: gi * 1024 + (j + 1) * 128])
                    for k in range(8):
                        mm(ps[:, b, 0:128], sc_rep[:, k, :], w_[:, k, :], start=(k == 0), stop=(k == 7))
                    tt('dve', g_[:], ps[:, b, 0:128], bm_[:], ALU.add)
                    if gi == 0:
                        ts('dve', g_[:], g_[:], 0.5, None, ALU.mult)
                    tag = ("gbc", gi, j)
                    GB_TAGS.append(tag)
                    dma('sp', gbc_dram[:, gi, j * 128:(j + 1) * 128], g_[:], dram_writes=[tag])
            mod_pieces.append(piece)
    piece_cnt = []
    memset('pool', xbL[:, 0:2], 0.0)
    memset('pool', xbL[:, NL + 2:NL + 4], 0.0)
    memset('pool', xbC[:, 0:2], 0.0)
    memset('pool', xbC[:, NCX + 2:NCX + 4], 0.0)
    w_in_v = w_in.rearrange("(k p) n -> p k n", p=128)
    psrot = [0]

    def nbank():
        b = psrot[0]
        psrot[0] = (b + 1) % 8
        return b

    def lru_loads(n):
        wload(wxb[n % 2][:], w_in_v[:, :, OFF_XB + n * 128: OFF_XB + (n + 1) * 128])
        wload(wyb[n % 2][:], w_in_v[:, :, OFF_YB + n * 128: OFF_YB + (n + 1) * 128])
        for d in range(2):
            wload(wg[n % 2][:, 2 * d, :], lru_wa[d, n])
            wload(wg[n % 2][:, 2 * d + 1, :], lru_wx[d, n])

    def xb_tasks(n):
        wx_ = wxb[n % 2]
        out = []
        for (st, sz) in blocks(NQ, NL) + blocks(NL, NA) + blocks(0, NQ):
            def task(st=st, sz=sz):
                b = nbank()
                for k in range(8):
                    mm(ps[:, b, 0:sz], wx_[:, k, :], hTc(k, st, sz), start=(k == 0), stop=(k == 7))
                ev = 'dve' if (OPTS.get('DVEEV', 1) and (st // 512) % 2 == 0) else 'act'
                if st < NL:
                    cp(ev, xbL[:, 2 + st:2 + st + sz], ps[:, b, 0:sz])
                else:
                    cp(ev, xbC[:, 2 + st - NL:2 + st - NL + sz], ps[:, b, 0:sz])
            out.append(task)
        return out

    def yb_tasks(n):
        wy_ = wyb[n % 2]
        out = []
        for (st, sz) in blocks(0, NQ):
            def task(st=st, sz=sz):
                b = nbank()
                for k in range(8):
                    mm(ps[:, b, 0:sz], wy_[:, k, :], hTc(k, st, sz), start=(k == 0), stop=(k == 7))
                cp('dve' if OPTS.get('DVEEV', 1) else 'act', ybs[:, st:st + sz], ps[:, b, 0:sz])
            out.append(task)
        return out

    SUBS = [
        (1, [(NL, NA, True, None), (3136, NL, True, None)]),
        (1, [(NQ, 3136, True, None), (NO, NQ, True, ('hB', NO))]),
        (1, [(1024, NO, True, ('hB', 1024))]),
        (1, [(0, 1024, True, ('hB', 0))]),
        (0, [(NL, NA, False, None), (0, 960, False, ('add', 0))]),
        (0, [(960, NQ, False, ('add', 960))]),
    ]

    deferred = []
    lru_loads(0)
    lru_loads(1)
    for t in xb_tasks(0):
        t()
    for n in range(10):
        wg_ = wg[n % 2]
        def conv(nn, parts, cast=True):
            for (src, o0, st, sz) in parts:
                dst = xc[:, o0 + st:o0 + st + sz]
                ts('dve', dst, src[:, st:st + sz], conv5[:, nn, 0:1], convb[:, nn:nn + 1], ALU.mult, ALU.add)
                for j in range(1, 5):
                    stt('dve', dst, src[:, st + j:st + j + sz], conv5[:, nn, j:j + 1], dst, ALU.mult, ALU.add)
                if cast:
                    cp('act', xcb[:, o0 + st:o0 + st + sz], dst)

        CE = 2304 if OPTS.get('CSPLIT', 1) else NL
        if n == 0 and CE < NL:
            conv(0, [(xbL, 0, CE, NL - CE)])
        if CE < NL:
            conv(n, [(xbC, NL, 0, NCX), (xbL, 0, NO, CE - NO)])
            own_parts = [(xbL, 0, 0, 1024), (xbL, 0, 1024, 1024)]
            while deferred:
                deferred.pop(0)()
        else:
            conv(n, [(xbL, 0, 0, 2048), (xbL, 0, 2048, 2048), (xbC, NL, 0, NCX)])
            own_parts = []
        pend = yb_tasks(n) + (xb_tasks(n + 1) if n + 1 < 10 else [])
        n_after_yb = len(pend) - len(blocks(0, NQ))
        for _ in range(4):
            if mod_pieces:
                pend.append(mod_pieces.pop(0))
        per = (len(pend) + 5) // 6

        def gates(d, segs, tset):
            T1, T2, T3 = tset
            pos = 0
            seginfo = []
            for (lo, hi, rev, outspec) in segs:
                for (st, sz) in blocks(lo, hi):
                    b1_, b2_ = nbank(), nbank()
                    mm(ps[:, b1_, 0:sz], wg_[:, 2 * d, :], xcb[:, st:st + sz])
                    mm(ps[:, b2_, 0:sz], wg_[:, 2 * d + 1, :], xcb[:, st:st + sz])
                    tb = trb[(st // 512) % 2]
                    q0 = pos + (st - lo)
                    act(tb[:, 0:sz], ps[:, b1_, 0:sz], AF.Tanh, bias=hba[:, d, n:n + 1], scale=0.5)
                    act(T1[:, q0:q0 + sz], ps[:, b2_, 0:sz], AF.Tanh, bias=hbx[:, d, n:n + 1], scale=0.5)
                    act(T2[:, q0:q0 + sz], tb[:, 0:sz], AF.Exp, bias=hc[:, d, n:n + 1], scale=hc[:, d, n:n + 1])
                    act(T3[:, q0:q0 + sz], tb[:, 0:sz], AF.Exp, bias=c1[:, d, n:n + 1], scale=c1[:, d, n:n + 1])
                    stt('dve', T1[:, q0:q0 + sz], T1[:, q0:q0 + sz], 1.0, xc[:, st:st + sz], ALU.add, ALU.mult)
                seginfo.append((pos, hi - lo, rev, outspec))
                pos += hi - lo
            return seginfo, pos

        def sqrt_u(tset, L):
            T1, T2, T3 = tset
            act(T3[:, 0:L], T3[:, 0:L], AF.Sqrt, bias=qtr[:], scale=-0.25)
            tt('dve', T1[:, 0:L], T1[:, 0:L], T3[:, 0:L], ALU.mult)

        def scans(tset, seginfo, cur):
            T1, T2, T3 = tset
            for (p0, ln, rev, outspec) in seginfo:
                if outspec is not None and outspec[0] == 'hB':
                    o = hB[:, outspec[1]:outspec[1] + ln]
                elif outspec is not None and outspec[0] == 'add':
                    o = ybs[:, outspec[1]:outspec[1] + ln]
                else:
                    o = T3[:, p0:p0 + ln]
                a_, u_ = T2[:, p0:p0 + ln], T1[:, p0:p0 + ln]
                if rev:
                    scan(o[:, ::-1], a_[:, ::-1], u_[:, ::-1], cur)
                    cur = o[:, 0:1]
                else:
                    scan(o, a_, u_, cur)
                    cur = o[:, ln - 1:ln]
            return cur

        cur = 0.0
        for pr in range(3):
            infos = []
            for j in range(2):
                d, segs = SUBS[2 * pr + j]
                infos.append(gates(d, segs, TS[j]))
                if own_parts:
                    pA, pB = own_parts
                    if pr == 0 and j == 0:
                        conv(n, [pB], cast=False)
                    if pr == 0 and j == 1:
                        cp('act', xcb[:, pB[2]:pB[2] + pB[3]], xc[:, pB[2]:pB[2] + pB[3]])
                    if pr == 1 and j == 0:
                        cp('act', xcb[:, pA[2]:pA[2] + pA[3]], xc[:, pA[2]:pA[2] + pA[3]])
                    if pr == 2 and j == 1 and n + 1 < 10:
                        cp('act', xcb[:, CE:NL], xc[:, CE:NL])
                for _ in range(per):
                    if pend:
                        pend.pop(0)()
            if pr == 2:
                cur = 0.0
            for j in range(2):
                sqrt_u(TS[j], infos[j][1])
            for j in range(2):
                cur = scans(TS[j], infos[j][0], cur)
            if pr == 0 and own_parts:
                conv(n, [own_parts[0]], cast=False)
            if pr == 1 and n + 1 < 10 and CE < NL:
                conv(n + 1, [(xbL, 0, CE, NL - CE)], cast=False)
            if pr == 0:
                while len(pend) > n_after_yb:
                    pend.pop(0)()
                if not OPTS.get('ACTGELU', 1):
                    tt('dve', gtmp[:], ybs[:], ybs[:], ALU.mult)
                    ts('dve', gtmp[:], gtmp[:], 0.044715, 1.0, ALU.mult, ALU.add)
                    tt('dve', gtmp[:], gtmp[:], ybs[:], ALU.mult)
            if pr == 1:
                if OPTS.get('ACTGELU', 1):
                    act(gtmp[:], ybs[:], AF.Gelu_apprx_tanh)
                else:
                    act(gtmp[:], gtmp[:], AF.Tanh, scale=GELU_C)
                    stt('dve', gtmp[:], gtmp[:], 1.0, ybs[:], ALU.add, ALU.mult)
        while pend:
            pend.pop(0)()
        if n == 0:
            dump("xc0", xc[:, 0:512])
            dump("xc0c", xc[:, NL:NA])
        if n + 2 < 10:
            lru_loads(n + 2)

        def epilogue(n=n):
            tt('dve', hB[:], hB[:], ybs[:], ALU.add)
            if n == 0:
                dump("hsum0", hB[:])
            tt('dve', sblk[:], gtmp[:], hB[:], ALU.mult)
            dma('sp', s_dram[n], sblk[:], dram_writes=[("s", n)])
            if n == 0:
                dump("s0", sblk[:], BF16)
        deferred.append(epilogue)
    while deferred:
        deferred.pop(0)()
    assert not mod_pieces
    stt('dve', A2[:], modfm[:, 3, :, 0], 1.0, n2g[:], ALU.add, ALU.mult)
    dump("modfm", modfm[:])
    Z.release(mL)

    if stop_after == 'L':
        return _finish()

    qT = Z.alloc_top("qT", [96, 8, NQ], BF16)
    krT = Z.alloc_top("krT", [96, NA], BF16)
    kvT = Z.alloc_top("kvT", [128, 2, NA], BF16)
    cs = Z.alloc_top("cs", [128, 32, 32], F32)
    wuq = Z.alloc_top("wuq", [128, 3, 768], BF16)
    wK = Z.alloc_top("wK", [128, 2, 8, 64], BF16)
    wV = Z.alloc_top("wV", [128, 2, 8, 64], BF16)
    dma('sp', cs[:], d_cs)
    mB = Z.mark()
    stg_q = Z.alloc("stg_q", [128, 3, 768], F32)
    stg_kv = Z.alloc("stg_kv", [128, 2, 8, 128], F32)
    dma('sp', stg_q[:], w_uq.rearrange("(k p) n -> p k n", p=128))
    dma('sp', stg_kv[:], w_ukv.rearrange("(k p) (h j) -> p k h j", p=128, j=128))
    for k in range(3):
        ts('dve', wuq[:, k, :], stg_q[:, k, :], qng[:, k:k + 1], None, ALU.mult)
    for k in range(2):
        ts('dve', wK[:, k, :, :], stg_kv[:, k, :, 0:64], kvng[:, k:k + 1], None, ALU.mult)
        ts('dve', wV[:, k, :, :], stg_kv[:, k, :, 64:128], kvng[:, k:k + 1], None, ALU.mult)
    Z.release(mB)
    qlT = Z.alloc("qlT", [128, 3, NQ], BF16)
    wA = Z.alloc("wA", [128, 8, 672], BF16)
    wload(wA[:], w_in_v[:, :, 0:672])
    latn = [Z.alloc("latn%d" % i, [128, 672], BF16) for i in range(2)]
    ssq2 = Z.alloc("ssq2", [128, 34, 2], F32)
    sq2 = Z.alloc("sq2", [128, 34, 2], F32)
    rs2 = Z.alloc("rs2", [128, 34, 2], F32)
    rtmp = Z.alloc("rtmp", [128, 4, 16], F32)
    junk2 = Z.alloc("junk2", [128, 384], BF16)
    qtm = [Z.alloc("qtm%d" % i, [128, 8, 96], BF16) for i in range(2)]
    rq = Z.alloc("rq", [128, 4, 4, 16], F32)
    memset('dve', ssq2[:], 0.0)
    b1state = {}

    def b1_s12(i):
        own = i < 17
        col0 = i * 128 if i < 32 else NL + (i - 32) * 128
        ln_ = latn[i % 2]
        bq, bk = nbank(), nbank()
        if own:
            for k in range(8):
                mm(ps[:, bq, 0:QL], hTc(k, col0, 128), wA[:, k, 0:QL], start=(k == 0), stop=(k == 7))
        for k in range(8):
            mm(ps[:, bk, 0:288], hTc(k, col0, 128), wA[:, k, QL:672], start=(k == 0), stop=(k == 7))
        if own:
            act(junk2[:, 0:QL], ps[:, bq, 0:QL], AF.Square, accum_out=ssq2[:, i, 0:1])
            act(sq2[:, i, 0:1], ssq2[:, i, 0:1], AF.Sqrt, bias=epsv[:], scale=1.0 / QL)
            recip(rs2[:, i, 0:1], sq2[:, i, 0:1])
            act(ln_[:, 0:QL], ps[:, bq, 0:QL], AF.Identity, scale=rs2[:, i, 0:1])
        act(junk2[:, 0:KVL], ps[:, bk, 0:KVL], AF.Square, accum_out=ssq2[:, i, 1:2])
        act(sq2[:, i, 1:2], ssq2[:, i, 1:2], AF.Sqrt, bias=epsv[:], scale=1.0 / KVL)
        recip(rs2[:, i, 1:2], sq2[:, i, 1:2])
        ts('dve', ln_[:, QL:QL + KVL], ps[:, bk, 0:KVL], rs2[:, i, 1:2], None, ALU.mult)
        if i < 32:
            x1_, x2_ = ps[:, bk, 256:272], ps[:, bk, 272:288]
            cos_, sin_ = cs[:, i, 0:16], cs[:, i, 16:32]
            tt('dve', rtmp[:, 0, :], x1_, cos_, ALU.mult)
            tt('dve', rtmp[:, 1, :], x2_, sin_, ALU.mult)
            tt('dve', rtmp[:, 2, :], x2_, cos_, ALU.mult)
            tt('dve', rtmp[:, 3, :], x1_, sin_, ALU.mult)
            tt('dve', ln_[:, 640:656], rtmp[:, 0, :], rtmp[:, 1, :], ALU.subtract)
            tt('dve', ln_[:, 656:672], rtmp[:, 2, :], rtmp[:, 3, :], ALU.add)
        else:
            cp('dve', ln_[:, 640:672], ps[:, bk, 256:288])

    def b1_s34(i):
        own = i < 17
        col0 = i * 128 if i < 32 else NL + (i - 32) * 128
        ln_ = latn[i % 2]
        bt = nbank()
        if own:
            for k in range(3):
                tr(psb[:, bt, k * 128:(k + 1) * 128], ln_[:, k * 128:(k + 1) * 128], identb[:])
        for k in range(2):
            tr(psb[:, bt, (3 + k) * 128:(4 + k) * 128], ln_[:, QL + k * 128:QL + (k + 1) * 128], identb[:])
        tr(psb[0:96, bt, 640:768], ln_[:, 576:672], identb[:])
        if own:
            cp('act', qlT[:, :, col0:col0 + 128], psb[:, bt, 0:384].rearrange("p (k c) -> p k c", c=128))
        cp('act', kvT[:, :, col0:col0 + 128], psb[:, bt, 384:640].rearrange("p (k c) -> p k c", c=128))
        cp('act', krT[64:96, col0:col0 + 128], psb[64:96, bt, 640:768])

    for it in range(35):
        if it < 34:
            b1_s12(it)
        if it >= 1:
            b1_s34(it - 1)
    dump("qlT", qlT[:, :, 0:256], BF16)
    dump("kvT", kvT[:, :, 0:256], BF16)
    dump("krT", krT[64:96, 0:256], BF16)
    def q_s12(i):
        col0 = i * 128
        qt_ = qtm[i % 2]
        bb = [nbank(), nbank()]
        for hh in range(2):
            for k in range(3):
                mm(ps[:, bb[hh], 0:384], qlT[:, k, col0:col0 + 128], wuq[:, k, hh * 384:(hh + 1) * 384],
                   start=(k == 0), stop=(k == 2))
        for hh in range(2):
            src = ps[:, bb[hh], 0:384].rearrange("p (h d) -> p h d", d=96)
            x1_, x2_ = src[:, :, 64:80], src[:, :, 80:96]
            cos_ = cs[:, i:i + 1, 0:16].to_broadcast([128, 4, 16])
            sin_ = cs[:, i:i + 1, 16:32].to_broadcast([128, 4, 16])
            cp('dve', qt_[:, hh * 4:(hh + 1) * 4, 0:64], src[:, :, 0:64])
            tt('dve', rq[:, 0], x1_, cos_, ALU.mult)
            tt('dve', rq[:, 1], x2_, sin_, ALU.mult)
            tt('dve', rq[:, 2], x2_, cos_, ALU.mult)
            tt('dve', rq[:, 3], x1_, sin_, ALU.mult)
            tt('dve', qt_[:, hh * 4:(hh + 1) * 4, 64:80], rq[:, 0], rq[:, 1], ALU.subtract)
            tt('dve', qt_[:, hh * 4:(hh + 1) * 4, 80:96], rq[:, 2], rq[:, 3], ALU.add)

    def q_s34(i):
        col0 = i * 128
        qt_ = qtm[i % 2]
        bt = nbank()
        for h in range(8):
            tr(psb[0:96, bt, h * 128:(h + 1) * 128], qt_[:, h, :], identb[:])
        cp('act', qT[:, :, col0:col0 + 128], psb[0:96, bt, :].rearrange("p (h c) -> p h c", c=128))

    for it in range(18):
        if it < 17:
            q_s12(it)
        if it >= 1:
            q_s34(it - 1)
    dump("qT", qT[:, :, 0:256], BF16)
    Z.release(mB)

    if stop_after == 'B':
        return _finish()

    Z.cur = Z.lo
    OT = Z.alloc("OT", [128, 4, NQ], BF16)
    mT = Z.mark()
    Kt = Z.alloc("Kt", [96, 2, NA], BF16)
    Vp = Z.alloc("Vp", [128, 34, 3, 64], BF16)
    pT = [Z.alloc("pT%d" % i, [128, 2, 512], BF16) for i in range(3)]
    rd = Z.alloc("rd", [128, 512], F32)
    memset('pool', Vp[:, :, 1, :], 1.0)
    unit = 0
    for p in range(4):
        for hh in range(2):
            h = 2 * p + hh
            for (st, sz) in blocks(0, NA):
                b = 6 + (st // 512) % 2
                for k in range(2):
                    mm(ps[0:64, b, 0:sz], wK[:, k, h, :], kvT[:, k, st:st + sz], start=(k == 0), stop=(k == 1))
                cp('dve' if (st // 512) % 2 else 'act', Kt[0:64, hh, st:st + sz], ps[0:64, b, 0:sz])
            cp('dve', Kt[64:96, hh, :], krT[64:96, :])
        for g in range(9):
            t0 = g * 4
            nt = min(4, 34 - t0)
            b = 6 + g % 2
            for t in range(nt):
                c0 = (t0 + t) * 128
                for k in range(2):
                    mm(ps[:, b, t * 128:(t + 1) * 128], kvT[:, k, c0:c0 + 128],
                       wV[:, k, 2 * p:2 * p + 2, :].rearrange("p h d -> p (h d)"), start=(k == 0), stop=(k == 1))
            cp('dve', Vp[:, t0:t0 + nt, 0:3:2, :],
               ps[:, b, 0:nt * 128].rearrange("p (t h d) -> p t h d", h=2, d=64))
        if p == 0:
            dump("Kt0", Kt[:, :, 0:256], BF16)
            dump("Vp0", Vp[:, 0:2, :, :], BF16)
        for hh in range(2):
            h = 2 * p + hh
            vsel = slice(0, 2) if hh == 0 else slice(1, 3)
            for (qs, qz) in blocks(0, NQ):
                ob = 4 + unit % 2
                unit += 1
                pend = None
                for ktp in range(17):
                    sb0 = 2 * (ktp % 2)
                    for j in range(2):
                        kt = 2 * ktp + j
                        mm(ps[:, sb0 + j, 0:qz], Kt[0:96, hh, kt * 128:(kt + 1) * 128], qT[0:96, h, qs:qs + qz])
                    pt_ = pT[ktp % 3]
                    act(pt_[:, :, 0:qz], ps[:, sb0:sb0 + 2, 0:qz], AF.Exp, scale=SM_SCALE)
                    if pend is not None:
                        pend()

                    def pv(ktp=ktp, pt_=pt_):
                        for j in range(2):
                            kt = 2 * ktp + j
                            mm(ps[:, ob, 0:qz], Vp[:, kt, vsel, :].rearrange("p a d -> p (a d)"), pt_[:, j, 0:qz],
                               start=(kt == 0), stop=(kt == 33))
                    pend = pv
                pend()
                if hh == 0:
                    recip(rd[0:64, 0:qz], ps[64:128, ob, 0:qz])
                    tt('dve', OT[0:64, p, qs:qs + qz], ps[0:64, ob, 0:qz], rd[0:64, 0:qz], ALU.mult)
                else:
                    recip(rd[64:128, 0:qz], ps[0:64, ob, 0:qz])
                    tt('dve', OT[64:128, p, qs:qs + qz], ps[64:128, ob, 0:qz], rd[64:128, 0:qz], ALU.mult)
    dump("OT", OT[:, :, 0:256], BF16)
    Z.release(mT)
    Z.hi = 229312

    if stop_after == 'T':
        return _finish()

    mix = Z.alloc("mix", [128, 8, NQ], BF16)
    wout = Z.alloc("wout", [128, 8, D], BF16)
    mM = Z.mark()
    s_sb = Z.alloc("s_sb", [128, 10, NQ], BF16)
    wsl = [Z.alloc("wsl%d" % i, [128, 30, 128], BF16) for i in range(2)]
    tg = [Z.alloc("tg%d" % i, [128, 2, 512], F32) for i in range(2)]
    for n in range(10):
        dma('sp', s_sb[:, n, :], s_dram[n], dram_reads=[("s", n)])
    wload(wout[:], w_out.rearrange("(k p) n -> p k n", p=128))
    woa_v = w_o_attn.rearrange("(k p) n -> p k n", p=128)
    wol_v = w_o_lru.rearrange("(k p) n -> p k n", p=128)

    def merge_loads(m):
        w = wsl[m % 2]
        wload(w[:, 0:4, :], woa_v[:, :, m * 128:(m + 1) * 128])
        wload(w[:, 4:14, :], wol_v[:, :, m * 128:(m + 1) * 128])
        wload(w[:, 14:22, :], w_in_v[:, :, OFF_G + m * 128:OFF_G + (m + 1) * 128])
        wload(w[:, 22:30, :], w_in_v[:, :, OFF_G + D + m * 128:OFF_G + D + (m + 1) * 128])

    merge_loads(0)
    it = 0
    for m in range(8):
        if m + 1 < 8:
            merge_loads(m + 1)
        w = wsl[m % 2]
        for (st, sz) in blocks(0, NQ):
            b0 = 4 * (it % 2)
            tg_ = tg[it % 2]
            mt_ = tg_
            it += 1
            for k in range(4):
                mm(ps[:, b0, 0:sz], w[:, k, :], OT[:, k, st:st + sz], start=(k == 0), stop=(k == 3))
            for k in range(10):
                mm(ps[:, b0 + 1, 0:sz], w[:, 4 + k, :], s_sb[:, k, st:st + sz], start=(k == 0), stop=(k == 9))
            for k in range(8):
                mm(ps[:, b0 + 2, 0:sz], w[:, 14 + k, :], hTo[:, k, st:st + sz], start=(k == 0), stop=(k == 7))
            for k in range(8):
                mm(ps[:, b0 + 3, 0:sz], w[:, 22 + k, :], hTo[:, k, st:st + sz], start=(k == 0), stop=(k == 7))
            act(tg_[:, 0, 0:sz], ps[:, b0 + 2, 0:sz], AF.Tanh, bias=hbg[:, m:m + 1], scale=0.5)
            act(tg_[:, 1, 0:sz], ps[:, b0 + 3, 0:sz], AF.Tanh, bias=hbg[:, 8 + m:9 + m], scale=0.5)
            stt('dve', mt_[:, 0, 0:sz], tg_[:, 0, 0:sz], 1.0, ps[:, b0, 0:sz], ALU.add, ALU.mult)
            stt('dve', mt_[:, 1, 0:sz], tg_[:, 1, 0:sz], 1.0, ps[:, b0 + 1, 0:sz], ALU.add, ALU.mult)
            tt('dve', mix[:, m, st:st + sz], mt_[:, 0, 0:sz], mt_[:, 1, 0:sz], ALU.add)
    dump("mix", mix[:, :, 0:256], BF16)
    h2T = hTo
    Z.release(mM)
    xr = [Z.alloc("xr%d" % i, [128, D], F32) for i in range(3)]
    x1t = [Z.alloc("x1t%d" % i, [128, D], F32) for i in range(3)]
    xn2 = [Z.alloc("xn2%d" % i, [128, D], F32) for i in range(3)]
    junk3 = Z.alloc("junk3", [128, D], BF16)
    g1h = Z.alloc("g1h", [128, D], F32)
    dma('sp', g1h[:], gbc_dram[:, 0, :], dram_reads=[t for t in GB_TAGS if t[1] == 0])
    ssq = Z.alloc("ssq3", [128, 34], F32)
    sqv = Z.alloc("sqv3", [128, 34], F32)
    rsv = Z.alloc("rsv3", [128, 34], F32)
    junk = junk3
    memset('dve', ssq[:], 0.0)

    def m2_s1(i):
        xr_, x1_, xn_ = xr[i % 3], x1t[i % 3], xn2[i % 3]
        dma('sp', xr_[:], xs[i * 128:(i + 1) * 128, :])
        for half in range(2):
            b = 4 + 2 * (i % 2) + half
            hs = slice(half * 512, (half + 1) * 512)
            for k in range(8):
                mm(ps[:, b, :], mix[:, k, i * 128:(i + 1) * 128], wout[:, k, hs], start=(k == 0), stop=(k == 7))
            tt('dve', x1_[:, hs], ps[:, b, :], g1h[:, hs], ALU.mult)
            tt('dve', x1_[:, hs], x1_[:, hs], xr_[:, hs], ALU.add)
        if i < 16:
            dma('sp', x1_dram[i * 128:(i + 1) * 128, :], x1_[:], dram_writes=[("x1", i)])
        if i == 0:
            dump("x1_0", x1_[:])
        norm_p1(x1_, xn_, i)

    def m2_s2(i):
        norm_p2(xn2[i % 3], lambda k, i=i: h2T[:, k, i * 128:(i + 1) * 128],
                lambda k: A2[:, k:k + 1], lambda k: modfm[:, 2, k, 0:1], 2 * (i % 2))

    for it in range(18):
        if it < 17:
            m2_s1(it)
        if it >= 1:
            m2_s2(it - 1)
    dump("h2T", h2T[:, :, 0:256], BF16)
    Z.cur = Z.lo
    if stop_after == 'M':
        return _finish()

    actb = Z.alloc("actb", [128, NFC, NO], BF16)
    mF = Z.mark()
    wup = [Z.alloc("wup%d" % i, [128, 2, 8, 128], BF16) for i in range(2)]
    abuf = Z.alloc("abuf", [128, NQ + 2], F32)
    acv = Z.alloc("acv", [128, NO], F32)
    sgv = Z.alloc("sgv", [128, NO], F32)
    memset('pool', abuf[:, 0:1], 0.0)
    w_up_v = w_up.rearrange("(k p) n -> p k n", p=128)

    def ffn_loads(c):
        wload(wup[c % 2][:, 0], w_up_v[:, :, c * 128:(c + 1) * 128])
        wload(wup[c % 2][:, 1], w_up_v[:, :, FFN + c * 128:FFN + (c + 1) * 128])

    ffn_loads(0)
    for c in range(NFC):
        if c + 1 < NFC:
            ffn_loads(c + 1)
        wu = wup[c % 2]
        for (st, sz) in blocks(0, NQ):
            b = nbank()
            for k in range(8):
                mm(ps[:, b, 0:sz], wu[:, 0, k, :], h2T[:, k, st:st + sz], start=(k == 0), stop=(k == 7))
            cp('act', abuf[:, 1 + st:1 + st + sz], ps[:, b, 0:sz])
        ts('dve', acv[:], abuf[:, 0:NO], fcw[:, c, 0:1], fcb[:, c:c + 1], ALU.mult, ALU.add)
        stt('dve', acv[:], abuf[:, 1:NO + 1], fcw[:, c, 1:2], acv[:], ALU.mult, ALU.add)
        stt('dve', acv[:], abuf[:, 2:NO + 2], fcw[:, c, 2:3], acv[:], ALU.mult, ALU.add)
        act(sgv[:], acv[:], AF.Tanh, scale=0.5)
        stt('dve', sgv[:], sgv[:], 1.0, acv[:], ALU.add, ALU.mult)
        for (st, sz) in blocks(0, NO):
            b = nbank()
            for k in range(8):
                mm(ps[:, b, 0:sz], wu[:, 1, k, :], h2T[:, k, st:st + sz], start=(k == 0), stop=(k == 7))
            stt('dve', actb[:, c, st:st + sz], sgv[:, st:st + sz], 0.5, ps[:, b, 0:sz], ALU.mult, ALU.mult)
        if c == 0:
            dump("act0", actb[:, 0, 0:256], BF16)
    Z.release(mF)
    wdn = Z.alloc("wdn", [128, NFC, D], BF16)
    wload(wdn[:], w_down.rearrange("(k p) n -> p k n", p=128))
    AH.cur = AH.lo
    x1l = [AH.alloc("x1l%d" % i, [128, D], F32) for i in range(2)]
    x2t = [AH.alloc("x2t%d" % i, [128, D], F32) for i in range(2)]
    junk4 = AH.alloc("junk4", [128, D], BF16)
    g2b = Z.alloc("g2b", [128, D], F32)
    fing = Z.alloc("fing", [128, D], F32)
    dma('sp', g2b[:], gbc_dram[:, 1, :], dram_reads=[t for t in GB_TAGS if t[1] == 1])
    dma('sp', fing[:], d_fing)
    ssq4 = Z.alloc("ssq4", [128, 16], F32)
    sq4 = Z.alloc("sq4", [128, 16], F32)
    rs4 = Z.alloc("rs4", [128, 16], F32)
    memset('dve', ssq4[:], 0.0)
    for i in range(16):
        xl, x2 = x1l[i % 2], x2t[i % 2]
        dma('sp', xl[:], x1_dram[i * 128:(i + 1) * 128, :], dram_reads=[("x1", i)])
        for half in range(2):
            b = 2 * (i % 2) + half
            for c in range(NFC):
                mm(ps[:, b, :], actb[:, c, i * 128:(i + 1) * 128], wdn[:, c, half * 512:(half + 1) * 512],
                   start=(c == 0), stop=(c == NFC - 1))
            tt('dve', x2[:, half * 512:(half + 1) * 512], ps[:, b, :], g2b[:, half * 512:(half + 1) * 512], ALU.mult)
            tt('pool', x2[:, half * 512:(half + 1) * 512], x2[:, half * 512:(half + 1) * 512],
               xl[:, half * 512:(half + 1) * 512], ALU.add)
        act(junk4[:], x2[:], AF.Square, accum_out=ssq4[:, i:i + 1])
        act(sq4[:, i:i + 1], ssq4[:, i:i + 1], AF.Sqrt, bias=epsv[:], scale=1.0 / D)
        recip(rs4[:, i:i + 1], sq4[:, i:i + 1])
        stt('dve', x2[:], x2[:], rs4[:, i:i + 1], fing[:], ALU.mult, ALU.mult)
        dma('sp', out[i * 128:(i + 1) * 128, :], x2[:])
    return _finish()


def _fm(v):
    v = np.asarray(v, np.float32)
    n = v.shape[-1] // 128
    return np.ascontiguousarray(v.reshape(n, 128).T)


def _rope_tables(pos):
    inv = (1.0 / (np.float32(10000.0) ** (np.arange(0, 16, 2, dtype=np.float32) / np.float32(16)))).astype(np.float32)
    row = (pos // 64).astype(np.float32)
    colp = (pos % 64).astype(np.float32)
    ang = np.concatenate([row[:, None] * inv[None, :], colp[:, None] * inv[None, :]], axis=-1).astype(np.float32)
    return np.cos(ang).astype(np.float32), np.sin(ang).astype(np.float32)


def make_in_maps(x, c, ctx, c_ctx, w_mod, b_mod, norm1_g, w_in, b_gate, q_norm_g, kv_norm_g,
                 w_uq, w_ukv, w_o_attn, lru_conv_w, lru_conv_b, lru_w_a, lru_b_a, lru_w_x,
                 lru_b_x, lru_lambda, w_o_lru, w_out, norm2_g, w_up, ffn_conv_w, ffn_conv_b,
                 w_down, final_g):
    import ml_dtypes
    f = lambda a: np.ascontiguousarray(np.asarray(a, np.float32))
    x, c, ctx, c_ctx = f(x), f(c), f(ctx), f(c_ctx)
    shared = {
        "w_mod": f(w_mod[0]), "w_in": f(w_in[0]), "w_uq": f(w_uq[0]), "w_ukv": f(w_ukv[0]),
        "w_o_attn": f(w_o_attn[0]), "w_o_lru": f(w_o_lru[0]), "w_out": f(w_out[0]), "w_up": f(w_up[0]),
        "w_down": f(w_down[0]),
        "bmodT": _fm(f(b_mod[0])),
        "bmodg": np.ascontiguousarray(np.broadcast_to(
            np.concatenate([f(b_mod[0])[2 * D:3 * D], f(b_mod[0])[5 * D:6 * D]])[None, :], (128, 2048))),
        "n1g": _fm(norm1_g[0]), "n2g": _fm(norm2_g[0]),
        "fing": np.ascontiguousarray(np.broadcast_to(f(final_g)[None, :], (128, D))),
        "bgate": _fm(b_gate[0]), "qng": _fm(q_norm_g[0]), "kvng": _fm(kv_norm_g[0]),
        "convb": _fm(lru_conv_b[0]), "fcb": _fm(ffn_conv_b[0]),
        "identf": np.eye(128, dtype=np.float32),
        "identb": np.eye(128, dtype=np.float32).astype(ml_dtypes.bfloat16),
    }
    lcw = f(lru_conv_w[0])
    fcw_ = f(ffn_conv_w[0])
    in_maps = []
    for core in range(8):
        b, half = core // 2, core % 2
        m = dict(shared)
        xb_ = x[b]
        cx = ctx[b]
        if half == 1:
            xb_ = xb_[::-1]
            cx = cx[::-1]
        m["xs"] = np.ascontiguousarray(xb_)
        m["ctxs"] = np.ascontiguousarray(cx)
        cT = np.stack([_fm(c[b]), _fm(c_ctx)], axis=-1)
        m["cT"] = np.ascontiguousarray(cT)
        dirs = [0, 1] if half == 0 else [1, 0]
        m["lru_wa"] = np.ascontiguousarray(f(lru_w_a[0])[dirs])
        m["lru_wx"] = np.ascontiguousarray(f(lru_w_x[0])[dirs])
        m["lba"] = np.ascontiguousarray(np.stack([_fm(f(lru_b_a[0])[d]) for d in dirs], axis=1))
        m["lbx"] = np.ascontiguousarray(np.stack([_fm(f(lru_b_x[0])[d]) for d in dirs], axis=1))
        m["lam"] = np.ascontiguousarray(np.stack([_fm(f(lru_lambda[0])[d]) for d in dirs], axis=1))
        w5 = np.zeros((5, LW), np.float32)
        if half == 0:
            w5[0:4] = lcw
        else:
            w5[1:5] = lcw[::-1]
        m["conv5"] = np.ascontiguousarray(np.stack([_fm(w5[j]) for j in range(5)], axis=-1))
        w3 = fcw_ if half == 0 else fcw_[::-1]
        m["fcw"] = np.ascontiguousarray(np.stack([_fm(w3[j]) for j in range(3)], axis=-1))
        pos = np.arange(NL)
        if half == 1:
            pos = NL - 1 - pos
        cos, sin = _rope_tables(pos)
        cs = np.concatenate([cos, sin], axis=-1).reshape(32, 128, 32).transpose(1, 0, 2)
        m["cs"] = np.ascontiguousarray(cs)
        in_maps.append(m)
    return in_maps


_CACHE = {}


def kernel(**inputs):
    in_maps = make_in_maps(**inputs)
    if "nc" not in _CACHE:
        from contextlib import ExitStack
        nc, P, A, _ = build_program(False)
        es = ExitStack()
        P.finalize(es)
        _CACHE["nc"] = nc
        _CACHE["es"] = es
    nc = _CACHE["nc"]
    res = run_bass_kernel_spmd(nc, in_maps, core_ids=list(range(8)))
    B = 4
    outp = np.zeros((B, NL, D), np.float32)
    for core in range(8):
        b, half = core // 2, core % 2
        o = np.asarray(res.results[core]["out"], np.float32)
        if half == 0:
            outp[b, 0:NO] = o
        else:
            outp[b, NO:NL] = o[::-1]
    return outp
```

```python
import numpy as np
from collections import defaultdict
import concourse.bass as bass
import concourse.mybir as mybir
from concourse.bass_utils import run_bass_kernel_spmd

F32 = mybir.dt.float32
BF16 = mybir.dt.bfloat16
AF = mybir.ActivationFunctionType
ALU = mybir.AluOpType

D = 1024
NL = 4096
NCX = 256
NA = NL + NCX
NQ = 2176
NO = 2048
QL, KVL, KR = 384, 256, 32
OFF_KV = QL
OFF_KR = OFF_KV + KVL
OFF_XB = OFF_KR + KR
LW = 1280
OFF_YB = OFF_XB + LW
OFF_G = OFF_YB + LW
IN_DIM = OFF_G + 2 * D
FFN = 2816
NFC = FFN // 128
EPS = 1e-6
SM_SCALE = 96 ** -0.5
GELU_C = 0.7978845608028654

OPTS = {}
ENGS = ['pe', 'act', 'dve', 'pool', 'sp']
BLOCK_ATTR = {'pe': 'tensor', 'act': 'scalar', 'dve': 'vector', 'pool': 'gpsimd', 'sp': 'sync'}
SAME_ENG_WINDOW = 1 << 30
NDMASEM = 8
FUSE_WAIT = True
BIN = 11


def dsize(dt):
    return mybir.dt.size(dt)


class Op:
    __slots__ = ('eng', 'idx', 'fn', 'waits', 'signal', 'dma', 'dsem', 'dcnt', 'snap', 'sigcount', 'dprev')


class Prog:
    def __init__(self, nc):
        self.nc = nc
        self.ops = {e: [] for e in ENGS}
        self.base = {}
        self.bins = {'sb': defaultdict(list), 'ps': defaultdict(list)}
        self.seen = {e: {e2: -1 for e2 in ENGS} for e in ENGS}
        self.seen_dma = {e: set() for e in ENGS}
        self.ndma = {e: 0 for e in ENGS}
        self.dram_w = {}
        self.dram_r = defaultdict(list)

    def region(self, ap):
        t = ap.tensor
        name = t.name
        if name not in self.base:
            return None
        space, base = self.base[name]
        pat = list(ap.ap)
        es = dsize(t.dtype)
        row = pat[0][0]
        off = ap.offset
        if row == 0:
            row = 1 << 40
        p0 = off // row
        f0 = off % row
        p1 = p0 + pat[0][1]
        lo = f0
        hi = f0
        for st, cnt in pat[1:]:
            ext = st * (cnt - 1)
            if ext < 0:
                lo += ext
            else:
                hi += ext
        b0, b1 = base + lo * es, base + (hi + 1) * es
        if space == 'ps':
            b0 = (b0 // 2048) * 2048
            b1 = ((b1 + 2047) // 2048) * 2048
            p0 = (p0 // 32) * 32
            p1 = ((p1 + 31) // 32) * 32
        return (space, p0, p1, b0, b1)

    def _conflicts(self, reg, is_write, out):
        space, p0, p1, b0, b1 = reg
        bins = self.bins[space]
        for bn in range(b0 >> BIN, ((b1 - 1) >> BIN) + 1):
            lst = bins.get(bn)
            if not lst:
                continue
            for rec in lst:
                if rec[1] < p1 and p0 < rec[2] and rec[3] < b1 and b0 < rec[4]:
                    if is_write or rec[0]:
                        out.add(rec[5])

    def _register(self, reg, is_write, opref):
        space, p0, p1, b0, b1 = reg
        bins = self.bins[space]
        rec = (is_write, p0, p1, b0, b1, opref)
        for bn in range(b0 >> BIN, ((b1 - 1) >> BIN) + 1):
            lst = bins[bn]
            if is_write:
                lst[:] = [r for r in lst if not (p0 <= r[1] and r[2] <= p1 and b0 <= r[3] and r[4] <= b1)]
            elif not opref[2]:
                lst[:] = [r for r in lst if not ((not r[0]) and r[1] == p0 and r[2] == p1 and r[3] == b0
                                                 and r[4] == b1 and r[5][0] == opref[0] and not r[5][2])]
            lst.append(rec)

    def emit(self, eng, fn, reads=(), writes=(), dma=False, dram_reads=(), dram_writes=()):
        deps = set()
        rr = [r for r in (self.region(a) for a in reads) if r is not None]
        ww = [r for r in (self.region(a) for a in writes) if r is not None]
        ww = ww + [r for r in rr if r[0] == 'ps']
        rr = [r for r in rr if r[0] != 'ps']
        for r in rr:
            self._conflicts(r, False, deps)
        for r in ww:
            self._conflicts(r, True, deps)
        for tag in dram_reads:
            if tag in self.dram_w:
                deps.add(self.dram_w[tag])
        for tag in dram_writes:
            if tag in self.dram_w:
                deps.add(self.dram_w[tag])
            deps.update(self.dram_r.get(tag, ()))
        op = Op()
        op.eng = eng
        op.idx = len(self.ops[eng])
        op.fn = fn
        op.signal = False
        op.dma = dma
        op.dsem = None
        op.dcnt = 0
        op.sigcount = 0
        op.dprev = None
        seen = self.seen[eng]
        sdma = self.seen_dma[eng]
        waits = []
        if dma:
            k = self.ndma[eng]
            self.ndma[eng] += 1
            op.dsem = (eng, k % NDMASEM)
            op.dcnt = 16 * (k // NDMASEM + 1)
            if k >= NDMASEM:
                prev = self._dma_ops[eng][k - NDMASEM]
                deps.add((eng, prev.idx, True))
            self._dma_ops.setdefault(eng, []).append(op)
        best = {}
        dl = []
        for (e2, i2, d2) in deps:
            if d2:
                dl.append((e2, i2, d2))
            elif i2 > best.get(e2, -1):
                best[e2] = i2
        dl.sort()
        dl += [(e2, i2, False) for e2, i2 in sorted(best.items())]
        for (e2, i2, d2) in dl:
            src = self.ops[e2][i2]
            if d2:
                if (e2, i2) in sdma:
                    continue
                sdma.add((e2, i2))
                waits.append((e2, i2))
            else:
                if e2 == eng:
                    if eng == 'pe' or op.idx - i2 > SAME_ENG_WINDOW:
                        continue
                if i2 <= seen[e2]:
                    continue
                seen[e2] = i2
                src.signal = True
                waits.append((e2, i2))
            for e3, v in src.snap.items():
                if v > seen[e3]:
                    seen[e3] = v
        op.waits = waits
        op.snap = dict(seen)
        self.ops[eng].append(op)
        ref = (eng, op.idx, dma)
        for r in rr:
            self._register(r, False, ref)
        for r in ww:
            self._register(r, True, ref)
        for tag in dram_reads:
            self.dram_r[tag].append(ref)
        for tag in dram_writes:
            self.dram_w[tag] = ref
            self.dram_r[tag] = []
        return op

    _dma_ops = None

    def finalize(self, es):
        nc = self.nc
        self.sem = {e: es.enter_context(nc.semaphore("s_" + e)) for e in ENGS}
        self.dsems = {}
        for e in ENGS:
            if self.ndma[e]:
                for j in range(min(NDMASEM, self.ndma[e])):
                    self.dsems[(e, j)] = es.enter_context(nc.semaphore("d_%s%d" % (e, j)))
        for e in ENGS:
            c = 0
            for op in self.ops[e]:
                if op.signal and not op.dma:
                    c += 1
                    op.sigcount = c
        block = es.enter_context(nc.Block())
        for e in ENGS:
            self._replay_engine(block, e)

    def _replay_engine(self, block, e):
        ops = self.ops[e]
        allops = self.ops
        sem = self.sem
        dsems = self.dsems

        def body(eng):
            for op in ops:
                wl = []
                for (e2, i2) in op.waits:
                    src = allops[e2][i2]
                    if src.dma:
                        wl.append((dsems[src.dsem], src.dcnt))
                    else:
                        wl.append((sem[e2], src.sigcount))
                fuse = wl.pop() if (wl and FUSE_WAIT and not op.dma) else None
                for (sm, v) in wl:
                    eng.wait_ge(sm, v)
                ins = op.fn(eng)
                if fuse is not None:
                    ins._wait_ge(fuse[0], fuse[1])
                if op.dma:
                    ins.then_inc(dsems[op.dsem], 16)
                elif op.signal:
                    ins.then_inc(sem[e], 1)
        getattr(block, BLOCK_ATTR[e])(body)


class Arena:
    def __init__(self, nc, prog, lo=20480, hi=229376):
        self.nc, self.prog, self.lo, self.hi, self.cur = nc, prog, lo, hi, lo
        self.n = 0
        self.peak = lo

    def alloc(self, name, shape, dt):
        per = int(np.prod(shape[1:])) * dsize(dt)
        per = (per + 63) // 64 * 64
        assert self.cur + per <= self.hi, "SBUF arena overflow at %s: need %d have %d" % (name, per, self.hi - self.cur)
        self.n += 1
        t = self.nc.alloc_sbuf_tensor_at("%s_%d_%d" % (name, self.lo, self.n), list(shape), dt, offset=self.cur)
        self.prog.base[t.name] = ('sb', self.cur)
        self.cur += per
        self.peak = max(self.peak, self.cur)
        return t

    def alloc_top(self, name, shape, dt):
        per = int(np.prod(shape[1:])) * dsize(dt)
        per = (per + 63) // 64 * 64
        assert self.hi - per >= self.cur, "SBUF arena overflow (top) at %s" % name
        self.hi -= per
        self.n += 1
        t = self.nc.alloc_sbuf_tensor_at("%s_t%d" % (name, self.n), list(shape), dt, offset=self.hi)
        self.prog.base[t.name] = ('sb', self.hi)
        return t

    def mark(self):
        return (self.cur, self.hi)

    def release(self, m):
        self.cur, self.hi = m


def blocks(lo, hi, step=512):
    out = []
    s = lo
    while s < hi:
        out.append((s, min(step, hi - s)))
        s += step
    return out


def build_program(debug=False, stop_after=None):
    nc = bass.Bass("TRN2", target_bir_lowering=False)
    P = Prog(nc)
    P._dma_ops = {}
    LO0 = 20480
    PERS = 5 * 1024
    HTO = 8 * NQ * 2
    A = Arena(nc, P, LO0, LO0 + PERS)
    AH = Arena(nc, P, LO0 + PERS, LO0 + PERS + HTO)
    Z = Arena(nc, P, LO0 + PERS + HTO, 229312)
    dbg_outs = {}

    def din(name, shape, dt=F32):
        return nc.dram_tensor(name, list(shape), dt, kind="ExternalInput").ap()

    xs = din("xs", [NL, D])
    ctxs = din("ctxs", [NCX, D])
    d_cT = din("cT", [128, 8, 2])
    d_bmodT = din("bmodT", [128, 48])
    d_bmodg = din("bmodg", [128, 2048])
    d_n1g = din("n1g", [128, 8])
    d_n2g = din("n2g", [128, 8])
    d_fing = din("fing", [128, D])
    d_bgate = din("bgate", [128, 16])
    d_qng = din("qng", [128, 3])
    d_kvng = din("kvng", [128, 2])
    d_conv5 = din("conv5", [128, 10, 5])
    d_convb = din("convb", [128, 10])
    d_lba = din("lba", [128, 2, 10])
    d_lbx = din("lbx", [128, 2, 10])
    d_lam = din("lam", [128, 2, 10])
    d_fcw = din("fcw", [128, NFC, 3])
    d_fcb = din("fcb", [128, NFC])
    d_cs = din("cs", [128, 32, 32])
    d_identf = din("identf", [128, 128])
    d_identb = din("identb", [128, 128], BF16)
    w_mod = din("w_mod", [D, 6 * D])
    w_in = din("w_in", [D, IN_DIM])
    w_uq = din("w_uq", [QL, 768])
    w_ukv = din("w_ukv", [KVL, 1024])
    w_o_attn = din("w_o_attn", [512, D])
    w_o_lru = din("w_o_lru", [LW, D])
    w_out = din("w_out", [D, D])
    w_up = din("w_up", [D, 2 * FFN])
    w_down = din("w_down", [FFN, D])
    lru_wa = din("lru_wa", [2, 10, 128, 128])
    lru_wx = din("lru_wx", [2, 10, 128, 128])
    out = nc.dram_tensor("out", [NO, D], F32, kind="ExternalOutput").ap()
    s_dram = nc.dram_tensor("s_scr", [10, 128, NQ], BF16, kind="Internal").ap()
    gbc_dram = nc.dram_tensor("gbc_scr", [128, 2, D], F32, kind="Internal").ap()
    x1_dram = nc.dram_tensor("x1_scr", [NO, D], F32, kind="Internal").ap()

    ps = nc.alloc_psum_tensor("ps", [128, 8, 512], F32)
    P.base[ps.name] = ('ps', 0)
    psb = ps[:, :, :].bitcast(BF16)
    P.base[psb.tensor.name] = ('ps', 0)

    def dma(q, out_, in_, dram_reads=(), dram_writes=()):
        return P.emit(q, lambda e: e.dma_start(out=out_, in_=in_), reads=[in_], writes=[out_], dma=True,
                      dram_reads=dram_reads, dram_writes=dram_writes)

    def mm(out_, lhsT, rhs, start=True, stop=True):
        return P.emit('pe', lambda e: e.matmul(out_, lhsT=lhsT, rhs=rhs, start=start, stop=stop),
                      reads=[lhsT, rhs], writes=[out_])

    def tr(out_, in_, ident):
        return P.emit('pe', lambda e: e.transpose(out=out_, in_=in_, identity=ident), reads=[in_, ident], writes=[out_])

    def act(out_, in_, func, bias=0.0, scale=1.0, accum_out=None, eng='act'):
        rd = [in_]
        if not isinstance(bias, float):
            rd.append(bias)
        if not isinstance(scale, float):
            rd.append(scale)
        wr = [out_]
        kw = {}
        if accum_out is not None:
            wr.append(accum_out)
            kw['accum_out'] = accum_out
        return P.emit('act', lambda e: e.activation(out=out_, in_=in_, func=func, bias=bias, scale=scale, **kw),
                      reads=rd, writes=wr)

    def tt(eng, out_, in0, in1, op):
        return P.emit(eng, lambda e: e.tensor_tensor(out=out_, in0=in0, in1=in1, op=op), reads=[in0, in1], writes=[out_])

    def ts(eng, out_, in0, s1, s2, op0, op1=None):
        rd = [in0]
        if not isinstance(s1, float):
            rd.append(s1)
        if s2 is not None and not isinstance(s2, float):
            rd.append(s2)
        if op1 is None:
            return P.emit(eng, lambda e: e.tensor_scalar(out=out_, in0=in0, scalar1=s1, scalar2=None, op0=op0),
                          reads=rd, writes=[out_])
        return P.emit(eng, lambda e: e.tensor_scalar(out=out_, in0=in0, scalar1=s1, scalar2=s2, op0=op0, op1=op1),
                      reads=rd, writes=[out_])

    def stt(eng, out_, in0, scalar, in1, op0, op1):
        rd = [in0, in1]
        if not isinstance(scalar, float):
            rd.append(scalar)
        return P.emit(eng, lambda e: e.scalar_tensor_tensor(out=out_, in0=in0, scalar=scalar, in1=in1, op0=op0, op1=op1),
                      reads=rd, writes=[out_])

    def cp(eng, out_, in_):
        if eng == 'act':
            return act(out_, in_, AF.Identity)
        return P.emit(eng, lambda e: e.tensor_copy(out=out_, in_=in_), reads=[in_], writes=[out_])

    def memset(eng, ap, val):
        return P.emit(eng, lambda e: e.memset(ap, val), writes=[ap])

    def recip(out_, in_):
        return P.emit('dve', lambda e: e.reciprocal(out=out_, in_=in_), reads=[in_], writes=[out_])

    def scan(out_, d0, d1, initial):
        rd = [d0, d1]
        if not isinstance(initial, float):
            rd.append(initial)
        return P.emit('dve', lambda e: e.tensor_tensor_scan(out=out_, data0=d0, data1=d1, initial=initial,
                                                            op0=ALU.mult, op1=ALU.add), reads=rd, writes=[out_])

    def dump(name, ap, dt=F32):
        if not debug:
            return
        shape = list(ap.shape)
        t = nc.dram_tensor("dbg_" + name, shape, dt, kind="ExternalOutput").ap()
        dbg_outs[name] = t
        dma('sp', t, ap)

    def wload(dst, src):
        return dma('pool', dst, src)

    def _finish():
        P.emit('sp', lambda e: e.nop(), reads=[], writes=[])
        last = P.ops['sp'][-1]
        for o in P._dma_ops.get('sp', [])[-NDMASEM:]:
            if (o.eng, o.idx) not in last.waits:
                last.waits.append((o.eng, o.idx))
        return nc, P, A, dbg_outs

    cT = A.alloc("cT", [128, 8, 2], F32)
    bmodT = A.alloc("bmodT", [128, 48], F32)
    n1g = A.alloc("n1g", [128, 8], F32)
    n2g = A.alloc("n2g", [128, 8], F32)
    bgate = A.alloc("bgate", [128, 16], F32)
    qng = A.alloc("qng", [128, 3], F32)
    kvng = A.alloc("kvng", [128, 2], F32)
    conv5 = A.alloc("conv5", [128, 10, 5], F32)
    convb = A.alloc("convb", [128, 10], F32)
    lba = A.alloc("lba", [128, 2, 10], F32)
    lbx = A.alloc("lbx", [128, 2, 10], F32)
    lam = A.alloc("lam", [128, 2, 10], F32)
    fcw = A.alloc("fcw", [128, NFC, 3], F32)
    fcb = A.alloc("fcb", [128, NFC], F32)
    identf = A.alloc("identf", [128, 128], F32)
    identb = A.alloc("identb", [128, 128], BF16)
    for dst, src in ((cT, d_cT), (bmodT, d_bmodT), (n1g, d_n1g), (n2g, d_n2g), (bgate, d_bgate),
                     (qng, d_qng), (kvng, d_kvng), (conv5, d_conv5), (convb, d_convb), (lba, d_lba), (lbx, d_lbx),
                     (lam, d_lam), (fcw, d_fcw), (fcb, d_fcb), (identf, d_identf), (identb, d_identb)):
        dma('sp', dst[:], src)

    epsv = A.alloc("epsv", [128, 1], F32)
    memset('dve', epsv[:], EPS)
    qtr = A.alloc("qtr", [128, 1], F32)
    memset('dve', qtr[:], 0.25)
    modfm = A.alloc("modfm", [128, 4, 8, 2], F32)
    A1 = A.alloc("A1", [128, 8, 2], F32)
    A2 = A.alloc("A2", [128, 8], F32)
    hbg = A.alloc("hbg", [128, 16], F32)
    hba = A.alloc("hba", [128, 2, 10], F32)
    hbx = A.alloc("hbx", [128, 2, 10], F32)
    c1 = A.alloc("c1", [128, 2, 10], F32)
    hc = A.alloc("hc", [128, 2, 10], F32)
    hTo = AH.alloc("hTo", [128, 8, NQ], BF16)
    hTx = Z.alloc("hTx", [128, 8, NA - NQ], BF16)

    def hTc(k, st, sz):
        if st + sz <= NQ:
            return hTo[:, k, st:st + sz]
        assert st >= NQ
        return hTx[:, k, st - NQ:st - NQ + sz]

    ALLBLK = blocks(0, NQ) + blocks(NQ, NL) + blocks(NL, NA)

    sc = Z.alloc("sc", [128, 8, 2], F32)
    sc_rep = Z.alloc("sc_rep", [128, 8, 128], F32)
    m0 = Z.mark()
    th0 = Z.alloc("th0", [128, 8, 2], F32)
    wm = [Z.alloc("wm%d" % i, [128, 8, 1024], F32) for i in range(2)]
    act(th0[:], cT[:], AF.Tanh, scale=0.5)
    ts('dve', th0[:], th0[:], 0.5, 0.5, ALU.mult, ALU.add)
    tt('dve', sc[:], th0[:], cT[:], ALU.mult)
    for k in range(8):
        cp('dve', sc_rep[:, k, :], sc[:, k, 0:1].to_broadcast([128, 128]))
    act(c1[:], lam[:], AF.Exp, scale=-1.0)
    act(c1[:], c1[:], AF.Ln, bias=1.0)
    ts('dve', hc[:], c1[:], -4.0, None, ALU.mult)
    ts('dve', c1[:], c1[:], -8.0, None, ALU.mult)
    ts('dve', hba[:], lba[:], 0.5, None, ALU.mult)
    ts('dve', hbx[:], lbx[:], 0.5, None, ALU.mult)
    ts('dve', hbg[:], bgate[:], 0.5, None, ALU.mult)
    w_mod_v = w_mod.rearrange("(k p) n -> p k n", p=128)
    fm_idx = {0: 0, 1: 1, 3: 2, 4: 3}

    def mod_slab(s):
        wb = wm[s % 2]
        dma('pool', wb[:], w_mod_v[:, :, s * 1024:(s + 1) * 1024])
        if s in fm_idx:
            bank = 4 + s % 2
            for j in range(8):
                for k in range(8):
                    mm(ps[:, bank, 2 * j:2 * j + 2], wb[:, k, j * 128:(j + 1) * 128], sc[:, k, :], start=(k == 0), stop=(k == 7))
            for col in range(2):
                tt('dve', modfm[:, fm_idx[s], :, col], ps[:, bank, col:16:2], bmodT[:, s * 8:(s + 1) * 8], ALU.add)
        else:
            gi = 0 if s == 2 else 1
            for half in range(2):
                bank = 6 + half
                for k in range(8):
                    mm(ps[:, bank, :], sc_rep[:, k, :], wb[:, k, half * 512:(half + 1) * 512], start=(k == 0), stop=(k == 7))
                tt('dve', gbc[:, gi, half * 512:(half + 1) * 512], ps[:, bank, :],
                   bmodg[:, gi * 1024 + half * 512: gi * 1024 + (half + 1) * 512], ALU.add)

    mod_slab(0)
    mod_slab(1)
    for col in range(2):
        stt('dve', A1[:, :, col], modfm[:, 1, :, col], 1.0, n1g[:], ALU.add, ALU.mult)

    xt = [Z.alloc("xt%d" % i, [128, D], F32) for i in range(3)]
    junk = Z.alloc("junk", [128, D], BF16)
    ssq = Z.alloc("ssq", [128, 34], F32)
    sqv = Z.alloc("sqv", [128, 34], F32)
    rsv = Z.alloc("rsv", [128, 34], F32)
    memset('dve', ssq[:], 0.0)

    def norm_p1(buf, bufo, statc):
        act(junk[:], buf[:], AF.Square, accum_out=ssq[:, statc:statc + 1])
        act(sqv[:, statc:statc + 1], ssq[:, statc:statc + 1], AF.Sqrt, bias=epsv[:], scale=1.0 / D)
        recip(rsv[:, statc:statc + 1], sqv[:, statc:statc + 1])
        ts('dve', bufo[:], buf[:], rsv[:, statc:statc + 1], None, ALU.mult)

    def norm_p2(bufo, dstf, Ascale, Abias, bank0):
        for k in range(8):
            tr(ps[:, bank0 + k // 4, (k % 4) * 128:(k % 4 + 1) * 128], bufo[:, k * 128:(k + 1) * 128], identf[:])
        for k in range(8):
            src = ps[:, bank0 + k // 4, (k % 4) * 128:(k % 4 + 1) * 128]
            if k < 4:
                act(dstf(k), src, AF.Identity, bias=Abias(k), scale=Ascale(k))
            else:
                ts('dve', dstf(k), src, Ascale(k), Abias(k), ALU.mult, ALU.add)

    def norm_to_T(buf, dstf, statc, Ascale, Abias, bank0):
        norm_p1(buf, buf, statc)
        norm_p2(buf, dstf, Ascale, Abias, bank0)

    for it in range(35):
        if it < 34:
            i = it
            buf = xt[i % 3]
            src = xs[i * 128:(i + 1) * 128, :] if i < 32 else ctxs[(i - 32) * 128:(i - 31) * 128, :]
            dma('sp', buf[:], src)
            norm_p1(buf, buf, i)
        if it >= 1:
            i = it - 1
            col = 0 if i < 32 else 1
            norm_p2(xt[i % 3], lambda k, i=i: hTc(k, i * 128, 128),
                    lambda k, col=col: A1[:, k, col:col + 1], lambda k, col=col: modfm[:, 0, k, col:col + 1],
                    2 * (i % 2))
    dump("hT", hTo[:, :, 0:256], BF16)
    dump("hTc", hTx[:, :, NL - NQ:NA - NQ], BF16)
    Z.release(m0)

    if stop_after in ('0', 'A'):
        return _finish()

    mL = Z.mark()
    SB = 1216
    wxb = [Z.alloc("wxb%d" % i, [128, 8, 128], BF16) for i in range(2)]
    wyb = [Z.alloc("wyb%d" % i, [128, 8, 128], BF16) for i in range(2)]
    wg = [Z.alloc("wg%d" % i, [128, 4, 128], BF16) for i in range(2)]
    xbL = Z.alloc("xbL", [128, NL + 4], F32)
    xbC = Z.alloc("xbC", [128, NCX + 4], F32)
    xc = Z.alloc("xc", [128, NA], F32)
    xcb = Z.alloc("xcb", [128, NA], BF16)
    TS = [[Z.alloc("T%d%s" % (j, "ab"[i]), [128, SB], F32) for j in range(3)] for i in range(2)]
    trb = [Z.alloc("trb%d" % i, [128, 512], F32) for i in range(2)]
    hB = Z.alloc("hB", [128, NQ], F32)
    ybs = Z.alloc("ybs", [128, NQ], F32)
    gtmp = Z.alloc("gtmp", [128, NQ], F32)
    sblk = Z.alloc("sblk", [128, NQ], BF16)
    wmp = [Z.alloc("wmp%d" % i, [128, 8, 128], F32) for i in range(2)]
    bmp = [Z.alloc("bmp%d" % i, [128, 128], F32) for i in range(2)]
    gpc = [Z.alloc("gpc%d" % i, [128, 128], F32) for i in range(2)]
    mod_pieces = []
    GB_TAGS = []
    for s_ in (2, 3, 4, 5):
        for j in range(8):
            def piece(s_=s_, j=j):
                i_ = len(piece_cnt)
                piece_cnt.append(1)
                w_ = wmp[i_ % 2]
                dma('pool', w_[:], w_mod_v[:, :, s_ * 1024 + j * 128: s_ * 1024 + (j + 1) * 128])
                b = nbank()
                if s_ in fm_idx:
                    for k in range(8):
                        mm(ps[:, b, 0:2], w_[:, k, :], sc[:, k, :], start=(k == 0), stop=(k == 7))
                    tt('dve', modfm[:, fm_idx[s_], j, :], ps[:, b, 0:2],
                       bmodT[:, s_ * 8 + j:s_ * 8 + j + 1].to_broadcast([128, 2]), ALU.add)
                else:
                    gi = 0 if s_ == 2 else 1
                    bm_, g_ = bmp[i_ % 2], gpc[i_ % 2]
                    dma('sp', bm_[:], d_bmodg[:, gi * 1024 + j * 128: gi * 1024 + (j + 1) * 128])
                    for k in range(8):
                        mm(ps[:, b, 0:128], sc_rep[:, k, :], w_[:, k, :], start=(k == 0), stop=(k == 7))
                    tt('dve', g_[:], ps[:, b, 0:128], bm_[:], ALU.add)
                    if gi == 0:
                        ts('dve', g_[:], g_[:], 0.5, None, ALU.mult)
                    tag = ("gbc", gi, j)
                    GB_TAGS.append(tag)
                    dma('sp', gbc_dram[:, gi, j * 128:(j + 1) * 128], g_[:], dram_writes=[tag])
            mod_pieces.append(piece)
    piece_cnt = []
    memset('pool', xbL[:, 0:2], 0.0)
    memset('pool', xbL[:, NL + 2:NL + 4], 0.0)
    memset('pool', xbC[:, 0:2], 0.0)
    memset('pool', xbC[:, NCX + 2:NCX + 4], 0.0)
    w_in_v = w_in.rearrange("(k p) n -> p k n", p=128)
    psrot = [0]

    def nbank():
        b = psrot[0]
        psrot[0] = (b + 1) % 8
        return b

    def lru_loads(n):
        wload(wxb[n % 2][:], w_in_v[:, :, OFF_XB + n * 128: OFF_XB + (n + 1) * 128])
        wload(wyb[n % 2][:], w_in_v[:, :, OFF_YB + n * 128: OFF_YB + (n + 1) * 128])
        for d in range(2):
            wload(wg[n % 2][:, 2 * d, :], lru_wa[d, n])
            wload(wg[n % 2][:, 2 * d + 1, :], lru_wx[d, n])

    def xb_tasks(n):
        wx_ = wxb[n % 2]
        out = []
        for (st, sz) in blocks(NQ, NL) + blocks(NL, NA) + blocks(0, NQ):
            def task(st=st, sz=sz):
                b = nbank()
                for k in range(8):
                    mm(ps[:, b, 0:sz], wx_[:, k, :], hTc(k, st, sz), start=(k == 0), stop=(k == 7))
                ev = 'dve' if (OPTS.get('DVEEV', 1) and (st // 512) % 2 == 0) else 'act'
                if st < NL:
                    cp(ev, xbL[:, 2 + st:2 + st + sz], ps[:, b, 0:sz])
                else:
                    cp(ev, xbC[:, 2 + st - NL:2 + st - NL + sz], ps[:, b, 0:sz])
            out.append(task)
        return out

    def yb_tasks(n):
        wy_ = wyb[n % 2]
        out = []
        for (st, sz) in blocks(0, NQ):
            def task(st=st, sz=sz):
                b = nbank()
                for k in range(8):
                    mm(ps[:, b, 0:sz], wy_[:, k, :], hTc(k, st, sz), start=(k == 0), stop=(k == 7))
                cp('dve' if OPTS.get('DVEEV', 1) else 'act', ybs[:, st:st + sz], ps[:, b, 0:sz])
            out.append(task)
        return out

    SUBS = [
        (1, [(NL, NA, True, None), (3136, NL, True, None)]),
        (1, [(NQ, 3136, True, None), (NO, NQ, True, ('hB', NO))]),
        (1, [(1024, NO, True, ('hB', 1024))]),
        (1, [(0, 1024, True, ('hB', 0))]),
        (0, [(NL, NA, False, None), (0, 960, False, ('add', 0))]),
        (0, [(960, NQ, False, ('add', 960))]),
    ]

    deferred = []
    lru_loads(0)
    lru_loads(1)
    for t in xb_tasks(0):
        t()
    for n in range(10):
        wg_ = wg[n % 2]
        def conv(nn, parts, cast=True):
            for (src, o0, st, sz) in parts:
                dst = xc[:, o0 + st:o0 + st + sz]
                ts('dve', dst, src[:, st:st + sz], conv5[:, nn, 0:1], convb[:, nn:nn + 1], ALU.mult, ALU.add)
                for j in range(1, 5):
                    stt('dve', dst, src[:, st + j:st + j + sz], conv5[:, nn, j:j + 1], dst, ALU.mult, ALU.add)
                if cast:
                    cp('act', xcb[:, o0 + st:o0 + st + sz], dst)

        CE = 2304 if OPTS.get('CSPLIT', 1) else NL
        if n == 0 and CE < NL:
            conv(0, [(xbL, 0, CE, NL - CE)])
        if CE < NL:
            conv(n, [(xbC, NL, 0, NCX), (xbL, 0, NO, CE - NO)])
            own_parts = [(xbL, 0, 0, 1024), (xbL, 0, 1024, 1024)]
            while deferred:
                deferred.pop(0)()
        else:
            conv(n, [(xbL, 0, 0, 2048), (xbL, 0, 2048, 2048), (xbC, NL, 0, NCX)])
            own_parts = []
        pend = yb_tasks(n) + (xb_tasks(n + 1) if n + 1 < 10 else [])
        n_after_yb = len(pend) - len(blocks(0, NQ))
        for _ in range(4):
            if mod_pieces:
                pend.append(mod_pieces.pop(0))
        per = (len(pend) + 5) // 6

        def gates(d, segs, tset):
            T1, T2, T3 = tset
            pos = 0
            seginfo = []
            for (lo, hi, rev, outspec) in segs:
                for (st, sz) in blocks(lo, hi):
                    b1_, b2_ = nbank(), nbank()
                    mm(ps[:, b1_, 0:sz], wg_[:, 2 * d, :], xcb[:, st:st + sz])
                    mm(ps[:, b2_, 0:sz], wg_[:, 2 * d + 1, :], xcb[:, st:st + sz])
                    tb = trb[(st // 512) % 2]
                    q0 = pos + (st - lo)
                    act(tb[:, 0:sz], ps[:, b1_, 0:sz], AF.Tanh, bias=hba[:, d, n:n + 1], scale=0.5)
                    act(T1[:, q0:q0 + sz], ps[:, b2_, 0:sz], AF.Tanh, bias=hbx[:, d, n:n + 1], scale=0.5)
                    act(T2[:, q0:q0 + sz], tb[:, 0:sz], AF.Exp, bias=hc[:, d, n:n + 1], scale=hc[:, d, n:n + 1])
                    act(T3[:, q0:q0 + sz], tb[:, 0:sz], AF.Exp, bias=c1[:, d, n:n + 1], scale=c1[:, d, n:n + 1])
                    stt('dve', T1[:, q0:q0 + sz], T1[:, q0:q0 + sz], 1.0, xc[:, st:st + sz], ALU.add, ALU.mult)
                seginfo.append((pos, hi - lo, rev, outspec))
                pos += hi - lo
            return seginfo, pos

        def sqrt_u(tset, L):
            T1, T2, T3 = tset
            act(T3[:, 0:L], T3[:, 0:L], AF.Sqrt, bias=qtr[:], scale=-0.25)
            tt('dve', T1[:, 0:L], T1[:, 0:L], T3[:, 0:L], ALU.mult)

        def scans(tset, seginfo, cur):
            T1, T2, T3 = tset
            for (p0, ln, rev, outspec) in seginfo:
                if outspec is not None and outspec[0] == 'hB':
                    o = hB[:, outspec[1]:outspec[1] + ln]
                elif outspec is not None and outspec[0] == 'add':
                    o = ybs[:, outspec[1]:outspec[1] + ln]
                else:
                    o = T3[:, p0:p0 + ln]
                a_, u_ = T2[:, p0:p0 + ln], T1[:, p0:p0 + ln]
                if rev:
                    scan(o[:, ::-1], a_[:, ::-1], u_[:, ::-1], cur)
                    cur = o[:, 0:1]
                else:
                    scan(o, a_, u_, cur)
                    cur = o[:, ln - 1:ln]
            return cur

        cur = 0.0
        for pr in range(3):
            infos = []
            for j in range(2):
                d, segs = SUBS[2 * pr + j]
                infos.append(gates(d, segs, TS[j]))
                if own_parts:
                    pA, pB = own_parts
                    if pr == 0 and j == 0:
                        conv(n, [pB], cast=False)
                    if pr == 0 and j == 1:
                        cp('act', xcb[:, pB[2]:pB[2] + pB[3]], xc[:, pB[2]:pB[2] + pB[3]])
                    if pr == 1 and j == 0:
                        cp('act', xcb[:, pA[2]:pA[2] + pA[3]], xc[:, pA[2]:pA[2] + pA[3]])
                    if pr == 2 and j == 1 and n + 1 < 10:
                        cp('act', xcb[:, CE:NL], xc[:, CE:NL])
                for _ in range(per):
                    if pend:
                        pend.pop(0)()
            if pr == 2:
                cur = 0.0
            for j in range(2):
                sqrt_u(TS[j], infos[j][1])
            for j in range(2):
                cur = scans(TS[j], infos[j][0], cur)
            if pr == 0 and own_parts:
                conv(n, [own_parts[0]], cast=False)
            if pr == 1 and n + 1 < 10 and CE < NL:
                conv(n + 1, [(xbL, 0, CE, NL - CE)], cast=False)
            if pr == 0:
                while len(pend) > n_after_yb:
                    pend.pop(0)()
                if not OPTS.get('ACTGELU', 1):
                    tt('dve', gtmp[:], ybs[:], ybs[:], ALU.mult)
                    ts('dve', gtmp[:], gtmp[:], 0.044715, 1.0, ALU.mult, ALU.add)
                    tt('dve', gtmp[:], gtmp[:], ybs[:], ALU.mult)
            if pr == 1:
                if OPTS.get('ACTGELU', 1):
                    act(gtmp[:], ybs[:], AF.Gelu_apprx_tanh)
                else:
                    act(gtmp[:], gtmp[:], AF.Tanh, scale=GELU_C)
                    stt('dve', gtmp[:], gtmp[:], 1.0, ybs[:], ALU.add, ALU.mult)
        while pend:
            pend.pop(0)()
        if n == 0:
            dump("xc0", xc[:, 0:512])
            dump("xc0c", xc[:, NL:NA])
        if n + 2 < 10:
            lru_loads(n + 2)

        def epilogue(n=n):
            tt('dve', hB[:], hB[:], ybs[:], ALU.add)
            if n == 0:
                dump("hsum0", hB[:])
            tt('dve', sblk[:], gtmp[:], hB[:], ALU.mult)
            dma('sp', s_dram[n], sblk[:], dram_writes=[("s", n)])
            if n == 0:
                dump("s0", sblk[:], BF16)
        deferred.append(epilogue)
    while deferred:
        deferred.pop(0)()
    assert not mod_pieces
    stt('dve', A2[:], modfm[:, 3, :, 0], 1.0, n2g[:], ALU.add, ALU.mult)
    dump("modfm", modfm[:])
    Z.release(mL)

    if stop_after == 'L':
        return _finish()

    qT = Z.alloc_top("qT", [96, 8, NQ], BF16)
    krT = Z.alloc_top("krT", [96, NA], BF16)
    kvT = Z.alloc_top("kvT", [128, 2, NA], BF16)
    cs = Z.alloc_top("cs", [128, 32, 32], F32)
    wuq = Z.alloc_top("wuq", [128, 3, 768], BF16)
    wK = Z.alloc_top("wK", [128, 2, 8, 64], BF16)
    wV = Z.alloc_top("wV", [128, 2, 8, 64], BF16)
    dma('sp', cs[:], d_cs)
    mB = Z.mark()
    stg_q = Z.alloc("stg_q", [128, 3, 768], F32)
    stg_kv = Z.alloc("stg_kv", [128, 2, 8, 128], F32)
    dma('sp', stg_q[:], w_uq.rearrange("(k p) n -> p k n", p=128))
    dma('sp', stg_kv[:], w_ukv.rearrange("(k p) (h j) -> p k h j", p=128, j=128))
    for k in range(3):
        ts('dve', wuq[:, k, :], stg_q[:, k, :], qng[:, k:k + 1], None, ALU.mult)
    for k in range(2):
        ts('dve', wK[:, k, :, :], stg_kv[:, k, :, 0:64], kvng[:, k:k + 1], None, ALU.mult)
        ts('dve', wV[:, k, :, :], stg_kv[:, k, :, 64:128], kvng[:, k:k + 1], None, ALU.mult)
    Z.release(mB)
    qlT = Z.alloc("qlT", [128, 3, NQ], BF16)
    wA = Z.alloc("wA", [128, 8, 672], BF16)
    wload(wA[:], w_in_v[:, :, 0:672])
    latn = [Z.alloc("latn%d" % i, [128, 672], BF16) for i in range(2)]
    ssq2 = Z.alloc("ssq2", [128, 34, 2], F32)
    sq2 = Z.alloc("sq2", [128, 34, 2], F32)
    rs2 = Z.alloc("rs2", [128, 34, 2], F32)
    rtmp = Z.alloc("rtmp", [128, 4, 16], F32)
    junk2 = Z.alloc("junk2", [128, 384], BF16)
    qtm = [Z.alloc("qtm%d" % i, [128, 8, 96], BF16) for i in range(2)]
    rq = Z.alloc("rq", [128, 4, 4, 16], F32)
    memset('dve', ssq2[:], 0.0)
    b1state = {}

    def b1_s12(i):
        own = i < 17
        col0 = i * 128 if i < 32 else NL + (i - 32) * 128
        ln_ = latn[i % 2]
        bq, bk = nbank(), nbank()
        if own:
            for k in range(8):
                mm(ps[:, bq, 0:QL], hTc(k, col0, 128), wA[:, k, 0:QL], start=(k == 0), stop=(k == 7))
        for k in range(8):
            mm(ps[:, bk, 0:288], hTc(k, col0, 128), wA[:, k, QL:672], start=(k == 0), stop=(k == 7))
        if own:
            act(junk2[:, 0:QL], ps[:, bq, 0:QL], AF.Square, accum_out=ssq2[:, i, 0:1])
            act(sq2[:, i, 0:1], ssq2[:, i, 0:1], AF.Sqrt, bias=epsv[:], scale=1.0 / QL)
            recip(rs2[:, i, 0:1], sq2[:, i, 0:1])
            act(ln_[:, 0:QL], ps[:, bq, 0:QL], AF.Identity, scale=rs2[:, i, 0:1])
        act(junk2[:, 0:KVL], ps[:, bk, 0:KVL], AF.Square, accum_out=ssq2[:, i, 1:2])
        act(sq2[:, i, 1:2], ssq2[:, i, 1:2], AF.Sqrt, bias=epsv[:], scale=1.0 / KVL)
        recip(rs2[:, i, 1:2], sq2[:, i, 1:2])
        ts('dve', ln_[:, QL:QL + KVL], ps[:, bk, 0:KVL], rs2[:, i, 1:2], None, ALU.mult)
        if i < 32:
            x1_, x2_ = ps[:, bk, 256:272], ps[:, bk, 272:288]
            cos_, sin_ = cs[:, i, 0:16], cs[:, i, 16:32]
            tt('dve', rtmp[:, 0, :], x1_, cos_, ALU.mult)
            tt('dve', rtmp[:, 1, :], x2_, sin_, ALU.mult)
            tt('dve', rtmp[:, 2, :], x2_, cos_, ALU.mult)
            tt('dve', rtmp[:, 3, :], x1_, sin_, ALU.mult)
            tt('dve', ln_[:, 640:656], rtmp[:, 0, :], rtmp[:, 1, :], ALU.subtract)
            tt('dve', ln_[:, 656:672], rtmp[:, 2, :], rtmp[:, 3, :], ALU.add)
        else:
            cp('dve', ln_[:, 640:672], ps[:, bk, 256:288])

    def b1_s34(i):
        own = i < 17
        col0 = i * 128 if i < 32 else NL + (i - 32) * 128
        ln_ = latn[i % 2]
        bt = nbank()
        if own:
            for k in range(3):
                tr(psb[:, bt, k * 128:(k + 1) * 128], ln_[:, k * 128:(k + 1) * 128], identb[:])
        for k in range(2):
            tr(psb[:, bt, (3 + k) * 128:(4 + k) * 128], ln_[:, QL + k * 128:QL + (k + 1) * 128], identb[:])
        tr(psb[0:96, bt, 640:768], ln_[:, 576:672], identb[:])
        if own:
            cp('act', qlT[:, :, col0:col0 + 128], psb[:, bt, 0:384].rearrange("p (k c) -> p k c", c=128))
        cp('act', kvT[:, :, col0:col0 + 128], psb[:, bt, 384:640].rearrange("p (k c) -> p k c", c=128))
        cp('act', krT[64:96, col0:col0 + 128], psb[64:96, bt, 640:768])

    for it in range(35):
        if it < 34:
            b1_s12(it)
        if it >= 1:
            b1_s34(it - 1)
    dump("qlT", qlT[:, :, 0:256], BF16)
    dump("kvT", kvT[:, :, 0:256], BF16)
    dump("krT", krT[64:96, 0:256], BF16)
    def q_s12(i):
        col0 = i * 128
        qt_ = qtm[i % 2]
        bb = [nbank(), nbank()]
        for hh in range(2):
            for k in range(3):
                mm(ps[:, bb[hh], 0:384], qlT[:, k, col0:col0 + 128], wuq[:, k, hh * 384:(hh + 1) * 384],
                   start=(k == 0), stop=(k == 2))
        for hh in range(2):
            src = ps[:, bb[hh], 0:384].rearrange("p (h d) -> p h d", d=96)
            x1_, x2_ = src[:, :, 64:80], src[:, :, 80:96]
            cos_ = cs[:, i:i + 1, 0:16].to_broadcast([128, 4, 16])
            sin_ = cs[:, i:i + 1, 16:32].to_broadcast([128, 4, 16])
            cp('dve', qt_[:, hh * 4:(hh + 1) * 4, 0:64], src[:, :, 0:64])
            tt('dve', rq[:, 0], x1_, cos_, ALU.mult)
            tt('dve', rq[:, 1], x2_, sin_, ALU.mult)
            tt('dve', rq[:, 2], x2_, cos_, ALU.mult)
            tt('dve', rq[:, 3], x1_, sin_, ALU.mult)
            tt('dve', qt_[:, hh * 4:(hh + 1) * 4, 64:80], rq[:, 0], rq[:, 1], ALU.subtract)
            tt('dve', qt_[:, hh * 4:(hh + 1) * 4, 80:96], rq[:, 2], rq[:, 3], ALU.add)

    def q_s34(i):
        col0 = i * 128
        qt_ = qtm[i % 2]
        bt = nbank()
        for h in range(8):
            tr(psb[0:96, bt, h * 128:(h + 1) * 128], qt_[:, h, :], identb[:])
        cp('act', qT[:, :, col0:col0 + 128], psb[0:96, bt, :].rearrange("p (h c) -> p h c", c=128))

    for it in range(18):
        if it < 17:
            q_s12(it)
        if it >= 1:
            q_s34(it - 1)
    dump("qT", qT[:, :, 0:256], BF16)
    Z.release(mB)

    if stop_after == 'B':
        return _finish()

    Z.cur = Z.lo
    OT = Z.alloc("OT", [128, 4, NQ], BF16)
    mT = Z.mark()
    Kt = Z.alloc("Kt", [96, 2, NA], BF16)
    Vp = Z.alloc("Vp", [128, 34, 3, 64], BF16)
    pT = [Z.alloc("pT%d" % i, [128, 2, 512], BF16) for i in range(3)]
    rd = Z.alloc("rd", [128, 512], F32)
    memset('pool', Vp[:, :, 1, :], 1.0)
    unit = 0
    for p in range(4):
        for hh in range(2):
            h = 2 * p + hh
            for (st, sz) in blocks(0, NA):
                b = 6 + (st // 512) % 2
                for k in range(2):
                    mm(ps[0:64, b, 0:sz], wK[:, k, h, :], kvT[:, k, st:st + sz], start=(k == 0), stop=(k == 1))
                cp('dve' if (st // 512) % 2 else 'act', Kt[0:64, hh, st:st + sz], ps[0:64, b, 0:sz])
            cp('dve', Kt[64:96, hh, :], krT[64:96, :])
        for g in range(9):
            t0 = g * 4
            nt = min(4, 34 - t0)
            b = 6 + g % 2
            for t in range(nt):
                c0 = (t0 + t) * 128
                for k in range(2):
                    mm(ps[:, b, t * 128:(t + 1) * 128], kvT[:, k, c0:c0 + 128],
                       wV[:, k, 2 * p:2 * p + 2, :].rearrange("p h d -> p (h d)"), start=(k == 0), stop=(k == 1))
            cp('dve', Vp[:, t0:t0 + nt, 0:3:2, :],
               ps[:, b, 0:nt * 128].rearrange("p (t h d) -> p t h d", h=2, d=64))
        if p == 0:
            dump("Kt0", Kt[:, :, 0:256], BF16)
            dump("Vp0", Vp[:, 0:2, :, :], BF16)
        for hh in range(2):
            h = 2 * p + hh
            vsel = slice(0, 2) if hh == 0 else slice(1, 3)
            for (qs, qz) in blocks(0, NQ):
                ob = 4 + unit % 2
                unit += 1
                pend = None
                for ktp in range(17):
                    sb0 = 2 * (ktp % 2)
                    for j in range(2):
                        kt = 2 * ktp + j
                        mm(ps[:, sb0 + j, 0:qz], Kt[0:96, hh, kt * 128:(kt + 1) * 128], qT[0:96, h, qs:qs + qz])
                    pt_ = pT[ktp % 3]
                    act(pt_[:, :, 0:qz], ps[:, sb0:sb0 + 2, 0:qz], AF.Exp, scale=SM_SCALE)
                    if pend is not None:
                        pend()

                    def pv(ktp=ktp, pt_=pt_):
                        for j in range(2):
                            kt = 2 * ktp + j
                            mm(ps[:, ob, 0:qz], Vp[:, kt, vsel, :].rearrange("p a d -> p (a d)"), pt_[:, j, 0:qz],
                               start=(kt == 0), stop=(kt == 33))
                    pend = pv
                pend()
                if hh == 0:
                    recip(rd[0:64, 0:qz], ps[64:128, ob, 0:qz])
                    tt('dve', OT[0:64, p, qs:qs + qz], ps[0:64, ob, 0:qz], rd[0:64, 0:qz], ALU.mult)
                else:
                    recip(rd[64:128, 0:qz], ps[0:64, ob, 0:qz])
                    tt('dve', OT[64:128, p, qs:qs + qz], ps[64:128, ob, 0:qz], rd[64:128, 0:qz], ALU.mult)
    dump("OT", OT[:, :, 0:256], BF16)
    Z.release(mT)
    Z.hi = 229312

    if stop_after == 'T':
        return _finish()

    mix = Z.alloc("mix", [128, 8, NQ], BF16)
    wout = Z.alloc("wout", [128, 8, D], BF16)
    mM = Z.mark()
    s_sb = Z.alloc("s_sb", [128, 10, NQ], BF16)
    wsl = [Z.alloc("wsl%d" % i, [128, 30, 128], BF16) for i in range(2)]
    tg = [Z.alloc("tg%d" % i, [128, 2, 512], F32) for i in range(2)]
    for n in range(10):
        dma('sp', s_sb[:, n, :], s_dram[n], dram_reads=[("s", n)])
    wload(wout[:], w_out.rearrange("(k p) n -> p k n", p=128))
    woa_v = w_o_attn.rearrange("(k p) n -> p k n", p=128)
    wol_v = w_o_lru.rearrange("(k p) n -> p k n", p=128)

    def merge_loads(m):
        w = wsl[m % 2]
        wload(w[:, 0:4, :], woa_v[:, :, m * 128:(m + 1) * 128])
        wload(w[:, 4:14, :], wol_v[:, :, m * 128:(m + 1) * 128])
        wload(w[:, 14:22, :], w_in_v[:, :, OFF_G + m * 128:OFF_G + (m + 1) * 128])
        wload(w[:, 22:30, :], w_in_v[:, :, OFF_G + D + m * 128:OFF_G + D + (m + 1) * 128])

    merge_loads(0)
    it = 0
    for m in range(8):
        if m + 1 < 8:
            merge_loads(m + 1)
        w = wsl[m % 2]
        for (st, sz) in blocks(0, NQ):
            b0 = 4 * (it % 2)
            tg_ = tg[it % 2]
            mt_ = tg_
            it += 1
            for k in range(4):
                mm(ps[:, b0, 0:sz], w[:, k, :], OT[:, k, st:st + sz], start=(k == 0), stop=(k == 3))
            for k in range(10):
                mm(ps[:, b0 + 1, 0:sz], w[:, 4 + k, :], s_sb[:, k, st:st + sz], start=(k == 0), stop=(k == 9))
            for k in range(8):
                mm(ps[:, b0 + 2, 0:sz], w[:, 14 + k, :], hTo[:, k, st:st + sz], start=(k == 0), stop=(k == 7))
            for k in range(8):
                mm(ps[:, b0 + 3, 0:sz], w[:, 22 + k, :], hTo[:, k, st:st + sz], start=(k == 0), stop=(k == 7))
            act(tg_[:, 0, 0:sz], ps[:, b0 + 2, 0:sz], AF.Tanh, bias=hbg[:, m:m + 1], scale=0.5)
            act(tg_[:, 1, 0:sz], ps[:, b0 + 3, 0:sz], AF.Tanh, bias=hbg[:, 8 + m:9 + m], scale=0.5)
            stt('dve', mt_[:, 0, 0:sz], tg_[:, 0, 0:sz], 1.0, ps[:, b0, 0:sz], ALU.add, ALU.mult)
            stt('dve', mt_[:, 1, 0:sz], tg_[:, 1, 0:sz], 1.0, ps[:, b0 + 1, 0:sz], ALU.add, ALU.mult)
            tt('dve', mix[:, m, st:st + sz], mt_[:, 0, 0:sz], mt_[:, 1, 0:sz], ALU.add)
    dump("mix", mix[:, :, 0:256], BF16)
    h2T = hTo
    Z.release(mM)
    xr = [Z.alloc("xr%d" % i, [128, D], F32) for i in range(3)]
    x1t = [Z.alloc("x1t%d" % i, [128, D], F32) for i in range(3)]
    xn2 = [Z.alloc("xn2%d" % i, [128, D], F32) for i in range(3)]
    junk3 = Z.alloc("junk3", [128, D], BF16)
    g1h = Z.alloc("g1h", [128, D], F32)
    dma('sp', g1h[:], gbc_dram[:, 0, :], dram_reads=[t for t in GB_TAGS if t[1] == 0])
    ssq = Z.alloc("ssq3", [128, 34], F32)
    sqv = Z.alloc("sqv3", [128, 34], F32)
    rsv = Z.alloc("rsv3", [128, 34], F32)
    junk = junk3
    memset('dve', ssq[:], 0.0)

    def m2_s1(i):
        xr_, x1_, xn_ = xr[i % 3], x1t[i % 3], xn2[i % 3]
        dma('sp', xr_[:], xs[i * 128:(i + 1) * 128, :])
        for half in range(2):
            b = 4 + 2 * (i % 2) + half
            hs = slice(half * 512, (half + 1) * 512)
            for k in range(8):
                mm(ps[:, b, :], mix[:, k, i * 128:(i + 1) * 128], wout[:, k, hs], start=(k == 0), stop=(k == 7))
            tt('dve', x1_[:, hs], ps[:, b, :], g1h[:, hs], ALU.mult)
            tt('dve', x1_[:, hs], x1_[:, hs], xr_[:, hs], ALU.add)
        if i < 16:
            dma('sp', x1_dram[i * 128:(i + 1) * 128, :], x1_[:], dram_writes=[("x1", i)])
        if i == 0:
            dump("x1_0", x1_[:])
        norm_p1(x1_, xn_, i)

    def m2_s2(i):
        norm_p2(xn2[i % 3], lambda k, i=i: h2T[:, k, i * 128:(i + 1) * 128],
                lambda k: A2[:, k:k + 1], lambda k: modfm[:, 2, k, 0:1], 2 * (i % 2))

    for it in range(18):
        if it < 17:
            m2_s1(it)
        if it >= 1:
            m2_s2(it - 1)
    dump("h2T", h2T[:, :, 0:256], BF16)
    Z.cur = Z.lo
    if stop_after == 'M':
        return _finish()

    actb = Z.alloc("actb", [128, NFC, NO], BF16)
    NWA = 16
    wdnA = Z.alloc("wdnA", [128, NWA, D], BF16)
    w_down_v = w_down.rearrange("(k p) n -> p k n", p=128)
    mF = Z.mark()
    wup = [Z.alloc("wup%d" % i, [128, 2, 8, 128], BF16) for i in range(2)]
    abuf = Z.alloc("abuf", [128, NQ + 2], F32)
    acv = Z.alloc("acv", [128, NO], F32)
    sgv = Z.alloc("sgv", [128, NO], F32)
    memset('pool', abuf[:, 0:1], 0.0)
    w_up_v = w_up.rearrange("(k p) n -> p k n", p=128)

    def ffn_loads(c):
        wload(wup[c % 2][:, 0], w_up_v[:, :, c * 128:(c + 1) * 128])
        wload(wup[c % 2][:, 1], w_up_v[:, :, FFN + c * 128:FFN + (c + 1) * 128])

    ffn_loads(0)
    for c in range(NFC):
        if c + 1 < NFC:
            ffn_loads(c + 1)
        if c == 4:
            wload(wdnA[:, 0:8, :], w_down_v[:, 0:8, :])
        if c == 9:
            wload(wdnA[:, 8:NWA, :], w_down_v[:, 8:NWA, :])
        wu = wup[c % 2]
        for (st, sz) in blocks(0, NQ):
            b = nbank()
            for k in range(8):
                mm(ps[:, b, 0:sz], wu[:, 0, k, :], h2T[:, k, st:st + sz], start=(k == 0), stop=(k == 7))
            cp('act', abuf[:, 1 + st:1 + st + sz], ps[:, b, 0:sz])
        ts('dve', acv[:], abuf[:, 0:NO], fcw[:, c, 0:1], fcb[:, c:c + 1], ALU.mult, ALU.add)
        stt('dve', acv[:], abuf[:, 1:NO + 1], fcw[:, c, 1:2], acv[:], ALU.mult, ALU.add)
        stt('dve', acv[:], abuf[:, 2:NO + 2], fcw[:, c, 2:3], acv[:], ALU.mult, ALU.add)
        act(sgv[:], acv[:], AF.Tanh, scale=0.5)
        stt('dve', sgv[:], sgv[:], 1.0, acv[:], ALU.add, ALU.mult)
        for (st, sz) in blocks(0, NO):
            b = nbank()
            for k in range(8):
                mm(ps[:, b, 0:sz], wu[:, 1, k, :], h2T[:, k, st:st + sz], start=(k == 0), stop=(k == 7))
            stt('dve', actb[:, c, st:st + sz], sgv[:, st:st + sz], 0.5, ps[:, b, 0:sz], ALU.mult, ALU.mult)
        if c == 0:
            dump("act0", actb[:, 0, 0:256], BF16)
    Z.release(mF)
    wdnB = Z.alloc("wdnB", [128, NFC - NWA, D], BF16)
    wload(wdnB[:], w_down_v[:, NWA:NFC, :])
    AH.cur = AH.lo
    x1l = [AH.alloc("x1l%d" % i, [128, D], F32) for i in range(2)]
    x2t = [AH.alloc("x2t%d" % i, [128, D], F32) for i in range(2)]
    junk4 = AH.alloc("junk4", [128, D], BF16)
    g2b = Z.alloc("g2b", [128, D], F32)
    fing = Z.alloc("fing", [128, D], F32)
    dma('sp', g2b[:], gbc_dram[:, 1, :], dram_reads=[t for t in GB_TAGS if t[1] == 1])
    dma('sp', fing[:], d_fing)
    ssq4 = Z.alloc("ssq4", [128, 16], F32)
    sq4 = Z.alloc("sq4", [128, 16], F32)
    rs4 = Z.alloc("rs4", [128, 16], F32)
    memset('dve', ssq4[:], 0.0)
    for i in range(16):
        xl, x2 = x1l[i % 2], x2t[i % 2]
        dma('sp', xl[:], x1_dram[i * 128:(i + 1) * 128, :], dram_reads=[("x1", i)])
        for half in range(2):
            b = 2 * (i % 2) + half
            for c in range(NFC):
                wsrc = wdnA[:, c, half * 512:(half + 1) * 512] if c < NWA else wdnB[:, c - NWA, half * 512:(half + 1) * 512]
                mm(ps[:, b, :], actb[:, c, i * 128:(i + 1) * 128], wsrc,
                   start=(c == 0), stop=(c == NFC - 1))
            tt('dve', x2[:, half * 512:(half + 1) * 512], ps[:, b, :], g2b[:, half * 512:(half + 1) * 512], ALU.mult)
            tt('dve', x2[:, half * 512:(half + 1) * 512], x2[:, half * 512:(half + 1) * 512],
               xl[:, half * 512:(half + 1) * 512], ALU.add)
        act(junk4[:], x2[:], AF.Square, accum_out=ssq4[:, i:i + 1])
        act(sq4[:, i:i + 1], ssq4[:, i:i + 1], AF.Sqrt, bias=epsv[:], scale=1.0 / D)
        recip(rs4[:, i:i + 1], sq4[:, i:i + 1])
        stt('dve', x2[:], x2[:], rs4[:, i:i + 1], fing[:], ALU.mult, ALU.mult)
        dma('sp', out[i * 128:(i + 1) * 128, :], x2[:])
    return _finish()


def _fm(v):
    v = np.asarray(v, np.float32)
    n = v.shape[-1] // 128
    return np.ascontiguousarray(v.reshape(n, 128).T)


def _rope_tables(pos):
    inv = (1.0 / (np.float32(10000.0) ** (np.arange(0, 16, 2, dtype=np.float32) / np.float32(16)))).astype(np.float32)
    row = (pos // 64).astype(np.float32)
    colp = (pos % 64).astype(np.float32)
    ang = np.concatenate([row[:, None] * inv[None, :], colp[:, None] * inv[None, :]], axis=-1).astype(np.float32)
    return np.cos(ang).astype(np.float32), np.sin(ang).astype(np.float32)


def make_in_maps(x, c, ctx, c_ctx, w_mod, b_mod, norm1_g, w_in, b_gate, q_norm_g, kv_norm_g,
                 w_uq, w_ukv, w_o_attn, lru_conv_w, lru_conv_b, lru_w_a, lru_b_a, lru_w_x,
                 lru_b_x, lru_lambda, w_o_lru, w_out, norm2_g, w_up, ffn_conv_w, ffn_conv_b,
                 w_down, final_g):
    import ml_dtypes
    f = lambda a: np.ascontiguousarray(np.asarray(a, np.float32))
    x, c, ctx, c_ctx = f(x), f(c), f(ctx), f(c_ctx)
    shared = {
        "w_mod": f(w_mod[0]), "w_in": f(w_in[0]), "w_uq": f(w_uq[0]), "w_ukv": f(w_ukv[0]),
        "w_o_attn": f(w_o_attn[0]), "w_o_lru": f(w_o_lru[0]), "w_out": f(w_out[0]), "w_up": f(w_up[0]),
        "w_down": f(w_down[0]),
        "bmodT": _fm(f(b_mod[0])),
        "bmodg": np.ascontiguousarray(np.broadcast_to(
            np.concatenate([f(b_mod[0])[2 * D:3 * D], f(b_mod[0])[5 * D:6 * D]])[None, :], (128, 2048))),
        "n1g": _fm(norm1_g[0]), "n2g": _fm(norm2_g[0]),
        "fing": np.ascontiguousarray(np.broadcast_to(f(final_g)[None, :], (128, D))),
        "bgate": _fm(b_gate[0]), "qng": _fm(q_norm_g[0]), "kvng": _fm(kv_norm_g[0]),
        "convb": _fm(lru_conv_b[0]), "fcb": _fm(ffn_conv_b[0]),
        "identf": np.eye(128, dtype=np.float32),
        "identb": np.eye(128, dtype=np.float32).astype(ml_dtypes.bfloat16),
    }
    lcw = f(lru_conv_w[0])
    fcw_ = f(ffn_conv_w[0])
    in_maps = []
    for core in range(8):
        b, half = core // 2, core % 2
        m = dict(shared)
        xb_ = x[b]
        cx = ctx[b]
        if half == 1:
            xb_ = xb_[::-1]
            cx = cx[::-1]
        m["xs"] = np.ascontiguousarray(xb_)
        m["ctxs"] = np.ascontiguousarray(cx)
        cT = np.stack([_fm(c[b]), _fm(c_ctx)], axis=-1)
        m["cT"] = np.ascontiguousarray(cT)
        dirs = [0, 1] if half == 0 else [1, 0]
        m["lru_wa"] = np.ascontiguousarray(f(lru_w_a[0])[dirs])
        m["lru_wx"] = np.ascontiguousarray(f(lru_w_x[0])[dirs])
        m["lba"] = np.ascontiguousarray(np.stack([_fm(f(lru_b_a[0])[d]) for d in dirs], axis=1))
        m["lbx"] = np.ascontiguousarray(np.stack([_fm(f(lru_b_x[0])[d]) for d in dirs], axis=1))
        m["lam"] = np.ascontiguousarray(np.stack([_fm(f(lru_lambda[0])[d]) for d in dirs], axis=1))
        w5 = np.zeros((5, LW), np.float32)
        if half == 0:
            w5[0:4] = lcw
        else:
            w5[1:5] = lcw[::-1]
        m["conv5"] = np.ascontiguousarray(np.stack([_fm(w5[j]) for j in range(5)], axis=-1))
        w3 = fcw_ if half == 0 else fcw_[::-1]
        m["fcw"] = np.ascontiguousarray(np.stack([_fm(w3[j]) for j in range(3)], axis=-1))
        pos = np.arange(NL)
        if half == 1:
            pos = NL - 1 - pos
        cos, sin = _rope_tables(pos)
        cs = np.concatenate([cos, sin], axis=-1).reshape(32, 128, 32).transpose(1, 0, 2)
        m["cs"] = np.ascontiguousarray(cs)
        in_maps.append(m)
    return in_maps


_CACHE = {}


def kernel(**inputs):
    in_maps = make_in_maps(**inputs)
    if "nc" not in _CACHE:
        from contextlib import ExitStack
        nc, P, A, _ = build_program(False)
        es = ExitStack()
        P.finalize(es)
        _CACHE["nc"] = nc
        _CACHE["es"] = es
    nc = _CACHE["nc"]
    res = run_bass_kernel_spmd(nc, in_maps, core_ids=list(range(8)))
    B = 4
    outp = np.zeros((B, NL, D), np.float32)
    for core in range(8):
        b, half = core // 2, core % 2
        o = np.asarray(res.results[core]["out"], np.float32)
        if half == 0:
            outp[b, 0:NO] = o
        else:
            outp[b, NO:NL] = o[::-1]
    return outp
```

```python
import numpy as np
from collections import defaultdict
import concourse.bass as bass
import concourse.mybir as mybir
from concourse.bass_utils import run_bass_kernel_spmd

F32 = mybir.dt.float32
BF16 = mybir.dt.bfloat16
AF = mybir.ActivationFunctionType
ALU = mybir.AluOpType

D = 1024
NL = 4096
NCX = 256
NA = NL + NCX
NQ = 2176
NO = 2048
QL, KVL, KR = 384, 256, 32
OFF_KV = QL
OFF_KR = OFF_KV + KVL
OFF_XB = OFF_KR + KR
LW = 1280
OFF_YB = OFF_XB + LW
OFF_G = OFF_YB + LW
IN_DIM = OFF_G + 2 * D
FFN = 2816
NFC = FFN // 128
EPS = 1e-6
SM_SCALE = 96 ** -0.5
GELU_C = 0.7978845608028654

OPTS = {}
ENGS = ['pe', 'act', 'dve', 'pool', 'sp']
BLOCK_ATTR = {'pe': 'tensor', 'act': 'scalar', 'dve': 'vector', 'pool': 'gpsimd', 'sp': 'sync'}
SAME_ENG_WINDOW = 1 << 30
NDMASEM = 8
FUSE_WAIT = True
BIN = 11


def dsize(dt):
    return mybir.dt.size(dt)


class Op:
    __slots__ = ('eng', 'idx', 'fn', 'waits', 'signal', 'dma', 'dsem', 'dcnt', 'snap', 'sigcount', 'dprev')


class Prog:
    def __init__(self, nc):
        self.nc = nc
        self.ops = {e: [] for e in ENGS}
        self.base = {}
        self.bins = {'sb': defaultdict(list), 'ps': defaultdict(list)}
        self.seen = {e: {e2: -1 for e2 in ENGS} for e in ENGS}
        self.seen_dma = {e: set() for e in ENGS}
        self.ndma = {e: 0 for e in ENGS}
        self.dram_w = {}
        self.dram_r = defaultdict(list)

    def region(self, ap):
        t = ap.tensor
        name = t.name
        if name not in self.base:
            return None
        space, base = self.base[name]
        pat = list(ap.ap)
        es = dsize(t.dtype)
        row = pat[0][0]
        off = ap.offset
        if row == 0:
            row = 1 << 40
        p0 = off // row
        f0 = off % row
        p1 = p0 + pat[0][1]
        lo = f0
        hi = f0
        for st, cnt in pat[1:]:
            ext = st * (cnt - 1)
            if ext < 0:
                lo += ext
            else:
                hi += ext
        b0, b1 = base + lo * es, base + (hi + 1) * es
        if space == 'ps':
            b0 = (b0 // 2048) * 2048
            b1 = ((b1 + 2047) // 2048) * 2048
            p0 = (p0 // 32) * 32
            p1 = ((p1 + 31) // 32) * 32
        return (space, p0, p1, b0, b1)

    def _conflicts(self, reg, is_write, out):
        space, p0, p1, b0, b1 = reg
        bins = self.bins[space]
        for bn in range(b0 >> BIN, ((b1 - 1) >> BIN) + 1):
            lst = bins.get(bn)
            if not lst:
                continue
            for rec in lst:
                if rec[1] < p1 and p0 < rec[2] and rec[3] < b1 and b0 < rec[4]:
                    if is_write or rec[0]:
                        out.add(rec[5])

    def _register(self, reg, is_write, opref):
        space, p0, p1, b0, b1 = reg
        bins = self.bins[space]
        rec = (is_write, p0, p1, b0, b1, opref)
        for bn in range(b0 >> BIN, ((b1 - 1) >> BIN) + 1):
            lst = bins[bn]
            if is_write:
                lst[:] = [r for r in lst if not (p0 <= r[1] and r[2] <= p1 and b0 <= r[3] and r[4] <= b1)]
            elif not opref[2]:
                lst[:] = [r for r in lst if not ((not r[0]) and r[1] == p0 and r[2] == p1 and r[3] == b0
                                                 and r[4] == b1 and r[5][0] == opref[0] and not r[5][2])]
            lst.append(rec)

    def emit(self, eng, fn, reads=(), writes=(), dma=False, dram_reads=(), dram_writes=()):
        deps = set()
        rr = [r for r in (self.region(a) for a in reads) if r is not None]
        ww = [r for r in (self.region(a) for a in writes) if r is not None]
        ww = ww + [r for r in rr if r[0] == 'ps']
        rr = [r for r in rr if r[0] != 'ps']
        for r in rr:
            self._conflicts(r, False, deps)
        for r in ww:
            self._conflicts(r, True, deps)
        for tag in dram_reads:
            if tag in self.dram_w:
                deps.add(self.dram_w[tag])
        for tag in dram_writes:
            if tag in self.dram_w:
                deps.add(self.dram_w[tag])
            deps.update(self.dram_r.get(tag, ()))
        op = Op()
        op.eng = eng
        op.idx = len(self.ops[eng])
        op.fn = fn
        op.signal = False
        op.dma = dma
        op.dsem = None
        op.dcnt = 0
        op.sigcount = 0
        op.dprev = None
        seen = self.seen[eng]
        sdma = self.seen_dma[eng]
        waits = []
        if dma:
            k = self.ndma[eng]
            self.ndma[eng] += 1
            op.dsem = (eng, k % NDMASEM)
            op.dcnt = 16 * (k // NDMASEM + 1)
            if k >= NDMASEM:
                prev = self._dma_ops[eng][k - NDMASEM]
                deps.add((eng, prev.idx, True))
            self._dma_ops.setdefault(eng, []).append(op)
        best = {}
        dl = []
        for (e2, i2, d2) in deps:
            if d2:
                dl.append((e2, i2, d2))
            elif i2 > best.get(e2, -1):
                best[e2] = i2
        dl.sort()
        dl += [(e2, i2, False) for e2, i2 in sorted(best.items())]
        for (e2, i2, d2) in dl:
            src = self.ops[e2][i2]
            if d2:
                if (e2, i2) in sdma:
                    continue
                sdma.add((e2, i2))
                waits.append((e2, i2))
            else:
                if e2 == eng:
                    if eng == 'pe' or op.idx - i2 > SAME_ENG_WINDOW:
                        continue
                if i2 <= seen[e2]:
                    continue
                seen[e2] = i2
                src.signal = True
                waits.append((e2, i2))
            for e3, v in src.snap.items():
                if v > seen[e3]:
                    seen[e3] = v
        op.waits = waits
        op.snap = dict(seen)
        self.ops[eng].append(op)
        ref = (eng, op.idx, dma)
        for r in rr:
            self._register(r, False, ref)
        for r in ww:
            self._register(r, True, ref)
        for tag in dram_reads:
            self.dram_r[tag].append(ref)
        for tag in dram_writes:
            self.dram_w[tag] = ref
            self.dram_r[tag] = []
        return op

    _dma_ops = None

    def finalize(self, es):
        nc = self.nc
        self.sem = {e: es.enter_context(nc.semaphore("s_" + e)) for e in ENGS}
        self.dsems = {}
        for e in ENGS:
            if self.ndma[e]:
                for j in range(min(NDMASEM, self.ndma[e])):
                    self.dsems[(e, j)] = es.enter_context(nc.semaphore("d_%s%d" % (e, j)))
        for e in ENGS:
            c = 0
            for op in self.ops[e]:
                if op.signal and not op.dma:
                    c += 1
                    op.sigcount = c
        block = es.enter_context(nc.Block())
        for e in ENGS:
            self._replay_engine(block, e)

    def _replay_engine(self, block, e):
        ops = self.ops[e]
        allops = self.ops
        sem = self.sem
        dsems = self.dsems

        def body(eng):
            for op in ops:
                wl = []
                for (e2, i2) in op.waits:
                    src = allops[e2][i2]
                    if src.dma:
                        wl.append((dsems[src.dsem], src.dcnt))
                    else:
                        wl.append((sem[e2], src.sigcount))
                fuse = wl.pop() if (wl and FUSE_WAIT and not op.dma) else None
                for (sm, v) in wl:
                    eng.wait_ge(sm, v)
                ins = op.fn(eng)
                if fuse is not None:
                    ins._wait_ge(fuse[0], fuse[1])
                if op.dma:
                    ins.then_inc(dsems[op.dsem], 16)
                elif op.signal:
                    ins.then_inc(sem[e], 1)
        getattr(block, BLOCK_ATTR[e])(body)


class Arena:
    def __init__(self, nc, prog, lo=20480, hi=229376):
        self.nc, self.prog, self.lo, self.hi, self.cur = nc, prog, lo, hi, lo
        self.n = 0
        self.peak = lo

    def alloc(self, name, shape, dt):
        per = int(np.prod(shape[1:])) * dsize(dt)
        per = (per + 63) // 64 * 64
        assert self.cur + per <= self.hi, "SBUF arena overflow at %s: need %d have %d" % (name, per, self.hi - self.cur)
        self.n += 1
        t = self.nc.alloc_sbuf_tensor_at("%s_%d_%d" % (name, self.lo, self.n), list(shape), dt, offset=self.cur)
        self.prog.base[t.name] = ('sb', self.cur)
        self.cur += per
        self.peak = max(self.peak, self.cur)
        return t

    def alloc_top(self, name, shape, dt):
        per = int(np.prod(shape[1:])) * dsize(dt)
        per = (per + 63) // 64 * 64
        assert self.hi - per >= self.cur, "SBUF arena overflow (top) at %s" % name
        self.hi -= per
        self.n += 1
        t = self.nc.alloc_sbuf_tensor_at("%s_t%d" % (name, self.n), list(shape), dt, offset=self.hi)
        self.prog.base[t.name] = ('sb', self.hi)
        return t

    def mark(self):
        return (self.cur, self.hi)

    def release(self, m):
        self.cur, self.hi = m


def blocks(lo, hi, step=512):
    out = []
    s = lo
    while s < hi:
        out.append((s, min(step, hi - s)))
        s += step
    return out


def build_program(debug=False, stop_after=None):
    nc = bass.Bass("TRN2", target_bir_lowering=False)
    P = Prog(nc)
    P._dma_ops = {}
    LO0 = 20480
    PERS = 5 * 1024
    HTO = 8 * NQ * 2
    A = Arena(nc, P, LO0, LO0 + PERS)
    AH = Arena(nc, P, LO0 + PERS, LO0 + PERS + HTO)
    Z = Arena(nc, P, LO0 + PERS + HTO, 229312)
    dbg_outs = {}

    def din(name, shape, dt=F32):
        return nc.dram_tensor(name, list(shape), dt, kind="ExternalInput").ap()

    xs = din("xs", [NL, D])
    ctxs = din("ctxs", [NCX, D])
    d_cT = din("cT", [128, 8, 2])
    d_bmodT = din("bmodT", [128, 48])
    d_bmodg = din("bmodg", [128, 2048])
    d_n1g = din("n1g", [128, 8])
    d_n2g = din("n2g", [128, 8])
    d_fing = din("fing", [128, D])
    d_bgate = din("bgate", [128, 16])
    d_qng = din("qng", [128, 3])
    d_kvng = din("kvng", [128, 2])
    d_conv5 = din("conv5", [128, 10, 5])
    d_convb = din("convb", [128, 10])
    d_lba = din("lba", [128, 2, 10])
    d_lbx = din("lbx", [128, 2, 10])
    d_lam = din("lam", [128, 2, 10])
    d_fcw = din("fcw", [128, NFC, 3])
    d_fcb = din("fcb", [128, NFC])
    d_cs = din("cs", [128, 32, 32])
    d_identf = din("identf", [128, 128])
    d_identb = din("identb", [128, 128], BF16)
    w_mod = din("w_mod", [D, 6 * D])
    w_in = din("w_in", [D, IN_DIM])
    w_uq = din("w_uq", [QL, 768])
    w_ukv = din("w_ukv", [KVL, 1024])
    w_o_attn = din("w_o_attn", [512, D])
    w_o_lru = din("w_o_lru", [LW, D])
    w_out = din("w_out", [D, D])
    w_up = din("w_up", [D, 2 * FFN])
    w_down = din("w_down", [FFN, D])
    lru_wa = din("lru_wa", [2, 10, 128, 128])
    lru_wx = din("lru_wx", [2, 10, 128, 128])
    out = nc.dram_tensor("out", [NO, D], F32, kind="ExternalOutput").ap()
    s_dram = nc.dram_tensor("s_scr", [10, 128, NQ], BF16, kind="Internal").ap()
    gbc_dram = nc.dram_tensor("gbc_scr", [128, 2, D], F32, kind="Internal").ap()
    x1_dram = nc.dram_tensor("x1_scr", [NO, D], F32, kind="Internal").ap()

    ps = nc.alloc_psum_tensor("ps", [128, 8, 512], F32)
    P.base[ps.name] = ('ps', 0)
    psb = ps[:, :, :].bitcast(BF16)
    P.base[psb.tensor.name] = ('ps', 0)

    def dma(q, out_, in_, dram_reads=(), dram_writes=()):
        return P.emit(q, lambda e: e.dma_start(out=out_, in_=in_), reads=[in_], writes=[out_], dma=True,
                      dram_reads=dram_reads, dram_writes=dram_writes)

    def mm(out_, lhsT, rhs, start=True, stop=True):
        return P.emit('pe', lambda e: e.matmul(out_, lhsT=lhsT, rhs=rhs, start=start, stop=stop),
                      reads=[lhsT, rhs], writes=[out_])

    def tr(out_, in_, ident):
        return P.emit('pe', lambda e: e.transpose(out=out_, in_=in_, identity=ident), reads=[in_, ident], writes=[out_])

    def act(out_, in_, func, bias=0.0, scale=1.0, accum_out=None, eng='act'):
        rd = [in_]
        if not isinstance(bias, float):
            rd.append(bias)
        if not isinstance(scale, float):
            rd.append(scale)
        wr = [out_]
        kw = {}
        if accum_out is not None:
            wr.append(accum_out)
            kw['accum_out'] = accum_out
        return P.emit('act', lambda e: e.activation(out=out_, in_=in_, func=func, bias=bias, scale=scale, **kw),
                      reads=rd, writes=wr)

    def tt(eng, out_, in0, in1, op):
        return P.emit(eng, lambda e: e.tensor_tensor(out=out_, in0=in0, in1=in1, op=op), reads=[in0, in1], writes=[out_])

    def ts(eng, out_, in0, s1, s2, op0, op1=None):
        rd = [in0]
        if not isinstance(s1, float):
            rd.append(s1)
        if s2 is not None and not isinstance(s2, float):
            rd.append(s2)
        if op1 is None:
            return P.emit(eng, lambda e: e.tensor_scalar(out=out_, in0=in0, scalar1=s1, scalar2=None, op0=op0),
                          reads=rd, writes=[out_])
        return P.emit(eng, lambda e: e.tensor_scalar(out=out_, in0=in0, scalar1=s1, scalar2=s2, op0=op0, op1=op1),
                      reads=rd, writes=[out_])

    def stt(eng, out_, in0, scalar, in1, op0, op1):
        rd = [in0, in1]
        if not isinstance(scalar, float):
            rd.append(scalar)
        return P.emit(eng, lambda e: e.scalar_tensor_tensor(out=out_, in0=in0, scalar=scalar, in1=in1, op0=op0, op1=op1),
                      reads=rd, writes=[out_])

    def cp(eng, out_, in_):
        if eng == 'act':
            return act(out_, in_, AF.Identity)
        return P.emit(eng, lambda e: e.tensor_copy(out=out_, in_=in_), reads=[in_], writes=[out_])

    def memset(eng, ap, val):
        return P.emit(eng, lambda e: e.memset(ap, val), writes=[ap])

    def recip(out_, in_):
        return P.emit('dve', lambda e: e.reciprocal(out=out_, in_=in_), reads=[in_], writes=[out_])

    def scan(out_, d0, d1, initial):
        rd = [d0, d1]
        if not isinstance(initial, float):
            rd.append(initial)
        return P.emit('dve', lambda e: e.tensor_tensor_scan(out=out_, data0=d0, data1=d1, initial=initial,
                                                            op0=ALU.mult, op1=ALU.add), reads=rd, writes=[out_])

    def dump(name, ap, dt=F32):
        if not debug:
            return
        shape = list(ap.shape)
        t = nc.dram_tensor("dbg_" + name, shape, dt, kind="ExternalOutput").ap()
        dbg_outs[name] = t
        dma('sp', t, ap)

    def wload(dst, src):
        return dma('pool', dst, src)

    def _finish():
        P.emit('sp', lambda e: e.nop(), reads=[], writes=[])
        last = P.ops['sp'][-1]
        for o in P._dma_ops.get('sp', [])[-NDMASEM:]:
            if (o.eng, o.idx) not in last.waits:
                last.waits.append((o.eng, o.idx))
        return nc, P, A, dbg_outs

    cT = A.alloc("cT", [128, 8, 2], F32)
    bmodT = A.alloc("bmodT", [128, 48], F32)
    n1g = A.alloc("n1g", [128, 8], F32)
    n2g = A.alloc("n2g", [128, 8], F32)
    bgate = A.alloc("bgate", [128, 16], F32)
    qng = A.alloc("qng", [128, 3], F32)
    kvng = A.alloc("kvng", [128, 2], F32)
    conv5 = A.alloc("conv5", [128, 10, 5], F32)
    convb = A.alloc("convb", [128, 10], F32)
    lba = A.alloc("lba", [128, 2, 10], F32)
    lbx = A.alloc("lbx", [128, 2, 10], F32)
    lam = A.alloc("lam", [128, 2, 10], F32)
    fcw = A.alloc("fcw", [128, NFC, 3], F32)
    fcb = A.alloc("fcb", [128, NFC], F32)
    identf = A.alloc("identf", [128, 128], F32)
    identb = A.alloc("identb", [128, 128], BF16)
    for dst, src in ((cT, d_cT), (bmodT, d_bmodT), (n1g, d_n1g), (n2g, d_n2g), (bgate, d_bgate),
                     (qng, d_qng), (kvng, d_kvng), (conv5, d_conv5), (convb, d_convb), (lba, d_lba), (lbx, d_lbx),
                     (lam, d_lam), (fcw, d_fcw), (fcb, d_fcb), (identf, d_identf), (identb, d_identb)):
        dma('sp', dst[:], src)

    epsv = A.alloc("epsv", [128, 1], F32)
    memset('dve', epsv[:], EPS)
    qtr = A.alloc("qtr", [128, 1], F32)
    memset('dve', qtr[:], 0.25)
    modfm = A.alloc("modfm", [128, 4, 8, 2], F32)
    A1 = A.alloc("A1", [128, 8, 2], F32)
    A2 = A.alloc("A2", [128, 8], F32)
    hbg = A.alloc("hbg", [128, 16], F32)
    hba = A.alloc("hba", [128, 2, 10], F32)
    hbx = A.alloc("hbx", [128, 2, 10], F32)
    c1 = A.alloc("c1", [128, 2, 10], F32)
    hc = A.alloc("hc", [128, 2, 10], F32)
    hTo = AH.alloc("hTo", [128, 8, NQ], BF16)
    hTx = Z.alloc("hTx", [128, 8, NA - NQ], BF16)

    def hTc(k, st, sz):
        if st + sz <= NQ:
            return hTo[:, k, st:st + sz]
        assert st >= NQ
        return hTx[:, k, st - NQ:st - NQ + sz]

    ALLBLK = blocks(0, NQ) + blocks(NQ, NL) + blocks(NL, NA)

    sc = Z.alloc("sc", [128, 8, 2], F32)
    sc_rep = Z.alloc("sc_rep", [128, 8, 128], F32)
    m0 = Z.mark()
    th0 = Z.alloc("th0", [128, 8, 2], F32)
    wm = [Z.alloc("wm%d" % i, [128, 8, 1024], F32) for i in range(2)]
    act(th0[:], cT[:], AF.Tanh, scale=0.5)
    ts('dve', th0[:], th0[:], 0.5, 0.5, ALU.mult, ALU.add)
    tt('dve', sc[:], th0[:], cT[:], ALU.mult)
    for k in range(8):
        cp('dve', sc_rep[:, k, :], sc[:, k, 0:1].to_broadcast([128, 128]))
    act(c1[:], lam[:], AF.Exp, scale=-1.0)
    act(c1[:], c1[:], AF.Ln, bias=1.0)
    ts('dve', hc[:], c1[:], -4.0, None, ALU.mult)
    ts('dve', c1[:], c1[:], -8.0, None, ALU.mult)
    ts('dve', hba[:], lba[:], 0.5, None, ALU.mult)
    ts('dve', hbx[:], lbx[:], 0.5, None, ALU.mult)
    ts('dve', hbg[:], bgate[:], 0.5, None, ALU.mult)
    w_mod_v = w_mod.rearrange("(k p) n -> p k n", p=128)
    fm_idx = {0: 0, 1: 1, 3: 2, 4: 3}

    def mod_slab(s):
        wb = wm[s % 2]
        dma('pool', wb[:], w_mod_v[:, :, s * 1024:(s + 1) * 1024])
        if s in fm_idx:
            bank = 4 + s % 2
            for j in range(8):
                for k in range(8):
                    mm(ps[:, bank, 2 * j:2 * j + 2], wb[:, k, j * 128:(j + 1) * 128], sc[:, k, :], start=(k == 0), stop=(k == 7))
            for col in range(2):
                tt('dve', modfm[:, fm_idx[s], :, col], ps[:, bank, col:16:2], bmodT[:, s * 8:(s + 1) * 8], ALU.add)
        else:
            gi = 0 if s == 2 else 1
            for half in range(2):
                bank = 6 + half
                for k in range(8):
                    mm(ps[:, bank, :], sc_rep[:, k, :], wb[:, k, half * 512:(half + 1) * 512], start=(k == 0), stop=(k == 7))
                tt('dve', gbc[:, gi, half * 512:(half + 1) * 512], ps[:, bank, :],
                   bmodg[:, gi * 1024 + half * 512: gi * 1024 + (half + 1) * 512], ALU.add)

    mod_slab(0)
    mod_slab(1)
    for col in range(2):
        stt('dve', A1[:, :, col], modfm[:, 1, :, col], 1.0, n1g[:], ALU.add, ALU.mult)

    xt = [Z.alloc("xt%d" % i, [128, D], F32) for i in range(3)]
    junk = Z.alloc("junk", [128, D], BF16)
    ssq = Z.alloc("ssq", [128, 34], F32)
    sqv = Z.alloc("sqv", [128, 34], F32)
    rsv = Z.alloc("rsv", [128, 34], F32)
    memset('dve', ssq[:], 0.0)

    def norm_p1(buf, bufo, statc):
        act(junk[:], buf[:], AF.Square, accum_out=ssq[:, statc:statc + 1])
        act(sqv[:, statc:statc + 1], ssq[:, statc:statc + 1], AF.Sqrt, bias=epsv[:], scale=1.0 / D)
        recip(rsv[:, statc:statc + 1], sqv[:, statc:statc + 1])
        ts('dve', bufo[:], buf[:], rsv[:, statc:statc + 1], None, ALU.mult)

    def norm_p2(bufo, dstf, Ascale, Abias, bank0):
        for k in range(8):
            tr(ps[:, bank0 + k // 4, (k % 4) * 128:(k % 4 + 1) * 128], bufo[:, k * 128:(k + 1) * 128], identf[:])
        for k in range(8):
            src = ps[:, bank0 + k // 4, (k % 4) * 128:(k % 4 + 1) * 128]
            if k < 4:
                act(dstf(k), src, AF.Identity, bias=Abias(k), scale=Ascale(k))
            else:
                ts('dve', dstf(k), src, Ascale(k), Abias(k), ALU.mult, ALU.add)

    def norm_to_T(buf, dstf, statc, Ascale, Abias, bank0):
        norm_p1(buf, buf, statc)
        norm_p2(buf, dstf, Ascale, Abias, bank0)

    for it in range(35):
        if it < 34:
            i = it
            buf = xt[i % 3]
            src = xs[i * 128:(i + 1) * 128, :] if i < 32 else ctxs[(i - 32) * 128:(i - 31) * 128, :]
            dma('sp', buf[:], src)
            norm_p1(buf, buf, i)
        if it >= 1:
            i = it - 1
            col = 0 if i < 32 else 1
            norm_p2(xt[i % 3], lambda k, i=i: hTc(k, i * 128, 128),
                    lambda k, col=col: A1[:, k, col:col + 1], lambda k, col=col: modfm[:, 0, k, col:col + 1],
                    2 * (i % 2))
    dump("hT", hTo[:, :, 0:256], BF16)
    dump("hTc", hTx[:, :, NL - NQ:NA - NQ], BF16)
    Z.release(m0)

    if stop_after in ('0', 'A'):
        return _finish()

    mL = Z.mark()
    SB = 1216
    wxb = [Z.alloc("wxb%d" % i, [128, 8, 128], BF16) for i in range(2)]
    wyb = [Z.alloc("wyb%d" % i, [128, 8, 128], BF16) for i in range(2)]
    wg = [Z.alloc("wg%d" % i, [128, 4, 128], BF16) for i in range(2)]
    xbL = Z.alloc("xbL", [128, NL + 4], F32)
    xbC = Z.alloc("xbC", [128, NCX + 4], F32)
    xc = Z.alloc("xc", [128, NA], F32)
    xcb = Z.alloc("xcb", [128, NA], BF16)
    TS = [[Z.alloc("T%d%s" % (j, "ab"[i]), [128, SB], F32) for j in range(3)] for i in range(2)]
    trb = [Z.alloc("trb%d" % i, [128, 512], F32) for i in range(2)]
    hB = Z.alloc("hB", [128, NQ], F32)
    ybs = Z.alloc("ybs", [128, NQ], F32)
    gtmp = Z.alloc("gtmp", [128, NQ], F32)
    sblk = Z.alloc("sblk", [128, NQ], BF16)
    wmp = [Z.alloc("wmp%d" % i, [128, 8, 128], F32) for i in range(2)]
    bmp = [Z.alloc("bmp%d" % i, [128, 128], F32) for i in range(2)]
    gpc = [Z.alloc("gpc%d" % i, [128, 128], F32) for i in range(2)]
    mod_pieces = []
    GB_TAGS = []
    for s_ in (2, 3, 4, 5):
        for j in range(8):
            def piece(s_=s_, j=j):
                i_ = len(piece_cnt)
                piece_cnt.append(1)
                w_ = wmp[i_ % 2]
                dma('pool', w_[:], w_mod_v[:, :, s_ * 1024 + j * 128: s_ * 1024 + (j + 1) * 128])
                b = nbank()
                if s_ in fm_idx:
                    for k in range(8):
                        mm(ps[:, b, 0:2], w_[:, k, :], sc[:, k, :], start=(k == 0), stop=(k == 7))
                    tt('dve', modfm[:, fm_idx[s_], j, :], ps[:, b, 0:2],
                       bmodT[:, s_ * 8 + j:s_ * 8 + j + 1].to_broadcast([128, 2]), ALU.add)
                else:
                    gi = 0 if s_ == 2 else 1
                    bm_, g_ = bmp[i_ % 2], gpc[i_ % 2]
                    dma('sp', bm_[:], d_bmodg[:, gi * 1024 + j * 128: gi * 1024 + (j + 1) * 128])
                    for k in range(8):
                        mm(ps[:, b, 0:128], sc_rep[:, k, :], w_[:, k, :], start=(k == 0), stop=(k == 7))
                    tt('dve', g_[:], ps[:, b, 0:128], bm_[:], ALU.add)
                    if gi == 0:
                        ts('dve', g_[:], g_[:], 0.5, None, ALU.mult)
                    tag = ("gbc", gi, j)
                    GB_TAGS.append(tag)
                    dma('sp', gbc_dram[:, gi, j * 128:(j + 1) * 128], g_[:], dram_writes=[tag])
            mod_pieces.append(piece)
    piece_cnt = []
    memset('pool', xbL[:, 0:2], 0.0)
    memset('pool', xbL[:, NL + 2:NL + 4], 0.0)
    memset('pool', xbC[:, 0:2], 0.0)
    memset('pool', xbC[:, NCX + 2:NCX + 4], 0.0)
    w_in_v = w_in.rearrange("(k p) n -> p k n", p=128)
    psrot = [0]

    def nbank():
        b = psrot[0]
        psrot[0] = (b + 1) % 8
        return b

    def lru_loads(n):
        wload(wxb[n % 2][:], w_in_v[:, :, OFF_XB + n * 128: OFF_XB + (n + 1) * 128])
        wload(wyb[n % 2][:], w_in_v[:, :, OFF_YB + n * 128: OFF_YB + (n + 1) * 128])
        for d in range(2):
            wload(wg[n % 2][:, 2 * d, :], lru_wa[d, n])
            wload(wg[n % 2][:, 2 * d + 1, :], lru_wx[d, n])

    def xb_tasks(n):
        wx_ = wxb[n % 2]
        out = []
        for (st, sz) in blocks(NQ, NL) + blocks(NL, NA) + blocks(0, NQ):
            def task(st=st, sz=sz):
                b = nbank()
                for k in range(8):
                    mm(ps[:, b, 0:sz], wx_[:, k, :], hTc(k, st, sz), start=(k == 0), stop=(k == 7))
                ev = 'dve' if (OPTS.get('DVEEV', 1) and (st // 512) % 2 == 0) else 'act'
                if st < NL:
                    cp(ev, xbL[:, 2 + st:2 + st + sz], ps[:, b, 0:sz])
                else:
                    cp(ev, xbC[:, 2 + st - NL:2 + st - NL + sz], ps[:, b, 0:sz])
            out.append(task)
        return out

    def yb_tasks(n):
        wy_ = wyb[n % 2]
        out = []
        for (st, sz) in blocks(0, NQ):
            def task(st=st, sz=sz):
                b = nbank()
                for k in range(8):
                    mm(ps[:, b, 0:sz], wy_[:, k, :], hTc(k, st, sz), start=(k == 0), stop=(k == 7))
                cp('dve' if OPTS.get('DVEEV', 1) else 'act', ybs[:, st:st + sz], ps[:, b, 0:sz])
            out.append(task)
        return out

    SUBS = [
        (1, [(NL, NA, True, None), (3136, NL, True, None)]),
        (1, [(NQ, 3136, True, None), (NO, NQ, True, ('hB', NO))]),
        (1, [(1024, NO, True, ('hB', 1024))]),
        (1, [(0, 1024, True, ('hB', 0))]),
        (0, [(NL, NA, False, None), (0, 960, False, ('add', 0))]),
        (0, [(960, NQ, False, ('add', 960))]),
    ]

    deferred = []
    lru_loads(0)
    lru_loads(1)
    for t in xb_tasks(0):
        t()
    for n in range(10):
        wg_ = wg[n % 2]
        def conv(nn, parts, cast=True):
            for (src, o0, st, sz) in parts:
                dst = xc[:, o0 + st:o0 + st + sz]
                ts('dve', dst, src[:, st:st + sz], conv5[:, nn, 0:1], convb[:, nn:nn + 1], ALU.mult, ALU.add)
                for j in range(1, 5):
                    stt('dve', dst, src[:, st + j:st + j + sz], conv5[:, nn, j:j + 1], dst, ALU.mult, ALU.add)
                if cast:
                    cp('act', xcb[:, o0 + st:o0 + st + sz], dst)

        CE = 2304 if OPTS.get('CSPLIT', 1) else NL
        if n == 0 and CE < NL:
            conv(0, [(xbL, 0, CE, NL - CE)])
        if CE < NL:
            conv(n, [(xbC, NL, 0, NCX), (xbL, 0, NO, CE - NO)])
            own_parts = [(xbL, 0, 0, 1024), (xbL, 0, 1024, 1024)]
            while deferred:
                deferred.pop(0)()
        else:
            conv(n, [(xbL, 0, 0, 2048), (xbL, 0, 2048, 2048), (xbC, NL, 0, NCX)])
            own_parts = []
        pend = yb_tasks(n) + (xb_tasks(n + 1) if n + 1 < 10 else [])
        n_after_yb = len(pend) - len(blocks(0, NQ))
        for _ in range(4):
            if mod_pieces:
                pend.append(mod_pieces.pop(0))
        per = (len(pend) + 5) // 6

        def gates(d, segs, tset):
            T1, T2, T3 = tset
            pos = 0
            seginfo = []
            for (lo, hi, rev, outspec) in segs:
                for (st, sz) in blocks(lo, hi):
                    b1_, b2_ = nbank(), nbank()
                    mm(ps[:, b1_, 0:sz], wg_[:, 2 * d, :], xcb[:, st:st + sz])
                    mm(ps[:, b2_, 0:sz], wg_[:, 2 * d + 1, :], xcb[:, st:st + sz])
                    tb = trb[(st // 512) % 2]
                    q0 = pos + (st - lo)
                    act(tb[:, 0:sz], ps[:, b1_, 0:sz], AF.Tanh, bias=hba[:, d, n:n + 1], scale=0.5)
                    act(T1[:, q0:q0 + sz], ps[:, b2_, 0:sz], AF.Tanh, bias=hbx[:, d, n:n + 1], scale=0.5)
                    act(T2[:, q0:q0 + sz], tb[:, 0:sz], AF.Exp, bias=hc[:, d, n:n + 1], scale=hc[:, d, n:n + 1])
                    act(T3[:, q0:q0 + sz], tb[:, 0:sz], AF.Exp, bias=c1[:, d, n:n + 1], scale=c1[:, d, n:n + 1])
                    stt('dve', T1[:, q0:q0 + sz], T1[:, q0:q0 + sz], 1.0, xc[:, st:st + sz], ALU.add, ALU.mult)
                seginfo.append((pos, hi - lo, rev, outspec))
                pos += hi - lo
            return seginfo, pos

        def sqrt_u(tset, L):
            T1, T2, T3 = tset
            act(T3[:, 0:L], T3[:, 0:L], AF.Sqrt, bias=qtr[:], scale=-0.25)
            tt('dve', T1[:, 0:L], T1[:, 0:L], T3[:, 0:L], ALU.mult)

        def scans(tset, seginfo, cur):
            T1, T2, T3 = tset
            for (p0, ln, rev, outspec) in seginfo:
                if outspec is not None and outspec[0] == 'hB':
                    o = hB[:, outspec[1]:outspec[1] + ln]
                elif outspec is not None and outspec[0] == 'add':
                    o = ybs[:, outspec[1]:outspec[1] + ln]
                else:
                    o = T3[:, p0:p0 + ln]
                a_, u_ = T2[:, p0:p0 + ln], T1[:, p0:p0 + ln]
                if rev:
                    scan(o[:, ::-1], a_[:, ::-1], u_[:, ::-1], cur)
                    cur = o[:, 0:1]
                else:
                    scan(o, a_, u_, cur)
                    cur = o[:, ln - 1:ln]
            return cur

        cur = 0.0
        for pr in range(3):
            infos = []
            for j in range(2):
                d, segs = SUBS[2 * pr + j]
                infos.append(gates(d, segs, TS[j]))
                if own_parts:
                    pA, pB = own_parts
                    if pr == 0 and j == 0:
                        conv(n, [pB], cast=False)
                    if pr == 0 and j == 1:
                        cp('act', xcb[:, pB[2]:pB[2] + pB[3]], xc[:, pB[2]:pB[2] + pB[3]])
                    if pr == 1 and j == 0:
                        cp('act', xcb[:, pA[2]:pA[2] + pA[3]], xc[:, pA[2]:pA[2] + pA[3]])
                    if pr == 2 and j == 1 and n + 1 < 10:
                        cp('act', xcb[:, CE:NL], xc[:, CE:NL])
                for _ in range(per):
                    if pend:
                        pend.pop(0)()
            if pr == 2:
                cur = 0.0
            for j in range(2):
                sqrt_u(TS[j], infos[j][1])
            for j in range(2):
                cur = scans(TS[j], infos[j][0], cur)
            if pr == 0 and own_parts:
                conv(n, [own_parts[0]], cast=False)
            if pr == 1 and n + 1 < 10 and CE < NL:
                conv(n + 1, [(xbL, 0, CE, NL - CE)], cast=False)
            if pr == 0:
                while len(pend) > n_after_yb:
                    pend.pop(0)()
                if not OPTS.get('ACTGELU', 1):
                    tt('dve', gtmp[:], ybs[:], ybs[:], ALU.mult)
                    ts('dve', gtmp[:], gtmp[:], 0.044715, 1.0, ALU.mult, ALU.add)
                    tt('dve', gtmp[:], gtmp[:], ybs[:], ALU.mult)
            if pr == 1:
                if OPTS.get('ACTGELU', 1):
                    act(gtmp[:], ybs[:], AF.Gelu_apprx_tanh)
                else:
                    act(gtmp[:], gtmp[:], AF.Tanh, scale=GELU_C)
                    stt('dve', gtmp[:], gtmp[:], 1.0, ybs[:], ALU.add, ALU.mult)
        while pend:
            pend.pop(0)()
        if n == 0:
            dump("xc0", xc[:, 0:512])
            dump("xc0c", xc[:, NL:NA])
        if n + 2 < 10:
            lru_loads(n + 2)

        def epilogue(n=n):
            tt('dve', hB[:], hB[:], ybs[:], ALU.add)
            if n == 0:
                dump("hsum0", hB[:])
            tt('dve', sblk[:], gtmp[:], hB[:], ALU.mult)
            dma('sp', s_dram[n], sblk[:], dram_writes=[("s", n)])
            if n == 0:
                dump("s0", sblk[:], BF16)
        deferred.append(epilogue)
    while deferred:
        deferred.pop(0)()
    assert not mod_pieces
    stt('dve', A2[:], modfm[:, 3, :, 0], 1.0, n2g[:], ALU.add, ALU.mult)
    dump("modfm", modfm[:])
    Z.release(mL)

    if stop_after == 'L':
        return _finish()

    qT = Z.alloc_top("qT", [96, 8, NQ], BF16)
    krT = Z.alloc_top("krT", [96, NA], BF16)
    kvT = Z.alloc_top("kvT", [128, 2, NA], BF16)
    cs = Z.alloc_top("cs", [128, 32, 32], F32)
    wuq = Z.alloc_top("wuq", [128, 3, 768], BF16)
    wK = Z.alloc_top("wK", [128, 2, 8, 64], BF16)
    wV = Z.alloc_top("wV", [128, 2, 8, 64], BF16)
    dma('sp', cs[:], d_cs)
    mB = Z.mark()
    stg_q = Z.alloc("stg_q", [128, 3, 768], F32)
    stg_kv = Z.alloc("stg_kv", [128, 2, 8, 128], F32)
    dma('sp', stg_q[:], w_uq.rearrange("(k p) n -> p k n", p=128))
    dma('sp', stg_kv[:], w_ukv.rearrange("(k p) (h j) -> p k h j", p=128, j=128))
    for k in range(3):
        ts('dve', wuq[:, k, :], stg_q[:, k, :], qng[:, k:k + 1], None, ALU.mult)
    for k in range(2):
        ts('dve', wK[:, k, :, :], stg_kv[:, k, :, 0:64], kvng[:, k:k + 1], None, ALU.mult)
        ts('dve', wV[:, k, :, :], stg_kv[:, k, :, 64:128], kvng[:, k:k + 1], None, ALU.mult)
    Z.release(mB)
    qlT = Z.alloc("qlT", [128, 3, NQ], BF16)
    wA = Z.alloc("wA", [128, 8, 672], BF16)
    wload(wA[:], w_in_v[:, :, 0:672])
    latn = [Z.alloc("latn%d" % i, [128, 672], BF16) for i in range(2)]
    ssq2 = Z.alloc("ssq2", [128, 34, 2], F32)
    sq2 = Z.alloc("sq2", [128, 34, 2], F32)
    rs2 = Z.alloc("rs2", [128, 34, 2], F32)
    rtmp = Z.alloc("rtmp", [128, 4, 16], F32)
    junk2 = Z.alloc("junk2", [128, 384], BF16)
    qtm = [Z.alloc("qtm%d" % i, [128, 8, 96], BF16) for i in range(2)]
    rq = Z.alloc("rq", [128, 4, 4, 16], F32)
    memset('dve', ssq2[:], 0.0)
    b1state = {}

    def b1_s12(i):
        own = i < 17
        col0 = i * 128 if i < 32 else NL + (i - 32) * 128
        ln_ = latn[i % 2]
        bq, bk = nbank(), nbank()
        if own:
            for k in range(8):
                mm(ps[:, bq, 0:QL], hTc(k, col0, 128), wA[:, k, 0:QL], start=(k == 0), stop=(k == 7))
        for k in range(8):
            mm(ps[:, bk, 0:288], hTc(k, col0, 128), wA[:, k, QL:672], start=(k == 0), stop=(k == 7))
        if own:
            act(junk2[:, 0:QL], ps[:, bq, 0:QL], AF.Square, accum_out=ssq2[:, i, 0:1])
            act(sq2[:, i, 0:1], ssq2[:, i, 0:1], AF.Sqrt, bias=epsv[:], scale=1.0 / QL)
            recip(rs2[:, i, 0:1], sq2[:, i, 0:1])
            act(ln_[:, 0:QL], ps[:, bq, 0:QL], AF.Identity, scale=rs2[:, i, 0:1])
        act(junk2[:, 0:KVL], ps[:, bk, 0:KVL], AF.Square, accum_out=ssq2[:, i, 1:2])
        act(sq2[:, i, 1:2], ssq2[:, i, 1:2], AF.Sqrt, bias=epsv[:], scale=1.0 / KVL)
        recip(rs2[:, i, 1:2], sq2[:, i, 1:2])
        ts('dve', ln_[:, QL:QL + KVL], ps[:, bk, 0:KVL], rs2[:, i, 1:2], None, ALU.mult)
        if i < 32:
            x1_, x2_ = ps[:, bk, 256:272], ps[:, bk, 272:288]
            cos_, sin_ = cs[:, i, 0:16], cs[:, i, 16:32]
            tt('dve', rtmp[:, 0, :], x1_, cos_, ALU.mult)
            tt('dve', rtmp[:, 1, :], x2_, sin_, ALU.mult)
            tt('dve', rtmp[:, 2, :], x2_, cos_, ALU.mult)
            tt('dve', rtmp[:, 3, :], x1_, sin_, ALU.mult)
            tt('dve', ln_[:, 640:656], rtmp[:, 0, :], rtmp[:, 1, :], ALU.subtract)
            tt('dve', ln_[:, 656:672], rtmp[:, 2, :], rtmp[:, 3, :], ALU.add)
        else:
            cp('dve', ln_[:, 640:672], ps[:, bk, 256:288])

    def b1_s34(i):
        own = i < 17
        col0 = i * 128 if i < 32 else NL + (i - 32) * 128
        ln_ = latn[i % 2]
        bt = nbank()
        if own:
            for k in range(3):
                tr(psb[:, bt, k * 128:(k + 1) * 128], ln_[:, k * 128:(k + 1) * 128], identb[:])
        for k in range(2):
            tr(psb[:, bt, (3 + k) * 128:(4 + k) * 128], ln_[:, QL + k * 128:QL + (k + 1) * 128], identb[:])
        tr(psb[0:96, bt, 640:768], ln_[:, 576:672], identb[:])
        if own:
            cp('act', qlT[:, :, col0:col0 + 128], psb[:, bt, 0:384].rearrange("p (k c) -> p k c", c=128))
        cp('act', kvT[:, :, col0:col0 + 128], psb[:, bt, 384:640].rearrange("p (k c) -> p k c", c=128))
        cp('act', krT[64:96, col0:col0 + 128], psb[64:96, bt, 640:768])

    for it in range(35):
        if it < 34:
            b1_s12(it)
        if it >= 1:
            b1_s34(it - 1)
    dump("qlT", qlT[:, :, 0:256], BF16)
    dump("kvT", kvT[:, :, 0:256], BF16)
    dump("krT", krT[64:96, 0:256], BF16)
    def q_s12(i):
        col0 = i * 128
        qt_ = qtm[i % 2]
        bb = [nbank(), nbank()]
        for hh in range(2):
            for k in range(3):
                mm(ps[:, bb[hh], 0:384], qlT[:, k, col0:col0 + 128], wuq[:, k, hh * 384:(hh + 1) * 384],
                   start=(k == 0), stop=(k == 2))
        for hh in range(2):
            src = ps[:, bb[hh], 0:384].rearrange("p (h d) -> p h d", d=96)
            x1_, x2_ = src[:, :, 64:80], src[:, :, 80:96]
            cos_ = cs[:, i:i + 1, 0:16].to_broadcast([128, 4, 16])
            sin_ = cs[:, i:i + 1, 16:32].to_broadcast([128, 4, 16])
            cp('dve', qt_[:, hh * 4:(hh + 1) * 4, 0:64], src[:, :, 0:64])
            tt('dve', rq[:, 0], x1_, cos_, ALU.mult)
            tt('dve', rq[:, 1], x2_, sin_, ALU.mult)
            tt('dve', rq[:, 2], x2_, cos_, ALU.mult)
            tt('dve', rq[:, 3], x1_, sin_, ALU.mult)
            tt('dve', qt_[:, hh * 4:(hh + 1) * 4, 64:80], rq[:, 0], rq[:, 1], ALU.subtract)
            tt('dve', qt_[:, hh * 4:(hh + 1) * 4, 80:96], rq[:, 2], rq[:, 3], ALU.add)

    def q_s34(i):
        col0 = i * 128
        qt_ = qtm[i % 2]
        bt = nbank()
        for h in range(8):
            tr(psb[0:96, bt, h * 128:(h + 1) * 128], qt_[:, h, :], identb[:])
        cp('act', qT[:, :, col0:col0 + 128], psb[0:96, bt, :].rearrange("p (h c) -> p h c", c=128))

    for it in range(18):
        if it < 17:
            q_s12(it)
        if it >= 1:
            q_s34(it - 1)
    dump("qT", qT[:, :, 0:256], BF16)
    Z.release(mB)

    if stop_after == 'B':
        return _finish()

    Z.cur = Z.lo
    OT = Z.alloc("OT", [128, 4, NQ], BF16)
    wsl = [Z.alloc("wsl%d" % i, [128, 30, 128], BF16) for i in range(2)]
    wout = Z.alloc("wout", [128, 8, D], BF16)
    woa_v = w_o_attn.rearrange("(k p) n -> p k n", p=128)
    wol_v = w_o_lru.rearrange("(k p) n -> p k n", p=128)

    def merge_loads(m):
        w = wsl[m % 2]
        wload(w[:, 0:4, :], woa_v[:, :, m * 128:(m + 1) * 128])
        wload(w[:, 4:14, :], wol_v[:, :, m * 128:(m + 1) * 128])
        wload(w[:, 14:22, :], w_in_v[:, :, OFF_G + m * 128:OFF_G + (m + 1) * 128])
        wload(w[:, 22:30, :], w_in_v[:, :, OFF_G + D + m * 128:OFF_G + D + (m + 1) * 128])

    mT = Z.mark()
    Kt = Z.alloc("Kt", [96, 2, NA], BF16)
    Vp = Z.alloc("Vp", [128, 34, 3, 64], BF16)
    pT = [Z.alloc("pT%d" % i, [128, 2, 512], BF16) for i in range(3)]
    rd = Z.alloc("rd", [128, 512], F32)
    memset('pool', Vp[:, :, 1, :], 1.0)
    unit = 0
    for p in range(4):
        if p == 2:
            wload(wout[:], w_out.rearrange("(k p) n -> p k n", p=128))
            merge_loads(0)
        for hh in range(2):
            h = 2 * p + hh
            for (st, sz) in blocks(0, NA):
                b = 6 + (st // 512) % 2
                for k in range(2):
                    mm(ps[0:64, b, 0:sz], wK[:, k, h, :], kvT[:, k, st:st + sz], start=(k == 0), stop=(k == 1))
                cp('dve' if (st // 512) % 2 else 'act', Kt[0:64, hh, st:st + sz], ps[0:64, b, 0:sz])
            cp('dve', Kt[64:96, hh, :], krT[64:96, :])
        for g in range(9):
            t0 = g * 4
            nt = min(4, 34 - t0)
            b = 6 + g % 2
            for t in range(nt):
                c0 = (t0 + t) * 128
                for k in range(2):
                    mm(ps[:, b, t * 128:(t + 1) * 128], kvT[:, k, c0:c0 + 128],
                       wV[:, k, 2 * p:2 * p + 2, :].rearrange("p h d -> p (h d)"), start=(k == 0), stop=(k == 1))
            cp('dve', Vp[:, t0:t0 + nt, 0:3:2, :],
               ps[:, b, 0:nt * 128].rearrange("p (t h d) -> p t h d", h=2, d=64))
        if p == 0:
            dump("Kt0", Kt[:, :, 0:256], BF16)
            dump("Vp0", Vp[:, 0:2, :, :], BF16)
        for hh in range(2):
            h = 2 * p + hh
            vsel = slice(0, 2) if hh == 0 else slice(1, 3)
            for (qs, qz) in blocks(0, NQ):
                ob = 4 + unit % 2
                unit += 1
                pend = None
                for ktp in range(17):
                    sb0 = 2 * (ktp % 2)
                    for j in range(2):
                        kt = 2 * ktp + j
                        mm(ps[:, sb0 + j, 0:qz], Kt[0:96, hh, kt * 128:(kt + 1) * 128], qT[0:96, h, qs:qs + qz])
                    pt_ = pT[ktp % 3]
                    act(pt_[:, :, 0:qz], ps[:, sb0:sb0 + 2, 0:qz], AF.Exp, scale=SM_SCALE)
                    if pend is not None:
                        pend()

                    def pv(ktp=ktp, pt_=pt_):
                        for j in range(2):
                            kt = 2 * ktp + j
                            mm(ps[:, ob, 0:qz], Vp[:, kt, vsel, :].rearrange("p a d -> p (a d)"), pt_[:, j, 0:qz],
                               start=(kt == 0), stop=(kt == 33))
                    pend = pv
                pend()
                if hh == 0:
                    recip(rd[0:64, 0:qz], ps[64:128, ob, 0:qz])
                    tt('dve', OT[0:64, p, qs:qs + qz], ps[0:64, ob, 0:qz], rd[0:64, 0:qz], ALU.mult)
                else:
                    recip(rd[64:128, 0:qz], ps[0:64, ob, 0:qz])
                    tt('dve', OT[64:128, p, qs:qs + qz], ps[64:128, ob, 0:qz], rd[64:128, 0:qz], ALU.mult)
    dump("OT", OT[:, :, 0:256], BF16)
    Z.release(mT)
    Z.hi = 229312

    if stop_after == 'T':
        return _finish()

    mix = Z.alloc("mix", [128, 8, NQ], BF16)
    mM = Z.mark()
    s_sb = Z.alloc("s_sb", [128, 10, NQ], BF16)
    tg = [Z.alloc("tg%d" % i, [128, 2, 512], F32) for i in range(2)]
    for n in range(10):
        dma('sp', s_sb[:, n, :], s_dram[n], dram_reads=[("s", n)])
    it = 0
    for m in range(8):
        if m + 1 < 8:
            merge_loads(m + 1)
        w = wsl[m % 2]
        for (st, sz) in blocks(0, NQ):
            b0 = 4 * (it % 2)
            tg_ = tg[it % 2]
            mt_ = tg_
            it += 1
            for k in range(4):
                mm(ps[:, b0, 0:sz], w[:, k, :], OT[:, k, st:st + sz], start=(k == 0), stop=(k == 3))
            for k in range(10):
                mm(ps[:, b0 + 1, 0:sz], w[:, 4 + k, :], s_sb[:, k, st:st + sz], start=(k == 0), stop=(k == 9))
            for k in range(8):
                mm(ps[:, b0 + 2, 0:sz], w[:, 14 + k, :], hTo[:, k, st:st + sz], start=(k == 0), stop=(k == 7))
            for k in range(8):
                mm(ps[:, b0 + 3, 0:sz], w[:, 22 + k, :], hTo[:, k, st:st + sz], start=(k == 0), stop=(k == 7))
            act(tg_[:, 0, 0:sz], ps[:, b0 + 2, 0:sz], AF.Tanh, bias=hbg[:, m:m + 1], scale=0.5)
            act(tg_[:, 1, 0:sz], ps[:, b0 + 3, 0:sz], AF.Tanh, bias=hbg[:, 8 + m:9 + m], scale=0.5)
            stt('dve', mt_[:, 0, 0:sz], tg_[:, 0, 0:sz], 1.0, ps[:, b0, 0:sz], ALU.add, ALU.mult)
            stt('dve', mt_[:, 1, 0:sz], tg_[:, 1, 0:sz], 1.0, ps[:, b0 + 1, 0:sz], ALU.add, ALU.mult)
            tt('dve', mix[:, m, st:st + sz], mt_[:, 0, 0:sz], mt_[:, 1, 0:sz], ALU.add)
    dump("mix", mix[:, :, 0:256], BF16)
    h2T = hTo
    Z.release(mM)
    xr = [Z.alloc("xr%d" % i, [128, D], F32) for i in range(3)]
    x1t = [Z.alloc("x1t%d" % i, [128, D], F32) for i in range(3)]
    xn2 = [Z.alloc("xn2%d" % i, [128, D], F32) for i in range(3)]
    junk3 = Z.alloc("junk3", [128, D], BF16)
    g1h = Z.alloc("g1h", [128, D], F32)
    dma('sp', g1h[:], gbc_dram[:, 0, :], dram_reads=[t for t in GB_TAGS if t[1] == 0])
    ssq = Z.alloc("ssq3", [128, 34], F32)
    sqv = Z.alloc("sqv3", [128, 34], F32)
    rsv = Z.alloc("rsv3", [128, 34], F32)
    junk = junk3
    memset('dve', ssq[:], 0.0)

    def m2_s1(i):
        xr_, x1_, xn_ = xr[i % 3], x1t[i % 3], xn2[i % 3]
        dma('sp', xr_[:], xs[i * 128:(i + 1) * 128, :])
        for half in range(2):
            b = 4 + 2 * (i % 2) + half
            hs = slice(half * 512, (half + 1) * 512)
            for k in range(8):
                mm(ps[:, b, :], mix[:, k, i * 128:(i + 1) * 128], wout[:, k, hs], start=(k == 0), stop=(k == 7))
            tt('dve', x1_[:, hs], ps[:, b, :], g1h[:, hs], ALU.mult)
            tt('dve', x1_[:, hs], x1_[:, hs], xr_[:, hs], ALU.add)
        if i < 16:
            dma('sp', x1_dram[i * 128:(i + 1) * 128, :], x1_[:], dram_writes=[("x1", i)])
        if i == 0:
            dump("x1_0", x1_[:])
        norm_p1(x1_, xn_, i)

    def m2_s2(i):
        norm_p2(xn2[i % 3], lambda k, i=i: h2T[:, k, i * 128:(i + 1) * 128],
                lambda k: A2[:, k:k + 1], lambda k: modfm[:, 2, k, 0:1], 2 * (i % 2))

    for it in range(18):
        if it < 17:
            m2_s1(it)
        if it >= 1:
            m2_s2(it - 1)
    dump("h2T", h2T[:, :, 0:256], BF16)
    Z.cur = Z.lo
    if stop_after == 'M':
        return _finish()

    actb = Z.alloc("actb", [128, NFC, NO], BF16)
    NWA = 16
    wdnA = Z.alloc("wdnA", [128, NWA, D], BF16)
    w_down_v = w_down.rearrange("(k p) n -> p k n", p=128)
    mF = Z.mark()
    wup = [Z.alloc("wup%d" % i, [128, 2, 8, 128], BF16) for i in range(2)]
    abuf = Z.alloc("abuf", [128, NQ + 2], F32)
    acv = Z.alloc("acv", [128, NO], F32)
    sgv = Z.alloc("sgv", [128, NO], F32)
    memset('pool', abuf[:, 0:1], 0.0)
    w_up_v = w_up.rearrange("(k p) n -> p k n", p=128)

    def ffn_loads(c):
        wload(wup[c % 2][:, 0], w_up_v[:, :, c * 128:(c + 1) * 128])
        wload(wup[c % 2][:, 1], w_up_v[:, :, FFN + c * 128:FFN + (c + 1) * 128])

    ffn_loads(0)
    for c in range(NFC):
        if c + 1 < NFC:
            ffn_loads(c + 1)
        if c == 4:
            wload(wdnA[:, 0:8, :], w_down_v[:, 0:8, :])
        if c == 9:
            wload(wdnA[:, 8:NWA, :], w_down_v[:, 8:NWA, :])
        wu = wup[c % 2]
        for (st, sz) in blocks(0, NQ):
            b = nbank()
            for k in range(8):
                mm(ps[:, b, 0:sz], wu[:, 0, k, :], h2T[:, k, st:st + sz], start=(k == 0), stop=(k == 7))
            cp('act', abuf[:, 1 + st:1 + st + sz], ps[:, b, 0:sz])
        ts('dve', acv[:], abuf[:, 0:NO], fcw[:, c, 0:1], fcb[:, c:c + 1], ALU.mult, ALU.add)
        stt('dve', acv[:], abuf[:, 1:NO + 1], fcw[:, c, 1:2], acv[:], ALU.mult, ALU.add)
        stt('dve', acv[:], abuf[:, 2:NO + 2], fcw[:, c, 2:3], acv[:], ALU.mult, ALU.add)
        act(sgv[:], acv[:], AF.Tanh, scale=0.5)
        stt('dve', sgv[:], sgv[:], 1.0, acv[:], ALU.add, ALU.mult)
        for (st, sz) in blocks(0, NO):
            b = nbank()
            for k in range(8):
                mm(ps[:, b, 0:sz], wu[:, 1, k, :], h2T[:, k, st:st + sz], start=(k == 0), stop=(k == 7))
            stt('dve', actb[:, c, st:st + sz], sgv[:, st:st + sz], 0.5, ps[:, b, 0:sz], ALU.mult, ALU.mult)
        if c == 0:
            dump("act0", actb[:, 0, 0:256], BF16)
    Z.release(mF)
    wdnB = Z.alloc("wdnB", [128, NFC - NWA, D], BF16)
    wload(wdnB[:], w_down_v[:, NWA:NFC, :])
    AH.cur = AH.lo
    x1l = [AH.alloc("x1l%d" % i, [128, D], F32) for i in range(2)]
    x2t = [AH.alloc("x2t%d" % i, [128, D], F32) for i in range(2)]
    junk4 = AH.alloc("junk4", [128, D], BF16)
    g2b = Z.alloc("g2b", [128, D], F32)
    fing = Z.alloc("fing", [128, D], F32)
    dma('sp', g2b[:], gbc_dram[:, 1, :], dram_reads=[t for t in GB_TAGS if t[1] == 1])
    dma('sp', fing[:], d_fing)
    ssq4 = Z.alloc("ssq4", [128, 16], F32)
    sq4 = Z.alloc("sq4", [128, 16], F32)
    rs4 = Z.alloc("rs4", [128, 16], F32)
    memset('dve', ssq4[:], 0.0)
    for i in range(16):
        xl, x2 = x1l[i % 2], x2t[i % 2]
        dma('sp', xl[:], x1_dram[i * 128:(i + 1) * 128, :], dram_reads=[("x1", i)])
        for half in range(2):
            b = 2 * (i % 2) + half
            for c in range(NFC):
                wsrc = wdnA[:, c, half * 512:(half + 1) * 512] if c < NWA else wdnB[:, c - NWA, half * 512:(half + 1) * 512]
                mm(ps[:, b, :], actb[:, c, i * 128:(i + 1) * 128], wsrc,
                   start=(c == 0), stop=(c == NFC - 1))
            tt('dve', x2[:, half * 512:(half + 1) * 512], ps[:, b, :], g2b[:, half * 512:(half + 1) * 512], ALU.mult)
            tt('dve', x2[:, half * 512:(half + 1) * 512], x2[:, half * 512:(half + 1) * 512],
               xl[:, half * 512:(half + 1) * 512], ALU.add)
        act(junk4[:], x2[:], AF.Square, accum_out=ssq4[:, i:i + 1])
        act(sq4[:, i:i + 1], ssq4[:, i:i + 1], AF.Sqrt, bias=epsv[:], scale=1.0 / D)
        recip(rs4[:, i:i + 1], sq4[:, i:i + 1])
        stt('dve', x2[:], x2[:], rs4[:, i:i + 1], fing[:], ALU.mult, ALU.mult)
        dma('sp', out[i * 128:(i + 1) * 128, :], x2[:])
    return _finish()


def _fm(v):
    v = np.asarray(v, np.float32)
    n = v.shape[-1] // 128
    return np.ascontiguousarray(v.reshape(n, 128).T)


def _rope_tables(pos):
    inv = (1.0 / (np.float32(10000.0) ** (np.arange(0, 16, 2, dtype=np.float32) / np.float32(16)))).astype(np.float32)
    row = (pos // 64).astype(np.float32)
    colp = (pos % 64).astype(np.float32)
    ang = np.concatenate([row[:, None] * inv[None, :], colp[:, None] * inv[None, :]], axis=-1).astype(np.float32)
    return np.cos(ang).astype(np.float32), np.sin(ang).astype(np.float32)


def make_in_maps(x, c, ctx, c_ctx, w_mod, b_mod, norm1_g, w_in, b_gate, q_norm_g, kv_norm_g,
                 w_uq, w_ukv, w_o_attn, lru_conv_w, lru_conv_b, lru_w_a, lru_b_a, lru_w_x,
                 lru_b_x, lru_lambda, w_o_lru, w_out, norm2_g, w_up, ffn_conv_w, ffn_conv_b,
                 w_down, final_g):
    import ml_dtypes
    f = lambda a: np.ascontiguousarray(np.asarray(a, np.float32))
    x, c, ctx, c_ctx = f(x), f(c), f(ctx), f(c_ctx)
    shared = {
        "w_mod": f(w_mod[0]), "w_in": f(w_in[0]), "w_uq": f(w_uq[0]), "w_ukv": f(w_ukv[0]),
        "w_o_attn": f(w_o_attn[0]), "w_o_lru": f(w_o_lru[0]), "w_out": f(w_out[0]), "w_up": f(w_up[0]),
        "w_down": f(w_down[0]),
        "bmodT": _fm(f(b_mod[0])),
        "bmodg": np.ascontiguousarray(np.broadcast_to(
            np.concatenate([f(b_mod[0])[2 * D:3 * D], f(b_mod[0])[5 * D:6 * D]])[None, :], (128, 2048))),
        "n1g": _fm(norm1_g[0]), "n2g": _fm(norm2_g[0]),
        "fing": np.ascontiguousarray(np.broadcast_to(f(final_g)[None, :], (128, D))),
        "bgate": _fm(b_gate[0]), "qng": _fm(q_norm_g[0]), "kvng": _fm(kv_norm_g[0]),
        "convb": _fm(lru_conv_b[0]), "fcb": _fm(ffn_conv_b[0]),
        "identf": np.eye(128, dtype=np.float32),
        "identb": np.eye(128, dtype=np.float32).astype(ml_dtypes.bfloat16),
    }
    lcw = f(lru_conv_w[0])
    fcw_ = f(ffn_conv_w[0])
    in_maps = []
    for core in range(8):
        b, half = core // 2, core % 2
        m = dict(shared)
        xb_ = x[b]
        cx = ctx[b]
        if half == 1:
            xb_ = xb_[::-1]
            cx = cx[::-1]
        m["xs"] = np.ascontiguousarray(xb_)
        m["ctxs"] = np.ascontiguousarray(cx)
        cT = np.stack([_fm(c[b]), _fm(c_ctx)], axis=-1)
        m["cT"] = np.ascontiguousarray(cT)
        dirs = [0, 1] if half == 0 else [1, 0]
        m["lru_wa"] = np.ascontiguousarray(f(lru_w_a[0])[dirs])
        m["lru_wx"] = np.ascontiguousarray(f(lru_w_x[0])[dirs])
        m["lba"] = np.ascontiguousarray(np.stack([_fm(f(lru_b_a[0])[d]) for d in dirs], axis=1))
        m["lbx"] = np.ascontiguousarray(np.stack([_fm(f(lru_b_x[0])[d]) for d in dirs], axis=1))
        m["lam"] = np.ascontiguousarray(np.stack([_fm(f(lru_lambda[0])[d]) for d in dirs], axis=1))
        w5 = np.zeros((5, LW), np.float32)
        if half == 0:
            w5[0:4] = lcw
        else:
            w5[1:5] = lcw[::-1]
        m["conv5"] = np.ascontiguousarray(np.stack([_fm(w5[j]) for j in range(5)], axis=-1))
        w3 = fcw_ if half == 0 else fcw_[::-1]
        m["fcw"] = np.ascontiguousarray(np.stack([_fm(w3[j]) for j in range(3)], axis=-1))
        pos = np.arange(NL)
        if half == 1:
            pos = NL - 1 - pos
        cos, sin = _rope_tables(pos)
        cs = np.concatenate([cos, sin], axis=-1).reshape(32, 128, 32).transpose(1, 0, 2)
        m["cs"] = np.ascontiguousarray(cs)
        in_maps.append(m)
    return in_maps


_CACHE = {}


def kernel(**inputs):
    in_maps = make_in_maps(**inputs)
    if "nc" not in _CACHE:
        from contextlib import ExitStack
        nc, P, A, _ = build_program(False)
        es = ExitStack()
        P.finalize(es)
        _CACHE["nc"] = nc
        _CACHE["es"] = es
    nc = _CACHE["nc"]
    res = run_bass_kernel_spmd(nc, in_maps, core_ids=list(range(8)))
    B = 4
    outp = np.zeros((B, NL, D), np.float32)
    for core in range(8):
        b, half = core // 2, core % 2
        o = np.asarray(res.results[core]["out"], np.float32)
        if half == 0:
            outp[b, 0:NO] = o
        else:
            outp[b, NO:NL] = o[::-1]
    return outp
```

```python
import numpy as np
from collections import defaultdict
import concourse.bass as bass
import concourse.mybir as mybir
from concourse.bass_utils import run_bass_kernel_spmd

F32 = mybir.dt.float32
BF16 = mybir.dt.bfloat16
AF = mybir.ActivationFunctionType
ALU = mybir.AluOpType

D = 1024
NL = 4096
NCX = 256
NA = NL + NCX
NQ = 2176
NO = 2048
QL, KVL, KR = 384, 256, 32
OFF_KV = QL
OFF_KR = OFF_KV + KVL
OFF_XB = OFF_KR + KR
LW = 1280
OFF_YB = OFF_XB + LW
OFF_G = OFF_YB + LW
IN_DIM = OFF_G + 2 * D
FFN = 2816
NFC = FFN // 128
EPS = 1e-6
SM_SCALE = 96 ** -0.5
GELU_C = 0.7978845608028654

OPTS = {}
ENGS = ['pe', 'act', 'dve', 'pool', 'sp']
BLOCK_ATTR = {'pe': 'tensor', 'act': 'scalar', 'dve': 'vector', 'pool': 'gpsimd', 'sp': 'sync'}
SAME_ENG_WINDOW = 1 << 30
NDMASEM = 8
FUSE_WAIT = True
BIN = 11


def dsize(dt):
    return mybir.dt.size(dt)


class Op:
    __slots__ = ('eng', 'idx', 'fn', 'waits', 'signal', 'dma', 'dsem', 'dcnt', 'snap', 'sigcount', 'dprev')


class Prog:
    def __init__(self, nc):
        self.nc = nc
        self.ops = {e: [] for e in ENGS}
        self.base = {}
        self.bins = {'sb': defaultdict(list), 'ps': defaultdict(list)}
        self.seen = {e: {e2: -1 for e2 in ENGS} for e in ENGS}
        self.seen_dma = {e: set() for e in ENGS}
        self.ndma = {e: 0 for e in ENGS}
        self.dram_w = {}
        self.dram_r = defaultdict(list)

    def region(self, ap):
        t = ap.tensor
        name = t.name
        if name not in self.base:
            return None
        space, base = self.base[name]
        pat = list(ap.ap)
        es = dsize(t.dtype)
        row = pat[0][0]
        off = ap.offset
        if row == 0:
            row = 1 << 40
        p0 = off // row
        f0 = off % row
        p1 = p0 + pat[0][1]
        lo = f0
        hi = f0
        for st, cnt in pat[1:]:
            ext = st * (cnt - 1)
            if ext < 0:
                lo += ext
            else:
                hi += ext
        b0, b1 = base + lo * es, base + (hi + 1) * es
        if space == 'ps':
            b0 = (b0 // 2048) * 2048
            b1 = ((b1 + 2047) // 2048) * 2048
            p0 = (p0 // 32) * 32
            p1 = ((p1 + 31) // 32) * 32
        return (space, p0, p1, b0, b1)

    def _conflicts(self, reg, is_write, out):
        space, p0, p1, b0, b1 = reg
        bins = self.bins[space]
        for bn in range(b0 >> BIN, ((b1 - 1) >> BIN) + 1):
            lst = bins.get(bn)
            if not lst:
                continue
            for rec in lst:
                if rec[1] < p1 and p0 < rec[2] and rec[3] < b1 and b0 < rec[4]:
                    if is_write or rec[0]:
                        out.add(rec[5])

    def _register(self, reg, is_write, opref):
        space, p0, p1, b0, b1 = reg
        bins = self.bins[space]
        rec = (is_write, p0, p1, b0, b1, opref)
        for bn in range(b0 >> BIN, ((b1 - 1) >> BIN) + 1):
            lst = bins[bn]
            if is_write:
                lst[:] = [r for r in lst if not (p0 <= r[1] and r[2] <= p1 and b0 <= r[3] and r[4] <= b1)]
            elif not opref[2]:
                lst[:] = [r for r in lst if not ((not r[0]) and r[1] == p0 and r[2] == p1 and r[3] == b0
                                                 and r[4] == b1 and r[5][0] == opref[0] and not r[5][2])]
            lst.append(rec)

    def emit(self, eng, fn, reads=(), writes=(), dma=False, dram_reads=(), dram_writes=()):
        deps = set()
        rr = [r for r in (self.region(a) for a in reads) if r is not None]
        ww = [r for r in (self.region(a) for a in writes) if r is not None]
        ww = ww + [r for r in rr if r[0] == 'ps']
        rr = [r for r in rr if r[0] != 'ps']
        for r in rr:
            self._conflicts(r, False, deps)
        for r in ww:
            self._conflicts(r, True, deps)
        for tag in dram_reads:
            if tag in self.dram_w:
                deps.add(self.dram_w[tag])
        for tag in dram_writes:
            if tag in self.dram_w:
                deps.add(self.dram_w[tag])
            deps.update(self.dram_r.get(tag, ()))
        op = Op()
        op.eng = eng
        op.idx = len(self.ops[eng])
        op.fn = fn
        op.signal = False
        op.dma = dma
        op.dsem = None
        op.dcnt = 0
        op.sigcount = 0
        op.dprev = None
        seen = self.seen[eng]
        sdma = self.seen_dma[eng]
        waits = []
        if dma:
            k = self.ndma[eng]
            self.ndma[eng] += 1
            op.dsem = (eng, k % NDMASEM)
            op.dcnt = 16 * (k // NDMASEM + 1)
            if k >= NDMASEM:
                prev = self._dma_ops[eng][k - NDMASEM]
                deps.add((eng, prev.idx, True))
            self._dma_ops.setdefault(eng, []).append(op)
        best = {}
        dl = []
        for (e2, i2, d2) in deps:
            if d2:
                dl.append((e2, i2, d2))
            elif i2 > best.get(e2, -1):
                best[e2] = i2
        dl.sort()
        dl += [(e2, i2, False) for e2, i2 in sorted(best.items())]
        for (e2, i2, d2) in dl:
            src = self.ops[e2][i2]
            if d2:
                if (e2, i2) in sdma:
                    continue
                sdma.add((e2, i2))
                waits.append((e2, i2))
            else:
                if e2 == eng:
                    if eng == 'pe' or op.idx - i2 > SAME_ENG_WINDOW:
                        continue
                if i2 <= seen[e2]:
                    continue
                seen[e2] = i2
                src.signal = True
                waits.append((e2, i2))
            for e3, v in src.snap.items():
                if v > seen[e3]:
                    seen[e3] = v
        op.waits = waits
        op.snap = dict(seen)
        self.ops[eng].append(op)
        ref = (eng, op.idx, dma)
        for r in rr:
            self._register(r, False, ref)
        for r in ww:
            self._register(r, True, ref)
        for tag in dram_reads:
            self.dram_r[tag].append(ref)
        for tag in dram_writes:
            self.dram_w[tag] = ref
            self.dram_r[tag] = []
        return op

    _dma_ops = None

    def finalize(self, es):
        nc = self.nc
        self.sem = {e: es.enter_context(nc.semaphore("s_" + e)) for e in ENGS}
        self.dsems = {}
        for e in ENGS:
            if self.ndma[e]:
                for j in range(min(NDMASEM, self.ndma[e])):
                    self.dsems[(e, j)] = es.enter_context(nc.semaphore("d_%s%d" % (e, j)))
        for e in ENGS:
            c = 0
            for op in self.ops[e]:
                if op.signal and not op.dma:
                    c += 1
                    op.sigcount = c
        block = es.enter_context(nc.Block())
        for e in ENGS:
            self._replay_engine(block, e)

    def _replay_engine(self, block, e):
        ops = self.ops[e]
        allops = self.ops
        sem = self.sem
        dsems = self.dsems

        def body(eng):
            for op in ops:
                wl = []
                for (e2, i2) in op.waits:
                    src = allops[e2][i2]
                    if src.dma:
                        wl.append((dsems[src.dsem], src.dcnt))
                    else:
                        wl.append((sem[e2], src.sigcount))
                fuse = wl.pop() if (wl and FUSE_WAIT and not op.dma) else None
                for (sm, v) in wl:
                    eng.wait_ge(sm, v)
                ins = op.fn(eng)
                if fuse is not None:
                    ins._wait_ge(fuse[0], fuse[1])
                if op.dma:
                    ins.then_inc(dsems[op.dsem], 16)
                elif op.signal:
                    ins.then_inc(sem[e], 1)
        getattr(block, BLOCK_ATTR[e])(body)


class Arena:
    def __init__(self, nc, prog, lo=20480, hi=229376):
        self.nc, self.prog, self.lo, self.hi, self.cur = nc, prog, lo, hi, lo
        self.n = 0
        self.peak = lo

    def alloc(self, name, shape, dt):
        per = int(np.prod(shape[1:])) * dsize(dt)
        per = (per + 63) // 64 * 64
        assert self.cur + per <= self.hi, "SBUF arena overflow at %s: need %d have %d" % (name, per, self.hi - self.cur)
        self.n += 1
        t = self.nc.alloc_sbuf_tensor_at("%s_%d_%d" % (name, self.lo, self.n), list(shape), dt, offset=self.cur)
        self.prog.base[t.name] = ('sb', self.cur)
        self.cur += per
        self.peak = max(self.peak, self.cur)
        return t

    def alloc_top(self, name, shape, dt):
        per = int(np.prod(shape[1:])) * dsize(dt)
        per = (per + 63) // 64 * 64
        assert self.hi - per >= self.cur, "SBUF arena overflow (top) at %s" % name
        self.hi -= per
        self.n += 1
        t = self.nc.alloc_sbuf_tensor_at("%s_t%d" % (name, self.n), list(shape), dt, offset=self.hi)
        self.prog.base[t.name] = ('sb', self.hi)
        return t

    def mark(self):
        return (self.cur, self.hi)

    def release(self, m):
        self.cur, self.hi = m


def blocks(lo, hi, step=512):
    out = []
    s = lo
    while s < hi:
        out.append((s, min(step, hi - s)))
        s += step
    return out


def build_program(debug=False, stop_after=None):
    nc = bass.Bass("TRN2", target_bir_lowering=False)
    P = Prog(nc)
    P._dma_ops = {}
    LO0 = 20480
    PERS = 5 * 1024
    HTO = 8 * NQ * 2
    A = Arena(nc, P, LO0, LO0 + PERS)
    AH = Arena(nc, P, LO0 + PERS, LO0 + PERS + HTO)
    Z = Arena(nc, P, LO0 + PERS + HTO, 229312)
    dbg_outs = {}

    def din(name, shape, dt=F32):
        return nc.dram_tensor(name, list(shape), dt, kind="ExternalInput").ap()

    xs = din("xs", [NL, D])
    ctxs = din("ctxs", [NCX, D])
    d_cT = din("cT", [128, 8, 2])
    d_bmodT = din("bmodT", [128, 48])
    d_bmodg = din("bmodg", [128, 2048])
    d_n1g = din("n1g", [128, 8])
    d_n2g = din("n2g", [128, 8])
    d_fing = din("fing", [128, D])
    d_bgate = din("bgate", [128, 16])
    d_qng = din("qng", [128, 3])
    d_kvng = din("kvng", [128, 2])
    d_conv5 = din("conv5", [128, 10, 5])
    d_convb = din("convb", [128, 10])
    d_lba = din("lba", [128, 2, 10])
    d_lbx = din("lbx", [128, 2, 10])
    d_lam = din("lam", [128, 2, 10])
    d_fcw = din("fcw", [128, NFC, 3])
    d_fcb = din("fcb", [128, NFC])
    d_cs = din("cs", [128, 32, 32])
    d_identf = din("identf", [128, 128])
    d_identb = din("identb", [128, 128], BF16)
    w_mod = din("w_mod", [D, 6 * D])
    w_in = din("w_in", [D, IN_DIM])
    w_uq = din("w_uq", [QL, 768])
    w_ukv = din("w_ukv", [KVL, 1024])
    w_o_attn = din("w_o_attn", [512, D])
    w_o_lru = din("w_o_lru", [LW, D])
    w_out = din("w_out", [D, D])
    w_up = din("w_up", [D, 2 * FFN])
    w_down = din("w_down", [FFN, D])
    lru_wa = din("lru_wa", [2, 10, 128, 128])
    lru_wx = din("lru_wx", [2, 10, 128, 128])
    out = nc.dram_tensor("out", [NO, D], F32, kind="ExternalOutput").ap()
    s_dram = nc.dram_tensor("s_scr", [10, 128, NQ], BF16, kind="Internal").ap()
    gbc_dram = nc.dram_tensor("gbc_scr", [128, 2, D], F32, kind="Internal").ap()
    x1_dram = nc.dram_tensor("x1_scr", [NO, D], F32, kind="Internal").ap()

    ps = nc.alloc_psum_tensor("ps", [128, 8, 512], F32)
    P.base[ps.name] = ('ps', 0)
    psb = ps[:, :, :].bitcast(BF16)
    P.base[psb.tensor.name] = ('ps', 0)

    def dma(q, out_, in_, dram_reads=(), dram_writes=()):
        return P.emit(q, lambda e: e.dma_start(out=out_, in_=in_), reads=[in_], writes=[out_], dma=True,
                      dram_reads=dram_reads, dram_writes=dram_writes)

    def mm(out_, lhsT, rhs, start=True, stop=True):
        return P.emit('pe', lambda e: e.matmul(out_, lhsT=lhsT, rhs=rhs, start=start, stop=stop),
                      reads=[lhsT, rhs], writes=[out_])

    def tr(out_, in_, ident):
        return P.emit('pe', lambda e: e.transpose(out=out_, in_=in_, identity=ident), reads=[in_, ident], writes=[out_])

    def act(out_, in_, func, bias=0.0, scale=1.0, accum_out=None, eng='act'):
        rd = [in_]
        if not isinstance(bias, float):
            rd.append(bias)
        if not isinstance(scale, float):
            rd.append(scale)
        wr = [out_]
        kw = {}
        if accum_out is not None:
            wr.append(accum_out)
            kw['accum_out'] = accum_out
        return P.emit('act', lambda e: e.activation(out=out_, in_=in_, func=func, bias=bias, scale=scale, **kw),
                      reads=rd, writes=wr)

    def tt(eng, out_, in0, in1, op):
        return P.emit(eng, lambda e: e.tensor_tensor(out=out_, in0=in0, in1=in1, op=op), reads=[in0, in1], writes=[out_])

    def ts(eng, out_, in0, s1, s2, op0, op1=None):
        rd = [in0]
        if not isinstance(s1, float):
            rd.append(s1)
        if s2 is not None and not isinstance(s2, float):
            rd.append(s2)
        if op1 is None:
            return P.emit(eng, lambda e: e.tensor_scalar(out=out_, in0=in0, scalar1=s1, scalar2=None, op0=op0),
                          reads=rd, writes=[out_])
        return P.emit(eng, lambda e: e.tensor_scalar(out=out_, in0=in0, scalar1=s1, scalar2=s2, op0=op0, op1=op1),
                      reads=rd, writes=[out_])

    def stt(eng, out_, in0, scalar, in1, op0, op1):
        rd = [in0, in1]
        if not isinstance(scalar, float):
            rd.append(scalar)
        return P.emit(eng, lambda e: e.scalar_tensor_tensor(out=out_, in0=in0, scalar=scalar, in1=in1, op0=op0, op1=op1),
                      reads=rd, writes=[out_])

    def cp(eng, out_, in_):
        if eng == 'act':
            return act(out_, in_, AF.Identity)
        return P.emit(eng, lambda e: e.tensor_copy(out=out_, in_=in_), reads=[in_], writes=[out_])

    def memset(eng, ap, val):
        return P.emit(eng, lambda e: e.memset(ap, val), writes=[ap])

    def recip(out_, in_):
        return P.emit('dve', lambda e: e.reciprocal(out=out_, in_=in_), reads=[in_], writes=[out_])

    def scan(out_, d0, d1, initial):
        rd = [d0, d1]
        if not isinstance(initial, float):
            rd.append(initial)
        return P.emit('dve', lambda e: e.tensor_tensor_scan(out=out_, data0=d0, data1=d1, initial=initial,
                                                            op0=ALU.mult, op1=ALU.add), reads=rd, writes=[out_])

    def dump(name, ap, dt=F32):
        if not debug:
            return
        shape = list(ap.shape)
        t = nc.dram_tensor("dbg_" + name, shape, dt, kind="ExternalOutput").ap()
        dbg_outs[name] = t
        dma('sp', t, ap)

    def wload(dst, src):
        return dma('pool', dst, src)

    def _finish():
        P.emit('sp', lambda e: e.nop(), reads=[], writes=[])
        last = P.ops['sp'][-1]
        for o in P._dma_ops.get('sp', [])[-NDMASEM:]:
            if (o.eng, o.idx) not in last.waits:
                last.waits.append((o.eng, o.idx))
        return nc, P, A, dbg_outs

    cT = A.alloc("cT", [128, 8, 2], F32)
    bmodT = A.alloc("bmodT", [128, 48], F32)
    n1g = A.alloc("n1g", [128, 8], F32)
    n2g = A.alloc("n2g", [128, 8], F32)
    bgate = A.alloc("bgate", [128, 16], F32)
    qng = A.alloc("qng", [128, 3], F32)
    kvng = A.alloc("kvng", [128, 2], F32)
    conv5 = A.alloc("conv5", [128, 10, 5], F32)
    convb = A.alloc("convb", [128, 10], F32)
    lba = A.alloc("lba", [128, 2, 10], F32)
    lbx = A.alloc("lbx", [128, 2, 10], F32)
    lam = A.alloc("lam", [128, 2, 10], F32)
    fcw = A.alloc("fcw", [128, NFC, 3], F32)
    fcb = A.alloc("fcb", [128, NFC], F32)
    identf = A.alloc("identf", [128, 128], F32)
    identb = A.alloc("identb", [128, 128], BF16)
    for dst, src in ((cT, d_cT), (bmodT, d_bmodT), (n1g, d_n1g), (n2g, d_n2g), (bgate, d_bgate),
                     (qng, d_qng), (kvng, d_kvng), (conv5, d_conv5), (convb, d_convb), (lba, d_lba), (lbx, d_lbx),
                     (lam, d_lam), (fcw, d_fcw), (fcb, d_fcb), (identf, d_identf), (identb, d_identb)):
        dma('sp', dst[:], src)

    epsv = A.alloc("epsv", [128, 1], F32)
    memset('dve', epsv[:], EPS)
    qtr = A.alloc("qtr", [128, 1], F32)
    memset('dve', qtr[:], 0.25)
    modfm = A.alloc("modfm", [128, 4, 8, 2], F32)
    A1 = A.alloc("A1", [128, 8, 2], F32)
    A2 = A.alloc("A2", [128, 8], F32)
    hbg = A.alloc("hbg", [128, 16], F32)
    hba = A.alloc("hba", [128, 2, 10], F32)
    hbx = A.alloc("hbx", [128, 2, 10], F32)
    c1 = A.alloc("c1", [128, 2, 10], F32)
    hc = A.alloc("hc", [128, 2, 10], F32)
    hTo = AH.alloc("hTo", [128, 8, NQ], BF16)
    hTx = Z.alloc("hTx", [128, 8, NA - NQ], BF16)

    def hTc(k, st, sz):
        if st + sz <= NQ:
            return hTo[:, k, st:st + sz]
        assert st >= NQ
        return hTx[:, k, st - NQ:st - NQ + sz]

    ALLBLK = blocks(0, NQ) + blocks(NQ, NL) + blocks(NL, NA)

    sc = Z.alloc("sc", [128, 8, 2], F32)
    sc_rep = Z.alloc("sc_rep", [128, 8, 128], F32)
    m0 = Z.mark()
    th0 = Z.alloc("th0", [128, 8, 2], F32)
    wm = [Z.alloc("wm%d" % i, [128, 8, 1024], F32) for i in range(2)]
    act(th0[:], cT[:], AF.Tanh, scale=0.5)
    ts('dve', th0[:], th0[:], 0.5, 0.5, ALU.mult, ALU.add)
    tt('dve', sc[:], th0[:], cT[:], ALU.mult)
    for k in range(8):
        cp('dve', sc_rep[:, k, :], sc[:, k, 0:1].to_broadcast([128, 128]))
    act(c1[:], lam[:], AF.Exp, scale=-1.0)
    act(c1[:], c1[:], AF.Ln, bias=1.0)
    ts('dve', hc[:], c1[:], -4.0, None, ALU.mult)
    ts('dve', c1[:], c1[:], -8.0, None, ALU.mult)
    ts('dve', hba[:], lba[:], 0.5, None, ALU.mult)
    ts('dve', hbx[:], lbx[:], 0.5, None, ALU.mult)
    ts('dve', hbg[:], bgate[:], 0.5, None, ALU.mult)
    w_mod_v = w_mod.rearrange("(k p) n -> p k n", p=128)
    fm_idx = {0: 0, 1: 1, 3: 2, 4: 3}

    def mod_slab(s):
        wb = wm[s % 2]
        dma('pool', wb[:], w_mod_v[:, :, s * 1024:(s + 1) * 1024])
        if s in fm_idx:
            bank = 4 + s % 2
            for j in range(8):
                for k in range(8):
                    mm(ps[:, bank, 2 * j:2 * j + 2], wb[:, k, j * 128:(j + 1) * 128], sc[:, k, :], start=(k == 0), stop=(k == 7))
            for col in range(2):
                tt('dve', modfm[:, fm_idx[s], :, col], ps[:, bank, col:16:2], bmodT[:, s * 8:(s + 1) * 8], ALU.add)
        else:
            gi = 0 if s == 2 else 1
            for half in range(2):
                bank = 6 + half
                for k in range(8):
                    mm(ps[:, bank, :], sc_rep[:, k, :], wb[:, k, half * 512:(half + 1) * 512], start=(k == 0), stop=(k == 7))
                tt('dve', gbc[:, gi, half * 512:(half + 1) * 512], ps[:, bank, :],
                   bmodg[:, gi * 1024 + half * 512: gi * 1024 + (half + 1) * 512], ALU.add)

    mod_slab(0)
    mod_slab(1)
    for col in range(2):
        stt('dve', A1[:, :, col], modfm[:, 1, :, col], 1.0, n1g[:], ALU.add, ALU.mult)

    xt = [Z.alloc("xt%d" % i, [128, D], F32) for i in range(6)]
    junk = Z.alloc("junk", [128, D], BF16)
    ssq = Z.alloc("ssq", [128, 34], F32)
    sqv = Z.alloc("sqv", [128, 34], F32)
    rsv = Z.alloc("rsv", [128, 34], F32)
    memset('dve', ssq[:], 0.0)

    def norm_p1(buf, bufo, statc):
        act(junk[:], buf[:], AF.Square, accum_out=ssq[:, statc:statc + 1])
        act(sqv[:, statc:statc + 1], ssq[:, statc:statc + 1], AF.Sqrt, bias=epsv[:], scale=1.0 / D)
        recip(rsv[:, statc:statc + 1], sqv[:, statc:statc + 1])
        ts('dve', bufo[:], buf[:], rsv[:, statc:statc + 1], None, ALU.mult)

    def norm_p2(bufo, dstf, Ascale, Abias, bank0):
        for k in range(8):
            tr(ps[:, bank0 + k // 4, (k % 4) * 128:(k % 4 + 1) * 128], bufo[:, k * 128:(k + 1) * 128], identf[:])
        for k in range(8):
            src = ps[:, bank0 + k // 4, (k % 4) * 128:(k % 4 + 1) * 128]
            if k < 4:
                act(dstf(k), src, AF.Identity, bias=Abias(k), scale=Ascale(k))
            else:
                ts('dve', dstf(k), src, Ascale(k), Abias(k), ALU.mult, ALU.add)

    def norm_to_T(buf, dstf, statc, Ascale, Abias, bank0):
        norm_p1(buf, buf, statc)
        norm_p2(buf, dstf, Ascale, Abias, bank0)

    for it in range(35):
        if it < 34:
            i = it
            buf = xt[i % 6]
            src = xs[i * 128:(i + 1) * 128, :] if i < 32 else ctxs[(i - 32) * 128:(i - 31) * 128, :]
            dma('sp', buf[:], src)
            norm_p1(buf, buf, i)
        if it >= 1:
            i = it - 1
            col = 0 if i < 32 else 1
            norm_p2(xt[i % 6], lambda k, i=i: hTc(k, i * 128, 128),
                    lambda k, col=col: A1[:, k, col:col + 1], lambda k, col=col: modfm[:, 0, k, col:col + 1],
                    2 * (i % 2))
    dump("hT", hTo[:, :, 0:256], BF16)
    dump("hTc", hTx[:, :, NL - NQ:NA - NQ], BF16)
    Z.release(m0)

    if stop_after in ('0', 'A'):
        return _finish()

    mL = Z.mark()
    SB = 1216
    wxb = [Z.alloc("wxb%d" % i, [128, 8, 128], BF16) for i in range(2)]
    wyb = [Z.alloc("wyb%d" % i, [128, 8, 128], BF16) for i in range(2)]
    wg = [Z.alloc("wg%d" % i, [128, 4, 128], BF16) for i in range(2)]
    xbL = Z.alloc("xbL", [128, NL + 4], F32)
    xbC = Z.alloc("xbC", [128, NCX + 4], F32)
    xc = Z.alloc("xc", [128, NA], F32)
    xcb = Z.alloc("xcb", [128, NA], BF16)
    TS = [[Z.alloc("T%d%s" % (j, "ab"[i]), [128, SB], F32) for j in range(3)] for i in range(2)]
    trb = [Z.alloc("trb%d" % i, [128, 512], F32) for i in range(2)]
    hB = Z.alloc("hB", [128, NQ], F32)
    ybs = Z.alloc("ybs", [128, NQ], F32)
    gtmp = Z.alloc("gtmp", [128, NQ], F32)
    sblk = Z.alloc("sblk", [128, NQ], BF16)
    wmp = [Z.alloc("wmp%d" % i, [128, 8, 128], F32) for i in range(2)]
    bmp = [Z.alloc("bmp%d" % i, [128, 128], F32) for i in range(2)]
    gpc = [Z.alloc("gpc%d" % i, [128, 128], F32) for i in range(2)]
    mod_pieces = []
    GB_TAGS = []
    for s_ in (2, 3, 4, 5):
        for j in range(8):
            def piece(s_=s_, j=j):
                i_ = len(piece_cnt)
                piece_cnt.append(1)
                w_ = wmp[i_ % 2]
                dma('pool', w_[:], w_mod_v[:, :, s_ * 1024 + j * 128: s_ * 1024 + (j + 1) * 128])
                b = nbank()
                if s_ in fm_idx:
                    for k in range(8):
                        mm(ps[:, b, 0:2], w_[:, k, :], sc[:, k, :], start=(k == 0), stop=(k == 7))
                    tt('dve', modfm[:, fm_idx[s_], j, :], ps[:, b, 0:2],
                       bmodT[:, s_ * 8 + j:s_ * 8 + j + 1].to_broadcast([128, 2]), ALU.add)
                else:
                    gi = 0 if s_ == 2 else 1
                    bm_, g_ = bmp[i_ % 2], gpc[i_ % 2]
                    dma('sp', bm_[:], d_bmodg[:, gi * 1024 + j * 128: gi * 1024 + (j + 1) * 128])
                    for k in range(8):
                        mm(ps[:, b, 0:128], sc_rep[:, k, :], w_[:, k, :], start=(k == 0), stop=(k == 7))
                    tt('dve', g_[:], ps[:, b, 0:128], bm_[:], ALU.add)
                    if gi == 0:
                        ts('dve', g_[:], g_[:], 0.5, None, ALU.mult)
                    tag = ("gbc", gi, j)
                    GB_TAGS.append(tag)
                    dma('sp', gbc_dram[:, gi, j * 128:(j + 1) * 128], g_[:], dram_writes=[tag])
            mod_pieces.append(piece)
    piece_cnt = []
    memset('pool', xbL[:, 0:2], 0.0)
    memset('pool', xbL[:, NL + 2:NL + 4], 0.0)
    memset('pool', xbC[:, 0:2], 0.0)
    memset('pool', xbC[:, NCX + 2:NCX + 4], 0.0)
    w_in_v = w_in.rearrange("(k p) n -> p k n", p=128)
    psrot = [0]

    def nbank():
        b = psrot[0]
        psrot[0] = (b + 1) % 8
        return b

    def lru_loads(n):
        wload(wxb[n % 2][:], w_in_v[:, :, OFF_XB + n * 128: OFF_XB + (n + 1) * 128])
        wload(wyb[n % 2][:], w_in_v[:, :, OFF_YB + n * 128: OFF_YB + (n + 1) * 128])
        for d in range(2):
            wload(wg[n % 2][:, 2 * d, :], lru_wa[d, n])
            wload(wg[n % 2][:, 2 * d + 1, :], lru_wx[d, n])

    def xb_tasks(n):
        wx_ = wxb[n % 2]
        out = []
        for (st, sz) in blocks(NQ, NL) + blocks(NL, NA) + blocks(0, NQ):
            def task(st=st, sz=sz):
                b = nbank()
                for k in range(8):
                    mm(ps[:, b, 0:sz], wx_[:, k, :], hTc(k, st, sz), start=(k == 0), stop=(k == 7))
                ev = 'dve' if (OPTS.get('DVEEV', 1) and (st // 512) % 2 == 0) else 'act'
                if st < NL:
                    cp(ev, xbL[:, 2 + st:2 + st + sz], ps[:, b, 0:sz])
                else:
                    cp(ev, xbC[:, 2 + st - NL:2 + st - NL + sz], ps[:, b, 0:sz])
            out.append(task)
        return out

    def yb_tasks(n):
        wy_ = wyb[n % 2]
        out = []
        for (st, sz) in blocks(0, NQ):
            def task(st=st, sz=sz):
                b = nbank()
                for k in range(8):
                    mm(ps[:, b, 0:sz], wy_[:, k, :], hTc(k, st, sz), start=(k == 0), stop=(k == 7))
                cp('dve' if OPTS.get('DVEEV', 1) else 'act', ybs[:, st:st + sz], ps[:, b, 0:sz])
            out.append(task)
        return out

    SUBS = [
        (1, [(NL, NA, True, None), (3136, NL, True, None)]),
        (1, [(NQ, 3136, True, None), (NO, NQ, True, ('hB', NO))]),
        (1, [(1024, NO, True, ('hB', 1024))]),
        (1, [(0, 1024, True, ('hB', 0))]),
        (0, [(NL, NA, False, None), (0, 960, False, ('add', 0))]),
        (0, [(960, NQ, False, ('add', 960))]),
    ]

    deferred = []
    lru_loads(0)
    lru_loads(1)
    for t in xb_tasks(0):
        t()
    for n in range(10):
        wg_ = wg[n % 2]
        def conv(nn, parts, cast=True):
            for (src, o0, st, sz) in parts:
                dst = xc[:, o0 + st:o0 + st + sz]
                ts('dve', dst, src[:, st:st + sz], conv5[:, nn, 0:1], convb[:, nn:nn + 1], ALU.mult, ALU.add)
                for j in range(1, 5):
                    stt('dve', dst, src[:, st + j:st + j + sz], conv5[:, nn, j:j + 1], dst, ALU.mult, ALU.add)
                if cast:
                    cp('act', xcb[:, o0 + st:o0 + st + sz], dst)

        CE = 2304 if OPTS.get('CSPLIT', 1) else NL
        if n == 0 and CE < NL:
            conv(0, [(xbL, 0, CE, NL - CE)])
        if CE < NL:
            conv(n, [(xbC, NL, 0, NCX), (xbL, 0, NO, CE - NO)])
            own_parts = [(xbL, 0, 0, 1024), (xbL, 0, 1024, 1024)]
            while deferred:
                deferred.pop(0)()
        else:
            conv(n, [(xbL, 0, 0, 2048), (xbL, 0, 2048, 2048), (xbC, NL, 0, NCX)])
            own_parts = []
        pend = yb_tasks(n) + (xb_tasks(n + 1) if n + 1 < 10 else [])
        n_after_yb = len(pend) - len(blocks(0, NQ))
        for _ in range(4):
            if mod_pieces:
                pend.append(mod_pieces.pop(0))
        per = (len(pend) + 5) // 6

        def gates(d, segs, tset):
            T1, T2, T3 = tset
            pos = 0
            seginfo = []
            for (lo, hi, rev, outspec) in segs:
                for (st, sz) in blocks(lo, hi):
                    b1_, b2_ = nbank(), nbank()
                    mm(ps[:, b1_, 0:sz], wg_[:, 2 * d, :], xcb[:, st:st + sz])
                    mm(ps[:, b2_, 0:sz], wg_[:, 2 * d + 1, :], xcb[:, st:st + sz])
                    tb = trb[(st // 512) % 2]
                    q0 = pos + (st - lo)
                    act(tb[:, 0:sz], ps[:, b1_, 0:sz], AF.Tanh, bias=hba[:, d, n:n + 1], scale=0.5)
                    act(T1[:, q0:q0 + sz], ps[:, b2_, 0:sz], AF.Tanh, bias=hbx[:, d, n:n + 1], scale=0.5)
                    act(T2[:, q0:q0 + sz], tb[:, 0:sz], AF.Exp, bias=hc[:, d, n:n + 1], scale=hc[:, d, n:n + 1])
                    act(T3[:, q0:q0 + sz], tb[:, 0:sz], AF.Exp, bias=c1[:, d, n:n + 1], scale=c1[:, d, n:n + 1])
                    stt('dve', T1[:, q0:q0 + sz], T1[:, q0:q0 + sz], 1.0, xc[:, st:st + sz], ALU.add, ALU.mult)
                seginfo.append((pos, hi - lo, rev, outspec))
                pos += hi - lo
            return seginfo, pos

        def sqrt_u(tset, L):
            T1, T2, T3 = tset
            act(T3[:, 0:L], T3[:, 0:L], AF.Sqrt, bias=qtr[:], scale=-0.25)
            tt('dve', T1[:, 0:L], T1[:, 0:L], T3[:, 0:L], ALU.mult)

        def scans(tset, seginfo, cur):
            T1, T2, T3 = tset
            for (p0, ln, rev, outspec) in seginfo:
                if outspec is not None and outspec[0] == 'hB':
                    o = hB[:, outspec[1]:outspec[1] + ln]
                elif outspec is not None and outspec[0] == 'add':
                    o = ybs[:, outspec[1]:outspec[1] + ln]
                else:
                    o = T3[:, p0:p0 + ln]
                a_, u_ = T2[:, p0:p0 + ln], T1[:, p0:p0 + ln]
                if rev:
                    scan(o[:, ::-1], a_[:, ::-1], u_[:, ::-1], cur)
                    cur = o[:, 0:1]
                else:
                    scan(o, a_, u_, cur)
                    cur = o[:, ln - 1:ln]
            return cur

        cur = 0.0
        for pr in range(3):
            infos = []
            for j in range(2):
                d, segs = SUBS[2 * pr + j]
                infos.append(gates(d, segs, TS[j]))
                if own_parts:
                    pA, pB = own_parts
                    if pr == 0 and j == 0:
                        conv(n, [pB], cast=False)
                    if pr == 0 and j == 1:
                        cp('act', xcb[:, pB[2]:pB[2] + pB[3]], xc[:, pB[2]:pB[2] + pB[3]])
                    if pr == 1 and j == 0:
                        cp('act', xcb[:, pA[2]:pA[2] + pA[3]], xc[:, pA[2]:pA[2] + pA[3]])
                    if pr == 2 and j == 1 and n + 1 < 10:
                        cp('act', xcb[:, CE:NL], xc[:, CE:NL])
                for _ in range(per):
                    if pend:
                        pend.pop(0)()
            if pr == 2:
                cur = 0.0
            for j in range(2):
                sqrt_u(TS[j], infos[j][1])
            for j in range(2):
                cur = scans(TS[j], infos[j][0], cur)
            if pr == 0 and own_parts:
                conv(n, [own_parts[0]], cast=False)
            if pr == 1 and n + 1 < 10 and CE < NL:
                conv(n + 1, [(xbL, 0, CE, NL - CE)], cast=False)
            if pr == 0:
                while len(pend) > n_after_yb:
                    pend.pop(0)()
                if not OPTS.get('ACTGELU', 1):
                    tt('dve', gtmp[:], ybs[:], ybs[:], ALU.mult)
                    ts('dve', gtmp[:], gtmp[:], 0.044715, 1.0, ALU.mult, ALU.add)
                    tt('dve', gtmp[:], gtmp[:], ybs[:], ALU.mult)
            if pr == 1:
                if OPTS.get('ACTGELU', 1):
                    act(gtmp[:], ybs[:], AF.Gelu_apprx_tanh)
                else:
                    act(gtmp[:], gtmp[:], AF.Tanh, scale=GELU_C)
                    stt('dve', gtmp[:], gtmp[:], 1.0, ybs[:], ALU.add, ALU.mult)
        while pend:
            pend.pop(0)()
        if n == 0:
            dump("xc0", xc[:, 0:512])
            dump("xc0c", xc[:, NL:NA])
        if n + 2 < 10:
            lru_loads(n + 2)

        def epilogue(n=n):
            tt('dve', hB[:], hB[:], ybs[:], ALU.add)
            if n == 0:
                dump("hsum0", hB[:])
            tt('dve', sblk[:], gtmp[:], hB[:], ALU.mult)
            dma('sp', s_dram[n], sblk[:], dram_writes=[("s", n)])
            if n == 0:
                dump("s0", sblk[:], BF16)
        deferred.append(epilogue)
    while deferred:
        deferred.pop(0)()
    assert not mod_pieces
    stt('dve', A2[:], modfm[:, 3, :, 0], 1.0, n2g[:], ALU.add, ALU.mult)
    dump("modfm", modfm[:])
    Z.release(mL)

    if stop_after == 'L':
        return _finish()

    qT = Z.alloc_top("qT", [96, 8, NQ], BF16)
    krT = Z.alloc_top("krT", [96, NA], BF16)
    kvT = Z.alloc_top("kvT", [128, 2, NA], BF16)
    cs = Z.alloc_top("cs", [128, 32, 32], F32)
    wuq = Z.alloc_top("wuq", [128, 3, 768], BF16)
    wK = Z.alloc_top("wK", [128, 2, 8, 64], BF16)
    wV = Z.alloc_top("wV", [128, 2, 8, 64], BF16)
    dma('sp', cs[:], d_cs)
    mB = Z.mark()
    stg_q = Z.alloc("stg_q", [128, 3, 768], F32)
    stg_kv = Z.alloc("stg_kv", [128, 2, 8, 128], F32)
    dma('sp', stg_q[:], w_uq.rearrange("(k p) n -> p k n", p=128))
    dma('sp', stg_kv[:], w_ukv.rearrange("(k p) (h j) -> p k h j", p=128, j=128))
    for k in range(3):
        ts('dve', wuq[:, k, :], stg_q[:, k, :], qng[:, k:k + 1], None, ALU.mult)
    for k in range(2):
        ts('dve', wK[:, k, :, :], stg_kv[:, k, :, 0:64], kvng[:, k:k + 1], None, ALU.mult)
        ts('dve', wV[:, k, :, :], stg_kv[:, k, :, 64:128], kvng[:, k:k + 1], None, ALU.mult)
    Z.release(mB)
    qlT = Z.alloc("qlT", [128, 3, NQ], BF16)
    wA = Z.alloc("wA", [128, 8, 672], BF16)
    wload(wA[:], w_in_v[:, :, 0:672])
    latn = [Z.alloc("latn%d" % i, [128, 672], BF16) for i in range(2)]
    ssq2 = Z.alloc("ssq2", [128, 34, 2], F32)
    sq2 = Z.alloc("sq2", [128, 34, 2], F32)
    rs2 = Z.alloc("rs2", [128, 34, 2], F32)
    rtmp = Z.alloc("rtmp", [128, 4, 16], F32)
    junk2 = Z.alloc("junk2", [128, 384], BF16)
    qtm = [Z.alloc("qtm%d" % i, [128, 8, 96], BF16) for i in range(2)]
    rq = Z.alloc("rq", [128, 4, 4, 16], F32)
    memset('dve', ssq2[:], 0.0)
    b1state = {}

    def b1_s12(i):
        own = i < 17
        col0 = i * 128 if i < 32 else NL + (i - 32) * 128
        ln_ = latn[i % 2]
        bq, bk = nbank(), nbank()
        if own:
            for k in range(8):
                mm(ps[:, bq, 0:QL], hTc(k, col0, 128), wA[:, k, 0:QL], start=(k == 0), stop=(k == 7))
        for k in range(8):
            mm(ps[:, bk, 0:288], hTc(k, col0, 128), wA[:, k, QL:672], start=(k == 0), stop=(k == 7))
        if own:
            act(junk2[:, 0:QL], ps[:, bq, 0:QL], AF.Square, accum_out=ssq2[:, i, 0:1])
            act(sq2[:, i, 0:1], ssq2[:, i, 0:1], AF.Sqrt, bias=epsv[:], scale=1.0 / QL)
            recip(rs2[:, i, 0:1], sq2[:, i, 0:1])
            act(ln_[:, 0:QL], ps[:, bq, 0:QL], AF.Identity, scale=rs2[:, i, 0:1])
        act(junk2[:, 0:KVL], ps[:, bk, 0:KVL], AF.Square, accum_out=ssq2[:, i, 1:2])
        act(sq2[:, i, 1:2], ssq2[:, i, 1:2], AF.Sqrt, bias=epsv[:], scale=1.0 / KVL)
        recip(rs2[:, i, 1:2], sq2[:, i, 1:2])
        ts('dve', ln_[:, QL:QL + KVL], ps[:, bk, 0:KVL], rs2[:, i, 1:2], None, ALU.mult)
        if i < 32:
            x1_, x2_ = ps[:, bk, 256:272], ps[:, bk, 272:288]
            cos_, sin_ = cs[:, i, 0:16], cs[:, i, 16:32]
            tt('dve', rtmp[:, 0, :], x1_, cos_, ALU.mult)
            tt('dve', rtmp[:, 1, :], x2_, sin_, ALU.mult)
            tt('dve', rtmp[:, 2, :], x2_, cos_, ALU.mult)
            tt('dve', rtmp[:, 3, :], x1_, sin_, ALU.mult)
            tt('dve', ln_[:, 640:656], rtmp[:, 0, :], rtmp[:, 1, :], ALU.subtract)
            tt('dve', ln_[:, 656:672], rtmp[:, 2, :], rtmp[:, 3, :], ALU.add)
        else:
            cp('dve', ln_[:, 640:672], ps[:, bk, 256:288])

    def b1_s34(i):
        own = i < 17
        col0 = i * 128 if i < 32 else NL + (i - 32) * 128
        ln_ = latn[i % 2]
        bt = nbank()
        if own:
            for k in range(3):
                tr(psb[:, bt, k * 128:(k + 1) * 128], ln_[:, k * 128:(k + 1) * 128], identb[:])
        for k in range(2):
            tr(psb[:, bt, (3 + k) * 128:(4 + k) * 128], ln_[:, QL + k * 128:QL + (k + 1) * 128], identb[:])
        tr(psb[0:96, bt, 640:768], ln_[:, 576:672], identb[:])
        if own:
            cp('act', qlT[:, :, col0:col0 + 128], psb[:, bt, 0:384].rearrange("p (k c) -> p k c", c=128))
        cp('act', kvT[:, :, col0:col0 + 128], psb[:, bt, 384:640].rearrange("p (k c) -> p k c", c=128))
        cp('act', krT[64:96, col0:col0 + 128], psb[64:96, bt, 640:768])

    for it in range(35):
        if it < 34:
            b1_s12(it)
        if it >= 1:
            b1_s34(it - 1)
    dump("qlT", qlT[:, :, 0:256], BF16)
    dump("kvT", kvT[:, :, 0:256], BF16)
    dump("krT", krT[64:96, 0:256], BF16)
    def q_s12(i):
        col0 = i * 128
        qt_ = qtm[i % 2]
        bb = [nbank(), nbank()]
        for hh in range(2):
            for k in range(3):
                mm(ps[:, bb[hh], 0:384], qlT[:, k, col0:col0 + 128], wuq[:, k, hh * 384:(hh + 1) * 384],
                   start=(k == 0), stop=(k == 2))
        for hh in range(2):
            src = ps[:, bb[hh], 0:384].rearrange("p (h d) -> p h d", d=96)
            x1_, x2_ = src[:, :, 64:80], src[:, :, 80:96]
            cos_ = cs[:, i:i + 1, 0:16].to_broadcast([128, 4, 16])
            sin_ = cs[:, i:i + 1, 16:32].to_broadcast([128, 4, 16])
            cp('dve', qt_[:, hh * 4:(hh + 1) * 4, 0:64], src[:, :, 0:64])
            tt('dve', rq[:, 0], x1_, cos_, ALU.mult)
            tt('dve', rq[:, 1], x2_, sin_, ALU.mult)
            tt('dve', rq[:, 2], x2_, cos_, ALU.mult)
            tt('dve', rq[:, 3], x1_, sin_, ALU.mult)
            tt('dve', qt_[:, hh * 4:(hh + 1) * 4, 64:80], rq[:, 0], rq[:, 1], ALU.subtract)
            tt('dve', qt_[:, hh * 4:(hh + 1) * 4, 80:96], rq[:, 2], rq[:, 3], ALU.add)

    def q_s34(i):
        col0 = i * 128
        qt_ = qtm[i % 2]
        bt = nbank()
        for h in range(8):
            tr(psb[0:96, bt, h * 128:(h + 1) * 128], qt_[:, h, :], identb[:])
        cp('act', qT[:, :, col0:col0 + 128], psb[0:96, bt, :].rearrange("p (h c) -> p h c", c=128))

    for it in range(18):
        if it < 17:
            q_s12(it)
        if it >= 1:
            q_s34(it - 1)
    dump("qT", qT[:, :, 0:256], BF16)
    Z.release(mB)

    if stop_after == 'B':
        return _finish()

    Z.cur = Z.lo
    OT = Z.alloc("OT", [128, 4, NQ], BF16)
    wsl = [Z.alloc("wsl%d" % i, [128, 30, 128], BF16) for i in range(2)]
    wout = Z.alloc("wout", [128, 8, D], BF16)
    woa_v = w_o_attn.rearrange("(k p) n -> p k n", p=128)
    wol_v = w_o_lru.rearrange("(k p) n -> p k n", p=128)

    def merge_loads(m):
        w = wsl[m % 2]
        wload(w[:, 0:4, :], woa_v[:, :, m * 128:(m + 1) * 128])
        wload(w[:, 4:14, :], wol_v[:, :, m * 128:(m + 1) * 128])
        wload(w[:, 14:22, :], w_in_v[:, :, OFF_G + m * 128:OFF_G + (m + 1) * 128])
        wload(w[:, 22:30, :], w_in_v[:, :, OFF_G + D + m * 128:OFF_G + D + (m + 1) * 128])

    mT = Z.mark()
    Kt = Z.alloc("Kt", [96, 2, NA], BF16)
    Vp = Z.alloc("Vp", [128, 34, 3, 64], BF16)
    pT = [Z.alloc("pT%d" % i, [128, 2, 512], BF16) for i in range(3)]
    rd = Z.alloc("rd", [128, 512], F32)
    memset('pool', Vp[:, :, 1, :], 1.0)
    unit = 0
    for p in range(4):
        if p == 2:
            wload(wout[:], w_out.rearrange("(k p) n -> p k n", p=128))
            merge_loads(0)
        for hh in range(2):
            h = 2 * p + hh
            for (st, sz) in blocks(0, NA):
                b = 6 + (st // 512) % 2
                for k in range(2):
                    mm(ps[0:64, b, 0:sz], wK[:, k, h, :], kvT[:, k, st:st + sz], start=(k == 0), stop=(k == 1))
                cp('dve' if (st // 512) % 2 else 'act', Kt[0:64, hh, st:st + sz], ps[0:64, b, 0:sz])
            cp('dve', Kt[64:96, hh, :], krT[64:96, :])
        for g in range(9):
            t0 = g * 4
            nt = min(4, 34 - t0)
            b = 6 + g % 2
            for t in range(nt):
                c0 = (t0 + t) * 128
                for k in range(2):
                    mm(ps[:, b, t * 128:(t + 1) * 128], kvT[:, k, c0:c0 + 128],
                       wV[:, k, 2 * p:2 * p + 2, :].rearrange("p h d -> p (h d)"), start=(k == 0), stop=(k == 1))
            cp('dve', Vp[:, t0:t0 + nt, 0:3:2, :],
               ps[:, b, 0:nt * 128].rearrange("p (t h d) -> p t h d", h=2, d=64))
        if p == 0:
            dump("Kt0", Kt[:, :, 0:256], BF16)
            dump("Vp0", Vp[:, 0:2, :, :], BF16)
        for hh in range(2):
            h = 2 * p + hh
            vsel = slice(0, 2) if hh == 0 else slice(1, 3)
            for (qs, qz) in blocks(0, NQ):
                ob = 4 + unit % 2
                unit += 1
                pend = None
                for ktp in range(17):
                    sb0 = 2 * (ktp % 2)
                    for j in range(2):
                        kt = 2 * ktp + j
                        mm(ps[:, sb0 + j, 0:qz], Kt[0:96, hh, kt * 128:(kt + 1) * 128], qT[0:96, h, qs:qs + qz])
                    pt_ = pT[ktp % 3]
                    act(pt_[:, :, 0:qz], ps[:, sb0:sb0 + 2, 0:qz], AF.Exp, scale=SM_SCALE)
                    if pend is not None:
                        pend()

                    def pv(ktp=ktp, pt_=pt_):
                        for j in range(2):
                            kt = 2 * ktp + j
                            mm(ps[:, ob, 0:qz], Vp[:, kt, vsel, :].rearrange("p a d -> p (a d)"), pt_[:, j, 0:qz],
                               start=(kt == 0), stop=(kt == 33))
                    pend = pv
                pend()
                if hh == 0:
                    recip(rd[0:64, 0:qz], ps[64:128, ob, 0:qz])
                    tt('dve', OT[0:64, p, qs:qs + qz], ps[0:64, ob, 0:qz], rd[0:64, 0:qz], ALU.mult)
                else:
                    recip(rd[64:128, 0:qz], ps[0:64, ob, 0:qz])
                    tt('dve', OT[64:128, p, qs:qs + qz], ps[64:128, ob, 0:qz], rd[64:128, 0:qz], ALU.mult)
    dump("OT", OT[:, :, 0:256], BF16)
    Z.release(mT)
    Z.hi = 229312

    if stop_after == 'T':
        return _finish()

    mix = Z.alloc("mix", [128, 8, NQ], BF16)
    mM = Z.mark()
    s_sb = Z.alloc("s_sb", [128, 10, NQ], BF16)
    tg = [Z.alloc("tg%d" % i, [128, 2, 512], F32) for i in range(2)]
    for n in range(10):
        dma('sp', s_sb[:, n, :], s_dram[n], dram_reads=[("s", n)])
    it = 0
    for m in range(8):
        if m + 1 < 8:
            merge_loads(m + 1)
        w = wsl[m % 2]
        for (st, sz) in blocks(0, NQ):
            b0 = 4 * (it % 2)
            tg_ = tg[it % 2]
            mt_ = tg_
            it += 1
            for k in range(4):
                mm(ps[:, b0, 0:sz], w[:, k, :], OT[:, k, st:st + sz], start=(k == 0), stop=(k == 3))
            for k in range(10):
                mm(ps[:, b0 + 1, 0:sz], w[:, 4 + k, :], s_sb[:, k, st:st + sz], start=(k == 0), stop=(k == 9))
            for k in range(8):
                mm(ps[:, b0 + 2, 0:sz], w[:, 14 + k, :], hTo[:, k, st:st + sz], start=(k == 0), stop=(k == 7))
            for k in range(8):
                mm(ps[:, b0 + 3, 0:sz], w[:, 22 + k, :], hTo[:, k, st:st + sz], start=(k == 0), stop=(k == 7))
            act(tg_[:, 0, 0:sz], ps[:, b0 + 2, 0:sz], AF.Tanh, bias=hbg[:, m:m + 1], scale=0.5)
            act(tg_[:, 1, 0:sz], ps[:, b0 + 3, 0:sz], AF.Tanh, bias=hbg[:, 8 + m:9 + m], scale=0.5)
            stt('dve', mt_[:, 0, 0:sz], tg_[:, 0, 0:sz], 1.0, ps[:, b0, 0:sz], ALU.add, ALU.mult)
            stt('dve', mt_[:, 1, 0:sz], tg_[:, 1, 0:sz], 1.0, ps[:, b0 + 1, 0:sz], ALU.add, ALU.mult)
            tt('dve', mix[:, m, st:st + sz], mt_[:, 0, 0:sz], mt_[:, 1, 0:sz], ALU.add)
    dump("mix", mix[:, :, 0:256], BF16)
    h2T = hTo
    Z.release(mM)
    xr = [Z.alloc("xr%d" % i, [128, D], F32) for i in range(3)]
    x1t = [Z.alloc("x1t%d" % i, [128, D], F32) for i in range(3)]
    xn2 = [Z.alloc("xn2%d" % i, [128, D], F32) for i in range(3)]
    junk3 = Z.alloc("junk3", [128, D], BF16)
    g1h = Z.alloc("g1h", [128, D], F32)
    dma('sp', g1h[:], gbc_dram[:, 0, :], dram_reads=[t for t in GB_TAGS if t[1] == 0])
    ssq = Z.alloc("ssq3", [128, 34], F32)
    sqv = Z.alloc("sqv3", [128, 34], F32)
    rsv = Z.alloc("rsv3", [128, 34], F32)
    junk = junk3
    memset('dve', ssq[:], 0.0)

    def m2_s1(i):
        xr_, x1_, xn_ = xr[i % 3], x1t[i % 3], xn2[i % 3]
        dma('sp', xr_[:], xs[i * 128:(i + 1) * 128, :])
        for half in range(2):
            b = 4 + 2 * (i % 2) + half
            hs = slice(half * 512, (half + 1) * 512)
            for k in range(8):
                mm(ps[:, b, :], mix[:, k, i * 128:(i + 1) * 128], wout[:, k, hs], start=(k == 0), stop=(k == 7))
            tt('dve', x1_[:, hs], ps[:, b, :], g1h[:, hs], ALU.mult)
            tt('dve', x1_[:, hs], x1_[:, hs], xr_[:, hs], ALU.add)
        if i < 16:
            dma('sp', x1_dram[i * 128:(i + 1) * 128, :], x1_[:], dram_writes=[("x1", i)])
        if i == 0:
            dump("x1_0", x1_[:])
        norm_p1(x1_, xn_, i)

    def m2_s2(i):
        norm_p2(xn2[i % 3], lambda k, i=i: h2T[:, k, i * 128:(i + 1) * 128],
                lambda k: A2[:, k:k + 1], lambda k: modfm[:, 2, k, 0:1], 2 * (i % 2))

    for it in range(18):
        if it < 17:
            m2_s1(it)
        if it >= 1:
            m2_s2(it - 1)
    dump("h2T", h2T[:, :, 0:256], BF16)
    Z.cur = Z.lo
    if stop_after == 'M':
        return _finish()

    actb = Z.alloc("actb", [128, NFC, NO], BF16)
    NWA = 16
    wdnA = Z.alloc("wdnA", [128, NWA, D], BF16)
    w_down_v = w_down.rearrange("(k p) n -> p k n", p=128)
    mF = Z.mark()
    wup = [Z.alloc("wup%d" % i, [128, 2, 8, 128], BF16) for i in range(2)]
    abuf = Z.alloc("abuf", [128, NQ + 2], F32)
    acv = Z.alloc("acv", [128, NO], F32)
    sgv = Z.alloc("sgv", [128, NO], F32)
    memset('pool', abuf[:, 0:1], 0.0)
    w_up_v = w_up.rearrange("(k p) n -> p k n", p=128)

    def ffn_loads(c):
        wload(wup[c % 2][:, 0], w_up_v[:, :, c * 128:(c + 1) * 128])
        wload(wup[c % 2][:, 1], w_up_v[:, :, FFN + c * 128:FFN + (c + 1) * 128])

    ffn_loads(0)
    for c in range(NFC):
        if c + 1 < NFC:
            ffn_loads(c + 1)
        if c == 4:
            wload(wdnA[:, 0:8, :], w_down_v[:, 0:8, :])
        if c == 9:
            wload(wdnA[:, 8:NWA, :], w_down_v[:, 8:NWA, :])
        wu = wup[c % 2]
        for (st, sz) in blocks(0, NQ):
            b = nbank()
            for k in range(8):
                mm(ps[:, b, 0:sz], wu[:, 0, k, :], h2T[:, k, st:st + sz], start=(k == 0), stop=(k == 7))
            cp('act', abuf[:, 1 + st:1 + st + sz], ps[:, b, 0:sz])
        ts('dve', acv[:], abuf[:, 0:NO], fcw[:, c, 0:1], fcb[:, c:c + 1], ALU.mult, ALU.add)
        stt('dve', acv[:], abuf[:, 1:NO + 1], fcw[:, c, 1:2], acv[:], ALU.mult, ALU.add)
        stt('dve', acv[:], abuf[:, 2:NO + 2], fcw[:, c, 2:3], acv[:], ALU.mult, ALU.add)
        act(sgv[:], acv[:], AF.Tanh, scale=0.5)
        stt('dve', sgv[:], sgv[:], 1.0, acv[:], ALU.add, ALU.mult)
        for (st, sz) in blocks(0, NO):
            b = nbank()
            for k in range(8):
                mm(ps[:, b, 0:sz], wu[:, 1, k, :], h2T[:, k, st:st + sz], start=(k == 0), stop=(k == 7))
            stt('dve', actb[:, c, st:st + sz], sgv[:, st:st + sz], 0.5, ps[:, b, 0:sz], ALU.mult, ALU.mult)
        if c == 0:
            dump("act0", actb[:, 0, 0:256], BF16)
    Z.release(mF)
    wdnB = Z.alloc("wdnB", [128, NFC - NWA, D], BF16)
    wload(wdnB[:], w_down_v[:, NWA:NFC, :])
    AH.cur = AH.lo
    x1l = [AH.alloc("x1l%d" % i, [128, D], F32) for i in range(2)]
    x2t = [AH.alloc("x2t%d" % i, [128, D], F32) for i in range(2)]
    junk4 = AH.alloc("junk4", [128, D], BF16)
    g2b = Z.alloc("g2b", [128, D], F32)
    fing = Z.alloc("fing", [128, D], F32)
    dma('sp', g2b[:], gbc_dram[:, 1, :], dram_reads=[t for t in GB_TAGS if t[1] == 1])
    dma('sp', fing[:], d_fing)
    ssq4 = Z.alloc("ssq4", [128, 16], F32)
    sq4 = Z.alloc("sq4", [128, 16], F32)
    rs4 = Z.alloc("rs4", [128, 16], F32)
    memset('dve', ssq4[:], 0.0)
    for i in range(16):
        xl, x2 = x1l[i % 2], x2t[i % 2]
        dma('sp', xl[:], x1_dram[i * 128:(i + 1) * 128, :], dram_reads=[("x1", i)])
        for half in range(2):
            b = 2 * (i % 2) + half
            for c in range(NFC):
                wsrc = wdnA[:, c, half * 512:(half + 1) * 512] if c < NWA else wdnB[:, c - NWA, half * 512:(half + 1) * 512]
                mm(ps[:, b, :], actb[:, c, i * 128:(i + 1) * 128], wsrc,
                   start=(c == 0), stop=(c == NFC - 1))
            tt('dve', x2[:, half * 512:(half + 1) * 512], ps[:, b, :], g2b[:, half * 512:(half + 1) * 512], ALU.mult)
            tt('dve', x2[:, half * 512:(half + 1) * 512], x2[:, half * 512:(half + 1) * 512],
               xl[:, half * 512:(half + 1) * 512], ALU.add)
        act(junk4[:], x2[:], AF.Square, accum_out=ssq4[:, i:i + 1])
        act(sq4[:, i:i + 1], ssq4[:, i:i + 1], AF.Sqrt, bias=epsv[:], scale=1.0 / D)
        recip(rs4[:, i:i + 1], sq4[:, i:i + 1])
        stt('dve', x2[:], x2[:], rs4[:, i:i + 1], fing[:], ALU.mult, ALU.mult)
        dma('sp', out[i * 128:(i + 1) * 128, :], x2[:])
    return _finish()


def _fm(v):
    v = np.asarray(v, np.float32)
    n = v.shape[-1] // 128
    return np.ascontiguousarray(v.reshape(n, 128).T)


def _rope_tables(pos):
    inv = (1.0 / (np.float32(10000.0) ** (np.arange(0, 16, 2, dtype=np.float32) / np.float32(16)))).astype(np.float32)
    row = (pos // 64).astype(np.float32)
    colp = (pos % 64).astype(np.float32)
    ang = np.concatenate([row[:, None] * inv[None, :], colp[:, None] * inv[None, :]], axis=-1).astype(np.float32)
    return np.cos(ang).astype(np.float32), np.sin(ang).astype(np.float32)


def make_in_maps(x, c, ctx, c_ctx, w_mod, b_mod, norm1_g, w_in, b_gate, q_norm_g, kv_norm_g,
                 w_uq, w_ukv, w_o_attn, lru_conv_w, lru_conv_b, lru_w_a, lru_b_a, lru_w_x,
                 lru_b_x, lru_lambda, w_o_lru, w_out, norm2_g, w_up, ffn_conv_w, ffn_conv_b,
                 w_down, final_g):
    import ml_dtypes
    f = lambda a: np.ascontiguousarray(np.asarray(a, np.float32))
    x, c, ctx, c_ctx = f(x), f(c), f(ctx), f(c_ctx)
    shared = {
        "w_mod": f(w_mod[0]), "w_in": f(w_in[0]), "w_uq": f(w_uq[0]), "w_ukv": f(w_ukv[0]),
        "w_o_attn": f(w_o_attn[0]), "w_o_lru": f(w_o_lru[0]), "w_out": f(w_out[0]), "w_up": f(w_up[0]),
        "w_down": f(w_down[0]),
        "bmodT": _fm(f(b_mod[0])),
        "bmodg": np.ascontiguousarray(np.broadcast_to(
            np.concatenate([f(b_mod[0])[2 * D:3 * D], f(b_mod[0])[5 * D:6 * D]])[None, :], (128, 2048))),
        "n1g": _fm(norm1_g[0]), "n2g": _fm(norm2_g[0]),
        "fing": np.ascontiguousarray(np.broadcast_to(f(final_g)[None, :], (128, D))),
        "bgate": _fm(b_gate[0]), "qng": _fm(q_norm_g[0]), "kvng": _fm(kv_norm_g[0]),
        "convb": _fm(lru_conv_b[0]), "fcb": _fm(ffn_conv_b[0]),
        "identf": np.eye(128, dtype=np.float32),
        "identb": np.eye(128, dtype=np.float32).astype(ml_dtypes.bfloat16),
    }
    lcw = f(lru_conv_w[0])
    fcw_ = f(ffn_conv_w[0])
    in_maps = []
    for core in range(8):
        b, half = core // 2, core % 2
        m = dict(shared)
        xb_ = x[b]
        cx = ctx[b]
        if half == 1:
            xb_ = xb_[::-1]
            cx = cx[::-1]
        m["xs"] = np.ascontiguousarray(xb_)
        m["ctxs"] = np.ascontiguousarray(cx)
        cT = np.stack([_fm(c[b]), _fm(c_ctx)], axis=-1)
        m["cT"] = np.ascontiguousarray(cT)
        dirs = [0, 1] if half == 0 else [1, 0]
        m["lru_wa"] = np.ascontiguousarray(f(lru_w_a[0])[dirs])
        m["lru_wx"] = np.ascontiguousarray(f(lru_w_x[0])[dirs])
        m["lba"] = np.ascontiguousarray(np.stack([_fm(f(lru_b_a[0])[d]) for d in dirs], axis=1))
        m["lbx"] = np.ascontiguousarray(np.stack([_fm(f(lru_b_x[0])[d]) for d in dirs], axis=1))
        m["lam"] = np.ascontiguousarray(np.stack([_fm(f(lru_lambda[0])[d]) for d in dirs], axis=1))
        w5 = np.zeros((5, LW), np.float32)
        if half == 0:
            w5[0:4] = lcw
        else:
            w5[1:5] = lcw[::-1]
        m["conv5"] = np.ascontiguousarray(np.stack([_fm(w5[j]) for j in range(5)], axis=-1))
        w3 = fcw_ if half == 0 else fcw_[::-1]
        m["fcw"] = np.ascontiguousarray(np.stack([_fm(w3[j]) for j in range(3)], axis=-1))
        pos = np.arange(NL)
        if half == 1:
            pos = NL - 1 - pos
        cos, sin = _rope_tables(pos)
        cs = np.concatenate([cos, sin], axis=-1).reshape(32, 128, 32).transpose(1, 0, 2)
        m["cs"] = np.ascontiguousarray(cs)
        in_maps.append(m)
    return in_maps


_CACHE = {}


def kernel(**inputs):
    in_maps = make_in_maps(**inputs)
    if "nc" not in _CACHE:
        from contextlib import ExitStack
        nc, P, A, _ = build_program(False)
        es = ExitStack()
        P.finalize(es)
        _CACHE["nc"] = nc
        _CACHE["es"] = es
    nc = _CACHE["nc"]
    res = run_bass_kernel_spmd(nc, in_maps, core_ids=list(range(8)))
    B = 4
    outp = np.zeros((B, NL, D), np.float32)
    for core in range(8):
        b, half = core // 2, core % 2
        o = np.asarray(res.results[core]["out"], np.float32)
        if half == 0:
            outp[b, 0:NO] = o
        else:
            outp[b, NO:NL] = o[::-1]
    return outp
```
